# Optimizing a Trainium2 kernel written in Bass

```python
import jax, jax.numpy as jnp
from jax import lax
import numpy as np

D_MODEL = 1024
BATCH = 4
SEQ = 8192
DEPTH = 4

GRID_W = 64
QBLOCK = 128
NORM_EPS = 1e-6
ROPE_THETA = 10000.0
NEG_INF = -1e30

MLA_HEADS = 8
MLA_Q_RANK = 384
MLA_KV_RANK = 256
MLA_NOPE = 64
MLA_ROPE = 32
MLA_V = 64

DIL_PAIRS = ((128, 1), (512, 4), (2048, 16))
DIL_HALF = 64
DIL_SLOTS = 4
DIL_GROUPS = len(DIL_PAIRS)
DIL_HEADS = DIL_SLOTS * DIL_GROUPS
DIL_HEAD_DIM = 64

GQA_HEADS = 16
GQA_KV_HEADS = 4
GQA_HEAD_DIM = 64

FFN_HIDDEN = -(-8 * D_MODEL // (3 * 256)) * 256

IN_A = MLA_Q_RANK + MLA_KV_RANK + MLA_ROPE
IN_B = 3 * DIL_HEADS * DIL_HEAD_DIM
MIX_IN = IN_A + IN_B
MIX_OUT = MLA_HEADS * MLA_V + DIL_SLOTS * DIL_HEAD_DIM
N_EVEN = (DEPTH + 1) // 2
N_ODD = DEPTH // 2

kernel_name = "hybrid_mla_dilated_axial_gqa_encoder"


def rmsnorm(x, g):
    xf = x.astype(jnp.float32)
    y = xf * lax.rsqrt(jnp.mean(xf * xf, axis=-1, keepdims=True) + NORM_EPS)
    return (y * g.astype(jnp.float32)).astype(x.dtype)


def rope_angles(pos, dim):
    freqs = ROPE_THETA ** (-jnp.arange(0, dim, 2, dtype=jnp.float32) / dim)
    ang = pos.astype(jnp.float32)[:, None] * freqs[None, :]
    return jnp.cos(ang), jnp.sin(ang)


def apply_rope(x, cos, sin):
    xf = x.astype(jnp.float32)
    x1, x2 = jnp.split(xf, 2, axis=-1)
    return jnp.concatenate([x1 * cos - x2 * sin, x1 * sin + x2 * cos], axis=-1).astype(x.dtype)


def blocked_attention(q, k, v, scale):
    b, h, s, dk = q.shape
    g = k.shape[1]
    r = h // g
    nq = s // QBLOCK
    qb = q.reshape(b, g, r, nq, QBLOCK, dk).transpose(3, 0, 1, 2, 4, 5)

    def one_block(qblk):
        sc = jnp.einsum('bgrqd,bgkd->bgrqk', qblk, k, preferred_element_type=jnp.float32) * scale
        p = jax.nn.softmax(sc, axis=-1)
        return jnp.einsum('bgrqk,bgkd->bgrqd', p.astype(v.dtype), v)

    out = lax.map(one_block, qb)
    return out.transpose(1, 2, 3, 0, 4, 5).reshape(b, h, s, -1)


def mla_mixer(h_a, q_norm_g, kv_norm_g, w_uq, w_ukv, cos, sin):
    b, s, _ = h_a.shape
    cq, ckv, k_rope = jnp.split(h_a, [MLA_Q_RANK, MLA_Q_RANK + MLA_KV_RANK], axis=-1)
    cq = rmsnorm(cq, q_norm_g)
    ckv = rmsnorm(ckv, kv_norm_g)
    q = jnp.einsum('bsr,rhd->bhsd', cq, w_uq)
    kv = jnp.einsum('bsr,rhd->bhsd', ckv, w_ukv)
    q_nope, q_rope = q[..., :MLA_NOPE], q[..., MLA_NOPE:]
    k_nope, v = kv[..., :MLA_NOPE], kv[..., MLA_NOPE:]
    q_rope = apply_rope(q_rope, cos, sin)
    k_rope = apply_rope(k_rope, cos, sin)[:, None]
    k_rope = jnp.broadcast_to(k_rope, (b, MLA_HEADS, s, MLA_ROPE))
    qh = jnp.concatenate([q_nope, q_rope], axis=-1)
    kh = jnp.concatenate([k_nope, k_rope], axis=-1)
    o = blocked_attention(qh, kh, v, (MLA_NOPE + MLA_ROPE) ** -0.5)
    return o.transpose(0, 2, 1, 3).reshape(b, s, MLA_HEADS * MLA_V)


def dilated_group_attention(q, k, v, dilation, slopes):
    b, hg, s, dh = q.shape
    P = DIL_HALF
    L = s // dilation
    nb = -(-L // P)
    Lp = nb * P

    def to_residue(t):
        return t.reshape(b, hg, L, dilation, dh).transpose(0, 1, 3, 2, 4)

    qr, kr, vr = to_residue(q), to_residue(k), to_residue(v)
    qb = jnp.pad(qr, ((0, 0), (0, 0), (0, 0), (0, Lp - L), (0, 0))).reshape(b, hg, dilation, nb, P, dh)

    def band(t):
        tb = jnp.pad(t, ((0, 0), (0, 0), (0, 0), (P, Lp - L + P), (0, 0))).reshape(b, hg, dilation, nb + 2, P, dh)
        return jnp.concatenate([tb[:, :, :, :-2], tb[:, :, :, 1:-1], tb[:, :, :, 2:]], axis=4)

    kb, vb = band(kr), band(vr)
    sc = jnp.einsum('bhrnqd,bhrnkd->bhrnqk', qb, kb, preferred_element_type=jnp.float32) * dh ** -0.5

    i = jnp.arange(P)[:, None]
    c = jnp.arange(3 * P)[None, :]
    rel = c - P - i
    uk = jnp.arange(nb)[:, None, None] * P + c - P
    valid = (jnp.abs(rel) <= DIL_HALF)[None] & (uk >= 0) & (uk < L)
    dist = (dilation * jnp.abs(rel)).astype(jnp.float32)
    bias = -slopes.astype(jnp.float32)[:, None, None, None, None] * dist

    sc = jnp.where(valid, sc + bias, NEG_INF)
    m = jnp.max(sc, axis=-1, keepdims=True)
    e = jnp.exp(sc - m)
    den = jnp.sum(e, axis=-1, keepdims=True)
    o = jnp.einsum('bhrnqk,bhrnkd->bhrnqd', (e / den).astype(v.dtype), vb)
    lse = (m + jnp.log(den))[..., 0]

    o = o.reshape(b, hg, dilation, Lp, dh)[:, :, :, :L].transpose(0, 1, 3, 2, 4).reshape(b, hg, s, dh)
    lse = lse.reshape(b, hg, dilation, Lp)[:, :, :, :L].transpose(0, 1, 3, 2).reshape(b, hg, s)
    return o, lse


def dilated_mixer(h_b, slopes):
    b, s, _ = h_b.shape

    def heads(t):
        return t.reshape(b, s, DIL_GROUPS, DIL_SLOTS, DIL_HEAD_DIM).transpose(2, 0, 3, 1, 4)

    q, k, v = (heads(t) for t in jnp.split(h_b, 3, axis=-1))
    outs, lses = [], []
    for g, (_, dil) in enumerate(DIL_PAIRS):
        o, lse = dilated_group_attention(q[g], k[g], v[g], dil, slopes[g * DIL_SLOTS:(g + 1) * DIL_SLOTS])
        outs.append(o)
        lses.append(lse)
    outs = jnp.stack(outs, axis=0)
    wts = jax.nn.softmax(jnp.stack(lses, axis=0), axis=0)
    comb = jnp.sum(wts[..., None] * outs.astype(jnp.float32), axis=0).astype(h_b.dtype)
    return comb.transpose(0, 2, 1, 3).reshape(b, s, DIL_SLOTS * DIL_HEAD_DIM)


def gqa_axial_mixer(hn, w_q, w_kv, q_gain, k_gain, w_o, cos_r, sin_r, cos_c, sin_c):
    b, s, _ = hn.shape
    q = (hn @ w_q).reshape(b, s, GQA_HEADS, GQA_HEAD_DIM).transpose(0, 2, 1, 3)
    kv = (hn @ w_kv).reshape(b, s, 2, GQA_KV_HEADS, GQA_HEAD_DIM)
    k = kv[:, :, 0].transpose(0, 2, 1, 3)
    v = kv[:, :, 1].transpose(0, 2, 1, 3)
    q = rmsnorm(q, q_gain)
    k = rmsnorm(k, k_gain)
    half = GQA_HEAD_DIM // 2

    def axial(t):
        return jnp.concatenate([apply_rope(t[..., :half], cos_r, sin_r),
                                apply_rope(t[..., half:], cos_c, sin_c)], axis=-1)

    o = blocked_attention(axial(q), axial(k), v, GQA_HEAD_DIM ** -0.5)
    return o.transpose(0, 2, 1, 3).reshape(b, s, GQA_HEADS * GQA_HEAD_DIM) @ w_o


def swiglu(hn, w_in, w_out):
    gate, up = jnp.split(hn @ w_in, 2, axis=-1)
    return (jax.nn.silu(gate) * up) @ w_out


def setup_inputs(seed: int = 0) -> dict:
    key = jax.random.key(seed)
    ks = jax.random.split(key, 18)
    f32 = jnp.float32

    def w(k, shape, fan_in):
        return jax.random.normal(k, shape, f32) * fan_in ** -0.5

    def gain(k, shape):
        return 1.0 + 0.02 * jax.random.normal(k, shape, f32)

    ne, no = N_EVEN, N_ODD
    return {
        "x": jax.random.normal(ks[0], (BATCH, SEQ, D_MODEL), f32),
        "mix_norm_ab": gain(ks[1], (ne, D_MODEL)),
        "w_in_ab": w(ks[2], (ne, D_MODEL, MIX_IN), D_MODEL),
        "mla_q_norm": gain(ks[3], (ne, MLA_Q_RANK)),
        "mla_kv_norm": gain(ks[4], (ne, MLA_KV_RANK)),
        "mla_w_uq": w(ks[5], (ne, MLA_Q_RANK, MLA_HEADS, MLA_NOPE + MLA_ROPE), MLA_Q_RANK),
        "mla_w_ukv": w(ks[6], (ne, MLA_KV_RANK, MLA_HEADS, MLA_NOPE + MLA_V), MLA_KV_RANK),
        "w_out_ab": w(ks[7], (ne, MIX_OUT, D_MODEL), MIX_OUT),
        "mix_norm_c": gain(ks[8], (no, D_MODEL)),
        "gqa_w_q": w(ks[9], (no, D_MODEL, GQA_HEADS * GQA_HEAD_DIM), D_MODEL),
        "gqa_w_kv": w(ks[10], (no, D_MODEL, 2 * GQA_KV_HEADS * GQA_HEAD_DIM), D_MODEL),
        "gqa_q_norm": gain(ks[11], (no, GQA_HEAD_DIM)),
        "gqa_k_norm": gain(ks[12], (no, GQA_HEAD_DIM)),
        "gqa_w_o": w(ks[13], (no, GQA_HEADS * GQA_HEAD_DIM, D_MODEL), GQA_HEADS * GQA_HEAD_DIM),
        "ffn_norm": gain(ks[14], (DEPTH, D_MODEL)),
        "ffn_w_in": w(ks[15], (DEPTH, D_MODEL, 2 * FFN_HIDDEN), D_MODEL),
        "ffn_w_out": w(ks[16], (DEPTH, FFN_HIDDEN, D_MODEL), FFN_HIDDEN),
        "final_norm": gain(ks[17], (D_MODEL,)),
    }


def reference(x, mix_norm_ab, w_in_ab, mla_q_norm, mla_kv_norm, mla_w_uq, mla_w_ukv, w_out_ab,
              mix_norm_c, gqa_w_q, gqa_w_kv, gqa_q_norm, gqa_k_norm, gqa_w_o,
              ffn_norm, ffn_w_in, ffn_w_out, final_norm):
    s = x.shape[1]
    rows = s // GRID_W
    pos = jnp.arange(s)
    cos_t, sin_t = rope_angles(pos, MLA_ROPE)
    row_idx = jnp.broadcast_to(jnp.arange(rows)[:, None], (rows, GRID_W)).reshape(-1)
    col_idx = jnp.broadcast_to(jnp.arange(GRID_W)[None, :], (rows, GRID_W)).reshape(-1)
    cos_r, sin_r = rope_angles(row_idx, GQA_HEAD_DIM // 2)
    cos_c, sin_c = rope_angles(col_idx, GQA_HEAD_DIM // 2)
    slopes = jnp.exp2(-8.0 * jnp.arange(1, DIL_HEADS + 1, dtype=jnp.float32) / DIL_HEADS)

    for layer in range(DEPTH):
        i = layer // 2
        if layer % 2 == 0:
            z = rmsnorm(x, mix_norm_ab[i]) @ w_in_ab[i]
            o_a = mla_mixer(z[..., :IN_A], mla_q_norm[i], mla_kv_norm[i], mla_w_uq[i], mla_w_ukv[i],
                            cos_t, sin_t)
            o_b = dilated_mixer(z[..., IN_A:], slopes)
            x = x + jnp.concatenate([o_a, o_b], axis=-1) @ w_out_ab[i]
        else:
            x = x + gqa_axial_mixer(rmsnorm(x, mix_norm_c[i]), gqa_w_q[i], gqa_w_kv[i], gqa_q_norm[i],
                                    gqa_k_norm[i], gqa_w_o[i], cos_r, sin_r, cos_c, sin_c)
        x = x + swiglu(rmsnorm(x, ffn_norm[layer]), ffn_w_in[layer], ffn_w_out[layer])
    return rmsnorm(x, final_norm)
```

```python
import contextlib
import numpy as np
import ml_dtypes
import concourse.bass as bass
import concourse.mybir as mybir
from concourse.bass_utils import run_bass_kernel_spmd

F32 = mybir.dt.float32
BF16 = mybir.dt.bfloat16
ALU = mybir.AluOpType
ACTF = mybir.ActivationFunctionType

D = 1024
BATCH = 4
SEQ = 8192
T = 4096
NCH = T // 512
EPS = 1e-6
FFN_H = 2816
N_CORES = 8
N_DUMMY = 0
SG = 3
VW = 272


class Buf:
    __slots__ = ("w", "r", "name")

    def __init__(self, name=""):
        self.w = None
        self.r = {}
        self.name = name


class Prog:
    ENGS = ("sync", "scalar", "vector", "tensor", "gpsimd")

    def __init__(self, nc, n_dma_sems=24):
        self.nc = nc
        self.lists = {e: [] for e in self.ENGS}
        self.sem = {e: nc.alloc_semaphore(name="c_" + e) for e in self.ENGS}
        self.cnt = {e: 0 for e in self.ENGS}
        self.waited = {e: {} for e in self.ENGS}
        self.semobj = {}
        for e in self.ENGS:
            self.semobj[id(self.sem[e])] = self.sem[e]
        self.dma_ring = {}
        for q in ("sync", "gpsimd", "scalar"):
            sems = [nc.alloc_semaphore(name="d_%s_%d" % (q, i)) for i in range(n_dma_sems)]
            for s in sems:
                self.semobj[id(s)] = s
            self.dma_ring[q] = {"sems": sems, "vals": [0] * n_dma_sems, "i": 0}
        self.cc_sem = nc.alloc_semaphore(name="c_collective")
        self.semobj[id(self.cc_sem)] = self.cc_sem
        self.cc_cnt = 0
        self.n_ops = 0

    def _wait(self, eng, toks):
        need = {}
        for (sid, val) in toks:
            if need.get(sid, 0) < val:
                need[sid] = val
        for sid, val in need.items():
            if sid == id(self.sem[eng]) and eng == "tensor":
                continue
            if self.waited[eng].get(sid, 0) >= val:
                continue
            self.waited[eng][sid] = val
            self.lists[eng].append(("wait", self.semobj[sid], val))

    def _deps(self, reads, writes):
        toks = []
        for b in reads:
            if b.w is not None:
                toks.append(b.w)
        for b in writes:
            if b.w is not None:
                toks.append(b.w)
            toks.extend(b.r.items())
        return toks

    def _commit(self, tok, reads, writes):
        for b in reads:
            if b.r.get(tok[0], 0) < tok[1]:
                b.r[tok[0]] = tok[1]
        for b in writes:
            b.w = tok
            b.r = {}

    def op(self, eng, emit, reads=(), writes=()):
        self._wait(eng, self._deps(reads, writes))
        self.cnt[eng] += 1
        tok = (id(self.sem[eng]), self.cnt[eng])
        self.lists[eng].append(("op", emit, self.sem[eng], 1))
        self._commit(tok, reads, writes)
        self.n_ops += 1
        return tok

    def dma(self, q, out, in_, reads=(), writes=(), **kw):
        ring = self.dma_ring[q]
        k = ring["i"] % len(ring["sems"])
        ring["i"] += 1
        s = ring["sems"][k]
        toks = self._deps(reads, writes)
        if ring["vals"][k] > 0:
            toks.append((id(s), ring["vals"][k]))
        self._wait_dma(q, toks)
        ring["vals"][k] += 16
        tok = (id(s), ring["vals"][k])
        self.lists[q].append(("op", lambda e: e.dma_start(out=out, in_=in_, **kw), s, 16))
        self._commit(tok, reads, writes)
        self.n_ops += 1
        return tok

    def _wait_dma(self, q, toks):
        need = {}
        for (sid, val) in toks:
            if need.get(sid, 0) < val:
                need[sid] = val
        for sid, val in need.items():
            if self.waited[q].get(sid, 0) >= val:
                continue
            self.waited[q][sid] = val
            self.lists[q].append(("wait", self.semobj[sid], val))

    def collective(self, kind, ins, outs, groups, reads=(), writes=()):
        q = "gpsimd"
        self._wait_dma(q, self._deps(reads, writes))
        self.cc_cnt += 1
        tok = (id(self.cc_sem), self.cc_cnt)
        self.lists[q].append(("op", lambda e: e.collective_compute(
            kind, ALU.bypass, replica_groups=groups, ins=ins, outs=outs), self.cc_sem, 1))
        self._commit(tok, reads, writes)
        return tok

    def barrier(self):
        toks = [(id(self.sem[e]), self.cnt[e]) for e in self.ENGS if self.cnt[e] > 0]
        if self.cc_cnt > 0:
            toks.append((id(self.cc_sem), self.cc_cnt))
        for q, ring in self.dma_ring.items():
            toks += [(id(s), v) for s, v in zip(ring["sems"], ring["vals"]) if v > 0]
        for e in self.ENGS:
            self._wait_dma(e, toks)

    def wait_all(self, eng, bufs):
        toks = []
        for b in bufs:
            if b.w is not None:
                toks.append(b.w)
            toks.extend(b.r.items())
        self._wait_dma(eng, toks)

    def emit(self):
        nc = self.nc
        with nc.Block() as block:
            def runner(name):
                def body(e):
                    for it in self.lists[name]:
                        if it[0] == "wait":
                            e.wait_ge(it[1], it[2])
                        else:
                            ins = it[1](e)
                            ins.then_inc(it[2], it[3])
                return body
            block.sync(runner("sync"))
            block.scalar(runner("scalar"))
            block.vector(runner("vector"))
            block.tensor(runner("tensor"))
            block.gpsimd(runner("gpsimd"))


class Ctx:
    def __init__(self, nc):
        self.nc = nc
        self.P = Prog(nc)
        self.stack = contextlib.ExitStack()
        self.banks = []
        self.bank_bufs = []
        self.trip = []
        for i in range(2):
            t = self.stack.enter_context(nc.psum_tensor("trip%d" % i, [128, SG * 512], F32))
            self.trip.append(t)
            for hf in range(SG):
                self.banks.append(t[:, hf * 512:(hf + 1) * 512])
                self.bank_bufs.append(Buf("bank%d" % (SG * i + hf)))
        for i in range(2):
            t = self.stack.enter_context(nc.psum_tensor("accb%d" % i, [128, 512], F32))
            self.banks.append(t[:, :])
            self.bank_bufs.append(Buf("bank%d" % (2 * SG + i)))
        self.uid = 0

    def sb(self, shape, dtype, name=None, stack=None):
        self.uid += 1
        nm = "%s_%d" % (name or "t", self.uid)
        return (stack or self.stack).enter_context(self.nc.sbuf_tensor(nm, list(shape), dtype))


def phase_transpose_in(cx, x_ap, xT_ap, xT_b, ident_f32):
    P, nc = cx.P, cx.nc
    with contextlib.ExitStack() as st:
        NB = 2
        xin = [cx.sb([128, D], F32, "xin", st) for _ in range(NB)]
        xin_b = [Buf() for _ in range(NB)]
        stg = [cx.sb([128, 8, 512], F32, "xTstg", st) for _ in range(2)]
        stg_b = [Buf() for _ in range(2)]
        ident, ident_b = ident_f32
        for tt in range(T // 128):
            c, j = tt // 4, tt % 4
            xi, xb = xin[tt % NB], xin_b[tt % NB]
            P.dma("sync", xi[:], x_ap[tt * 128:(tt + 1) * 128, :], writes=[xb])
            sg, sgb = stg[c % 2], stg_b[c % 2]
            for half in range(2):
                bank = (tt * 2 + half) % 8
                pb, pbb = cx.banks[bank], cx.bank_bufs[bank]
                for k in range(4):
                    dc = half * 4 + k
                    P.op("tensor", lambda e, pb=pb, xi=xi, dc=dc, k=k: e.transpose(
                        pb[:, k * 128:(k + 1) * 128], xi[:, dc * 128:(dc + 1) * 128], ident[:]),
                        reads=[xb, ident_b], writes=[pbb])
                eng = "vector" if half == 0 else "scalar"
                src = pb[:].rearrange("p (k t) -> p k t", k=4)
                dst = sg[:, half * 4:(half + 1) * 4, j * 128:(j + 1) * 128]
                if eng == "vector":
                    P.op("vector", lambda e, dst=dst, src=src: e.tensor_copy(out=dst, in_=src),
                         reads=[pbb], writes=[sgb])
                else:
                    P.op("scalar", lambda e, dst=dst, src=src: e.activation(out=dst, in_=src, func=ACTF.Copy),
                         reads=[pbb], writes=[sgb])
            if j == 3:
                P.dma("gpsimd", xT_ap.rearrange("(k p) t -> p k t", p=128)[:, :, c * 512:(c + 1) * 512],
                      sg[:], reads=[sgb], writes=xT_b[c])


def phase_final(cx, xT_ap, xT_b, out_ap, g_bc, ident_f32):
    P, nc = cx.P, cx.nc
    ident, ident_b = ident_f32
    with contextlib.ExitStack() as st:
        g_t = cx.sb([128, D], F32, "gfin", st)
        g_b = Buf()
        P.dma("sync", g_t[:], g_bc[:, :], writes=[g_b])
        xs = [cx.sb([128, 8, 512], F32, "fx", st) for _ in range(2)]
        xs_b = [Buf() for _ in range(2)]
        ot = [cx.sb([128, D], F32, "fo", st) for _ in range(2)]
        ot_b = [Buf() for _ in range(2)]
        junk = cx.sb([128, 512], F32, "fjunk", st)
        junk_b = Buf()
        ss = [cx.sb([128, 2], F32, "fss", st) for _ in range(2)]
        ss_b = [Buf() for _ in range(2)]
        rs = [cx.sb([128, 1], F32, "frs", st) for _ in range(2)]
        rs_b = [Buf() for _ in range(2)]
        for c in range(NCH):
            xc, xcb = xs[c % 2], xs_b[c % 2]
            P.dma("sync", xc[:], xT_ap.rearrange("(k p) t -> p k t", p=128)[:, :, c * 512:(c + 1) * 512],
                  reads=xT_b[c], writes=[xcb])
            for j in range(4):
                tt = c * 4 + j
                o, ob = ot[tt % 2], ot_b[tt % 2]
                s_, sb_ = ss[tt % 2], ss_b[tt % 2]
                r_, rb_ = rs[tt % 2], rs_b[tt % 2]
                bk = [(tt * 2) % 8, (tt * 2 + 1) % 8]
                for half in range(2):
                    pb, pbb = cx.banks[bk[half]], cx.bank_bufs[bk[half]]
                    for k in range(4):
                        dc = half * 4 + k
                        P.op("tensor", lambda e, pb=pb, xc=xc, dc=dc, k=k, j=j: e.transpose(
                            pb[:, k * 128:(k + 1) * 128], xc[:, dc, j * 128:(j + 1) * 128], ident[:]),
                            reads=[xcb, ident_b], writes=[pbb])
                    P.op("scalar", lambda e, pb=pb, s_=s_, half=half: e.activation(
                        out=junk[:], in_=pb[:], func=ACTF.Square, accum_out=s_[:, half:half + 1]),
                        reads=[pbb], writes=[junk_b, sb_])
                P.op("vector", lambda e, s_=s_, r_=r_: e.tensor_tensor(
                    out=r_[:], in0=s_[:, 0:1], in1=s_[:, 1:2], op=ALU.add), reads=[sb_], writes=[rb_])
                P.op("scalar", lambda e, r_=r_: e.activation(
                    out=r_[:], in_=r_[:], func=ACTF.Sqrt, scale=1.0 / D, bias=EPS), reads=[rb_], writes=[rb_])
                P.op("vector", lambda e, r_=r_: e.reciprocal(out=r_[:], in_=r_[:]), reads=[rb_], writes=[rb_])
                for half in range(2):
                    pb, pbb = cx.banks[bk[half]], cx.bank_bufs[bk[half]]
                    P.op("vector", lambda e, pb=pb, o=o, r_=r_, half=half: e.scalar_tensor_tensor(
                        out=o[:, half * 512:(half + 1) * 512], in0=pb[:], scalar=r_[:, 0:1],
                        in1=g_t[:, half * 512:(half + 1) * 512], op0=ALU.mult, op1=ALU.mult),
                        reads=[pbb, rb_, g_b], writes=[ob])
                P.dma("gpsimd", out_ap[tt * 128:(tt + 1) * 128, :], o[:], reads=[ob])


def load_w(cx, st, wdst, wbufs, w_ap, K, N, row_gain=None, col0=0, piece=1024):
    for _ in load_w_iter(cx, wdst, wbufs, w_ap, K, N, col0, piece):
        pass


def load_w_iter(cx, wdst, wbufs, w_ap, K, N, col0=0, piece=1024):
    P = cx.P
    KC = K // 128
    for kc in range(KC):
        P.dma("gpsimd", wdst[:, kc, col0:col0 + N], w_ap[kc * 128:(kc + 1) * 128, :], writes=[wbufs[kc % 2]])
        yield


def rms_chunk(cx, xc, xcb, KC, xn, xnb, sqs, sqs_b, ones_bf, statbank, rstd, rstd_b, Dn, ncol=512, gain=None):
    P = cx.P
    ones, ones_b = ones_bf
    pb, pbb = cx.banks[statbank], cx.bank_bufs[statbank]
    for k in range(KC):
        sq, sqb = sqs[k % len(sqs)], sqs_b[k % len(sqs)]
        P.op("scalar", lambda e, sq=sq, k=k: e.activation(out=sq[:, 0:ncol], in_=xc[:, k, 0:ncol], func=ACTF.Square),
             reads=[xcb], writes=[sqb])
        P.op("tensor", lambda e, sq=sq, k=k: e.matmul(pb[:, 0:ncol], lhsT=ones[:, :], rhs=sq[:, 0:ncol],
                                                       start=(k == 0), stop=(k == KC - 1)),
             reads=[sqb, ones_b], writes=[pbb])
    P.op("scalar", lambda e: e.activation(out=rstd[:, 0:ncol], in_=pb[:, 0:ncol], func=ACTF.Sqrt, scale=1.0 / Dn, bias=EPS),
         reads=[pbb], writes=[rstd_b])
    P.op("vector", lambda e: e.reciprocal(out=rstd[:, 0:ncol], in_=rstd[:, 0:ncol]), reads=[rstd_b], writes=[rstd_b])
    g, gb = gain
    for k in range(KC):
        P.op("vector", lambda e, k=k: e.scalar_tensor_tensor(
            out=xn[:, k, 0:ncol], in0=xc[:, k, 0:ncol], scalar=g[:, k:k + 1], in1=rstd[:, 0:ncol], op0=ALU.mult, op1=ALU.mult),
            reads=[xcb, rstd_b, gb], writes=[xnb[k % 2]])


def xT_view(xT_ap, c):
    return xT_ap.rearrange("(k p) t -> p k t", p=128)[:, :, c * 512:(c + 1) * 512]


def ffn_weights(cx, st, w_in_ap, w_out_ap, g_ap):
    P = cx.P
    HC = FFN_H // 128
    w1 = cx.sb([128, 8, 2 * FFN_H], BF16, "w1", st)
    w2 = cx.sb([128, HC, D], BF16, "w2", st)
    w1b, w2b = [Buf(), Buf()], [Buf(), Buf()]
    g = cx.sb([128, 8], F32, "ffg", st)
    gb = Buf()
    P.dma("sync", g[:], g_ap, writes=[gb])

    def gen():
        yield from load_w_iter(cx, w1, w1b, w_in_ap, D, 2 * FFN_H)
        yield from load_w_iter(cx, w2, w2b, w_out_ap, FFN_H, D)
    return (w1, w2, w1b, w2b, g, gb), gen()


def phase_outproj_ffn(cx, xT_ap, xT_b, wo_ap, KC, scr, w_in_ap, w_out_ap, g_ap, consts):
    P = cx.P
    with contextlib.ExitStack() as st:
        pre, gen = ffn_weights(cx, st, w_in_ap, w_out_ap, g_ap)
        phase_out_proj(cx, xT_ap, xT_b, wo_ap, KC, scr, bg=gen)
        for _ in gen:
            pass
        P.barrier()
        phase_ffn(cx, xT_ap, xT_b, w_in_ap, w_out_ap, g_ap, consts, pre=(st, pre))


def phase_ffn(cx, xT_ap, xT_b, w_in_ap, w_out_ap, g_ap, consts, pre=None):
    P, nc = cx.P, cx.nc
    ones_bf = consts["ones_bf"]
    HC = FFN_H // 128
    with contextlib.ExitStack() as st_own:
        if pre is None:
            st = st_own
            (w1, w2, w1b, w2b, g, gb), gen = ffn_weights(cx, st, w_in_ap, w_out_ap, g_ap)
            for _ in gen:
                pass
        else:
            st, (w1, w2, w1b, w2b, g, gb) = pre
            st = st_own
        xs = [cx.sb([128, 8, 512], F32, "fx", st)]
        xs_b = [Buf()]
        xr = [cx.sb([128, 512], F32, "fxr", st) for _ in range(2)]
        xr_b = [Buf(), Buf()]
        xns = [cx.sb([128, 8, 512], BF16, "fxn", st) for _ in range(2)]
        xnbs = [[Buf(), Buf()], [Buf(), Buf()]]
        act = cx.sb([128, HC, 512], BF16, "fact", st)
        act_b = [Buf() for _ in range(HC)]
        sqs = [cx.sb([128, 512], BF16, "fsq", st) for _ in range(2)]
        sqs_b = [Buf(), Buf()]
        sg = [cx.sb([128, 512], BF16, "fsg", st) for _ in range(2)]
        sg_b = [Buf(), Buf()]
        rstd = cx.sb([128, 512], F32, "frstd", st)
        rstd_b = Buf()

        def load(c):
            P.dma("sync", xs[0][:], xT_view(xT_ap, c), reads=xT_b[c], writes=[xs_b[0]])

        def norm(c):
            rms_chunk(cx, xs[0], xs_b[0], 8, xns[c % 2], xnbs[c % 2], sqs, sqs_b, ones_bf, 6, rstd, rstd_b, D, gain=(g, gb))

        load(0)
        norm(0)
        for c in range(NCH):
            xn, xnb = xns[c % 2], xnbs[c % 2]
            if c + 1 < NCH:
                load(c + 1)
            for j in range(HC):
                if j == 6 and c + 1 < NCH:
                    norm(c + 1)
                ga, gab = cx.banks[j % 2], cx.bank_bufs[j % 2]
                ua, uab = cx.banks[2 + j % 2], cx.bank_bufs[2 + j % 2]
                for k in range(8):
                    P.op("tensor", lambda e, ga=ga, j=j, k=k, xn=xn: e.matmul(
                        ga[:, :], lhsT=w1[:, k, j * 128:(j + 1) * 128], rhs=xn[:, k, :], start=(k == 0), stop=(k == 7)),
                        reads=xnb + w1b, writes=[gab])
                for k in range(8):
                    P.op("tensor", lambda e, ua=ua, j=j, k=k, xn=xn: e.matmul(
                        ua[:, :], lhsT=w1[:, k, FFN_H + j * 128:FFN_H + (j + 1) * 128], rhs=xn[:, k, :],
                        start=(k == 0), stop=(k == 7)), reads=xnb + w1b, writes=[uab])
                s_, sb_ = sg[j % 2], sg_b[j % 2]
                P.op("scalar", lambda e, s_=s_, ga=ga: e.activation(out=s_[:], in_=ga[:, :], func=ACTF.Silu),
                     reads=[gab], writes=[sb_])
                P.op("vector", lambda e, s_=s_, ua=ua, j=j: e.tensor_tensor(
                    out=act[:, j, :], in0=ua[:, :], in1=s_[:], op=ALU.mult), reads=[uab, sb_], writes=[act_b[j]])
            for dc in range(8):
                ya, yab = cx.banks[4 + dc % 2], cx.bank_bufs[4 + dc % 2]
                xr_, xrb_ = xr[dc % 2], xr_b[dc % 2]
                xsl = xT_ap[dc * 128:(dc + 1) * 128, c * 512:(c + 1) * 512]
                P.dma("sync", xr_[:], xsl, reads=[xT_b[c][dc]], writes=[xrb_])
                for j in range(HC):
                    P.op("tensor", lambda e, ya=ya, j=j, dc=dc: e.matmul(
                        ya[:, :], lhsT=w2[:, j, dc * 128:(dc + 1) * 128], rhs=act[:, j, :],
                        start=(j == 0), stop=(j == HC - 1)), reads=[act_b[j]] + w2b, writes=[yab])
                P.op("vector", lambda e, ya=ya, xr_=xr_: e.tensor_tensor(
                    out=xr_[:], in0=ya[:, :], in1=xr_[:], op=ALU.add), reads=[yab], writes=[xrb_])
                P.dma("gpsimd", xsl, xr_[:], reads=[xrb_], writes=[xT_b[c][dc]])


GQ_L = [0, 1, 2, 3, 8, 9, 10, 11]
GQ_U = [4, 5, 6, 7, 12, 13, 14, 15]


class AttnStream:
    PV_LAG = 2

    def __init__(self, cx, st_bufs):
        self.cx = cx
        self.b = st_bufs
        self.pending = []
        self.gidx = 0
        self.deferred = []
        self.mask_i = 0

    def block(self, q_rhs, q_bufs, tiles, scale, vcols, acc_bank, finish=None):
        cx, P, b = self.cx, self.cx.P, self.b
        pts, pts_b = b["pt"], b["pt_b"]
        n = len(tiles)
        groups = [list(range(i, min(i + SG, n))) for i in range(0, n, SG)]
        for gi, grp in enumerate(groups):
            u = b["cnt"][0]
            b["cnt"][0] += 1
            slot = u % 2
            trip = cx.trip[slot]
            pi = self.gidx % len(pts)
            self.gidx += 1
            pt, ptb = pts[pi], pts_b[pi]
            bbs = []
            for ii, ti in enumerate(grp):
                lhsT, v_ap, rb, mask, q_ap = tiles[ti]
                rhs = q_rhs if q_ap is None else q_ap
                sbb_ = cx.bank_bufs[SG * slot + ii]
                bbs.append(sbb_)
                P.op("tensor", lambda e, trip=trip, ii=ii, lhsT=lhsT, rhs=rhs: e.matmul(
                    trip[:, ii * 512:(ii + 1) * 512], lhsT=lhsT, rhs=rhs, start=True, stop=True),
                    reads=list(rb) + list(q_bufs), writes=[sbb_])
            w = 512 * len(grp)
            P.op("scalar", lambda e, trip=trip, pt=pt, w=w: e.activation(
                out=pt[:, 0:w], in_=trip[:, 0:w], func=ACTF.Exp, scale=scale), reads=bbs, writes=[ptb])
            for ii, ti in enumerate(grp):
                mask = tiles[ti][3]
                if mask is not None:
                    self.mask_i += 1
                    P.op("gpsimd" if self.mask_i % 4 == 0 else "vector", lambda e, pt=pt, mask=mask, ii=ii: e.tensor_tensor(
                        out=pt[:, ii * 512:(ii + 1) * 512], in0=pt[:, ii * 512:(ii + 1) * 512], in1=mask, op=ALU.mult),
                        reads=[ptb] + b.get("mask_b", []), writes=[ptb])
            items = [(tiles[ti][1], tiles[ti][2], ii, ti == 0, ti == n - 1) for ii, ti in enumerate(grp)]
            self.pending.append((items, pt, ptb, acc_bank, vcols, finish if gi == len(groups) - 1 else None))
            if len(self.pending) > self.PV_LAG:
                self._pv(self.pending.pop(0))
            self._tick()

    def _tick(self):
        for d in self.deferred:
            d[0] -= 1
        while self.deferred and self.deferred[0][0] <= 0:
            self.deferred.pop(0)[1]()

    def _pv(self, entry):
        cx, P = self.cx, self.cx.P
        items, pt, ptb, acc_bank, vcols, finish = entry
        ab, abb = cx.banks[acc_bank], cx.bank_bufs[acc_bank]
        for v_ap, rb, ii, first, last in items:
            P.op("tensor", lambda e, v_ap=v_ap, pt=pt, ii=ii, first=first, last=last, ab=ab, vcols=vcols: e.matmul(
                ab[0:vcols, :], lhsT=v_ap, rhs=pt[:, ii * 512:(ii + 1) * 512], start=first, stop=last),
                reads=list(rb) + [ptb], writes=[abb])
        if finish is not None:
            part2 = finish()
            if part2 is not None:
                self.deferred.append([3, part2])

    def flush(self):
        while self.pending:
            self._pv(self.pending.pop(0))
        while self.deferred:
            self.deferred.pop(0)[1]()


def take_bank(cx, st_bufs):
    u = st_bufs["cnt"][0]
    st_bufs["cnt"][0] += 1
    return SG * (u % 2)


def attn_finish(cx, st_bufs, acc_bank, bc_bank, dst_ap, dst_bufs, consts):
    P = cx.P
    ab, abb = cx.banks[acc_bank], cx.bank_bufs[acc_bank]
    ones, ones_b = consts["ones_bf"]
    k = st_bufs["fin_i"][0]
    st_bufs["fin_i"][0] += 1
    rd, rdb = st_bufs["rd"][k % 2], st_bufs["rd_b"][k % 2]
    rdh, rdhb = st_bufs["rdh"][k % 2], st_bufs["rdh_b"][k % 2]
    bcs, bcsb = st_bufs["bcs"][k % 2], st_bufs["bcs_b"][k % 2]
    ot, otb = st_bufs["ot"][k % 2], st_bufs["ot_b"][k % 2]
    P.op("vector", lambda e: e.reciprocal(out=rd[64:65, :], in_=ab[64:65, :]), reads=[abb], writes=[rdb])
    P.op("vector", lambda e: e.tensor_copy(out=rdh[64:65, :], in_=rd[64:65, :]), reads=[rdb], writes=[rdhb])

    def part2():
        bc_bank = take_bank(cx, st_bufs)
        bb, bbb = cx.banks[bc_bank], cx.bank_bufs[bc_bank]
        P.op("tensor", lambda e: e.matmul(bb[0:64, :], lhsT=ones[64:65, 0:64], rhs=rdh[64:65, :], start=True, stop=True),
             reads=[rdhb, ones_b], writes=[bbb])
        P.op("vector", lambda e: e.tensor_copy(out=bcs[0:64, :], in_=bb[0:64, :]), reads=[bbb], writes=[bcsb])
        P.op("vector", lambda e: e.tensor_tensor(out=ot[0:64, :], in0=ab[0:64, :], in1=bcs[0:64, :], op=ALU.mult),
             reads=[abb, bcsb], writes=[otb])
        P.dma("sync", dst_ap, ot[0:64, :], reads=[otb], writes=dst_bufs)
    return part2


def attn_bufs(cx, st):
    b = {}
    b["pt"] = [cx.sb([128, SG * 512], BF16, "pt", st) for _ in range(3)]
    b["pt_b"] = [Buf() for _ in range(3)]
    b["cnt"] = [0]
    b["fin_i"] = [0]
    b["rd"] = [cx.sb([128, 512], F32, "rd", st) for _ in range(2)]
    b["rd_b"] = [Buf(), Buf()]
    b["rdh"] = [cx.sb([128, 512], BF16, "rdh", st) for _ in range(2)]
    b["rdh_b"] = [Buf(), Buf()]
    b["bcs"] = [cx.sb([64, 512], F32, "bcs", st) for _ in range(2)]
    b["bcs_b"] = [Buf(), Buf()]
    b["ot"] = [cx.sb([64, 512], BF16, "ot", st) for _ in range(2)]
    b["ot_b"] = [Buf(), Buf()]
    return b


def phase_gqa_proj(cx, xT_ap, xT_b, aps, consts, scr):
    P, nc = cx.P, cx.nc
    ones_bf = consts["ones_bf"]
    blk, blk_b = consts["blk_bf"]
    NW = 2816
    with contextlib.ExitStack() as st:
        w = cx.sb([128, 8, NW], BF16, "gw", st)
        wb = [Buf(), Buf()]
        g = cx.sb([128, 8], F32, "gg", st)
        gb = Buf()
        P.dma("sync", g[:], aps["norm"], writes=[gb])
        load_w(cx, st, w, wb, aps["w"], D, NW, row_gain=(g, gb))
        hg = cx.sb([128, 4], F32, "ghg", st)
        hgb = Buf()
        P.dma("sync", hg[:], aps["hgain"], writes=[hgb])
        ctab = cx.sb([128, T], F32, "ctab", st)
        stab = cx.sb([128, T], F32, "stab", st)
        tab_b = Buf()
        P.dma("sync", ctab[:], aps["ctab"], writes=[tab_b])
        P.dma("scalar", stab[:], aps["stab"], writes=[tab_b])
        xs = cx.sb([128, 8, 512], F32, "gx", st)
        xsb = Buf()
        xns = [cx.sb([128, 8, 512], BF16, "gxn", st) for _ in range(2)]
        xnbs = [[Buf(), Buf()], [Buf(), Buf()]]
        sqs = [cx.sb([128, 512], BF16, "gsq", st) for _ in range(2)]
        sqs_b = [Buf(), Buf()]
        rstd = cx.sb([128, 512], F32, "grstd", st)
        rstd_b = Buf()
        hr = [cx.sb([128, 512], F32, "ghr", st) for _ in range(2)]
        hr_b = [Buf(), Buf()]
        t1 = [cx.sb([128, 512], F32, "gt1", st) for _ in range(2)]
        t1_b = [Buf(), Buf()]
        t2 = [cx.sb([128, 512], F32, "gt2", st) for _ in range(2)]
        t2_b = [Buf(), Buf()]
        qo = [cx.sb([128, 512], BF16, "gqo", st) for _ in range(2)]
        qo_b = [Buf(), Buf()]
        vst = [cx.sb([128, 4, 256], BF16, "gvst", st) for _ in range(2)]
        vst_b = [Buf(), Buf()]
        it = 0

        def xload(c):
            P.dma("sync", xs[:], xT_view(xT_ap, c), reads=xT_b[c], writes=[xsb])

        def xnorm(c):
            rms_chunk(cx, xs, xsb, 8, xns[c % 2], xnbs[c % 2], sqs, sqs_b, ones_bf, 6, rstd, rstd_b, D, gain=(g, gb))

        xload(0)
        xnorm(0)
        for c in range(NCH):
            xn, xnb = xns[c % 2], xnbs[c % 2]
            if c + 1 < NCH:
                xload(c + 1)
            for cc in range(10):
                if cc == 5 and c + 1 < NCH:
                    xnorm(c + 1)
                isq = cc < 8
                ca = cc * 128 if isq else 2048 + (cc - 8) * 128
                cbo = 1024 + cc * 128 if isq else 2304 + (cc - 8) * 128
                gcol = 0 if isq else 2
                A, Ab = cx.banks[it % 2], cx.bank_bufs[it % 2]
                B, Bb = cx.banks[2 + it % 2], cx.bank_bufs[2 + it % 2]
                S, Sb = cx.banks[4 + it % 2], cx.bank_bufs[4 + it % 2]
                for k in range(8):
                    P.op("tensor", lambda e, A=A, k=k, ca=ca, xn=xn: e.matmul(
                        A[:, :], lhsT=w[:, k, ca:ca + 128], rhs=xn[:, k, :], start=(k == 0), stop=(k == 7)),
                        reads=xnb + wb, writes=[Ab])
                for k in range(8):
                    P.op("tensor", lambda e, B=B, k=k, cbo=cbo, xn=xn: e.matmul(
                        B[:, :], lhsT=w[:, k, cbo:cbo + 128], rhs=xn[:, k, :], start=(k == 0), stop=(k == 7)),
                        reads=xnb + wb, writes=[Bb])
                sq, sqb = sqs[it % 2], sqs_b[it % 2]
                P.op("scalar", lambda e, sq=sq, A=A: e.activation(out=sq[:], in_=A[:, :], func=ACTF.Square),
                     reads=[Ab], writes=[sqb])
                P.op("tensor", lambda e, S=S, sq=sq: e.matmul(S[:, :], lhsT=blk[:, :], rhs=sq[:], start=True, stop=True),
                     reads=[sqb, blk_b], writes=[Sb])
                h_, hb_ = hr[it % 2], hr_b[it % 2]
                P.op("scalar", lambda e, h_=h_, S=S: e.activation(
                    out=h_[:], in_=S[:, :], func=ACTF.Sqrt, scale=1.0 / 64, bias=EPS), reads=[Sb], writes=[hb_])
                P.op("vector", lambda e, h_=h_: e.reciprocal(out=h_[:], in_=h_[:]), reads=[hb_], writes=[hb_])
                a_, ab_ = t1[it % 2], t1_b[it % 2]
                b_, bb_ = t2[it % 2], t2_b[it % 2]
                P.op("vector", lambda e, a_=a_, A=A, gcol=gcol, c=c: e.scalar_tensor_tensor(
                    out=a_[:], in0=A[:, :], scalar=hg[:, gcol:gcol + 1], in1=ctab[:, c * 512:(c + 1) * 512],
                    op0=ALU.mult, op1=ALU.mult), reads=[Ab, hgb, tab_b], writes=[ab_])
                P.op("vector", lambda e, b_=b_, B=B, gcol=gcol, c=c: e.scalar_tensor_tensor(
                    out=b_[:], in0=B[:, :], scalar=hg[:, gcol + 1:gcol + 2], in1=stab[:, c * 512:(c + 1) * 512],
                    op0=ALU.mult, op1=ALU.mult), reads=[Bb, hgb, tab_b], writes=[bb_])
                P.op("gpsimd", lambda e, a_=a_, b_=b_: e.tensor_tensor(out=a_[:], in0=a_[:], in1=b_[:], op=ALU.add),
                     reads=[ab_, bb_], writes=[ab_])
                q_, qb_ = qo[it % 2], qo_b[it % 2]
                P.op("gpsimd", lambda e, a_=a_, h_=h_, q_=q_: e.tensor_tensor(out=q_[:], in0=a_[:], in1=h_[:], op=ALU.mult),
                     reads=[ab_, hb_], writes=[qb_])
                if isq:
                    dst = scr["qT"][cc * 128:(cc + 1) * 128, c * 512:(c + 1) * 512]
                    P.dma("sync", dst, q_[:], reads=[qb_], writes=[scr["qT_b"][cc]])
                else:
                    dst = scr["kT"][(cc - 8) * 128:(cc - 7) * 128, c * 512:(c + 1) * 512]
                    P.dma("sync", dst, q_[:], reads=[qb_], writes=[scr["kT_b"]])
                it += 1
            vs, vsb = vst[c % 2], vst_b[c % 2]
            for j in range(4):
                V, Vb = cx.banks[7], cx.bank_bufs[7]
                for k in range(8):
                    P.op("tensor", lambda e, V=V, k=k, j=j, xn=xn: e.matmul(
                        V[:, 0:256], lhsT=xn[:, k, j * 128:(j + 1) * 128], rhs=w[:, k, 2560:2816],
                        start=(k == 0), stop=(k == 7)), reads=xnb + wb, writes=[Vb])
                P.op("scalar", lambda e, V=V, vs=vs, j=j: e.activation(
                    out=vs[:, j, :], in_=V[:, 0:256], func=ACTF.Copy), reads=[Vb], writes=[vsb])
            P.dma("gpsimd", scr["v"][c * 512:(c + 1) * 512, :].rearrange("(j p) f -> p j f", p=128), vs[:],
                  reads=[vsb], writes=[scr["v_b"]])


def phase_gqa_attn(cx, consts, scr):
    P = cx.P
    with contextlib.ExitStack() as st:
        kres = cx.sb([128, 2, SEQ], BF16, "kres", st)
        kres_b = Buf()
        VT = 260
        vflat = cx.sb([128, 64 * VT + 128], BF16, "vres", st)
        vres = vflat[:, 0:64 * VT].rearrange("p (t f) -> p t f", f=VT)
        vres_b = Buf()
        for r in range(2):
            for pr in range(2):
                P.dma("sync" if pr == 0 else "scalar", kres[:, pr, r * T:(r + 1) * T],
                      scr["kT_all"][r * 256 + pr * 128:r * 256 + (pr + 1) * 128, :], reads=[scr["kT_all_b"]], writes=[kres_b])
        P.op("vector", lambda e: e.memset(vflat[:], 1.0), writes=[vres_b])
        for i in range(4):
            for hh in range(4):
                P.dma("sync" if hh % 2 == 0 else "scalar", vres[:, i * 16:(i + 1) * 16, hh * 65:hh * 65 + 64],
                      scr["v_all"][i * 2048:(i + 1) * 2048, hh * 64:(hh + 1) * 64].rearrange("(t p) d -> p t d", p=128),
                      reads=[scr["v_all_b"]], writes=[vres_b])
        qz = [[cx.sb([128, T], BF16, "qz", st) for _ in range(2)] for _ in range(2)]
        qz_b = [[Buf(), Buf()] for _ in range(2)]
        for i in range(2):
            P.op("gpsimd", lambda e, i=i: e.memset(qz[i][0][64:128, :], 0.0), writes=[qz_b[i][0]])
            P.op("gpsimd", lambda e, i=i: e.memset(qz[i][1][0:64, :], 0.0), writes=[qz_b[i][1]])
        ab = attn_bufs(cx, st)
        stream = AttnStream(cx, ab)
        blk_i = 0
        for c in range(8):
            pr = c // 4
            for hf in range(2):
                lo = hf * 64
                P.dma("sync" if hf == 0 else "scalar", qz[c % 2][hf][lo:lo + 64, :], scr["qT"][c * 128 + lo:c * 128 + lo + 64, :],
                      reads=[scr["qT_b"][c]], writes=[qz_b[c % 2][hf]])
            for hf in range(2):
                gkv = 2 * pr + hf
                lo = hf * 64
                qr, qrb = qz[c % 2][hf], qz_b[c % 2][hf]
                for qc in range(NCH):
                    tiles = []
                    for kt in range(64):
                        v0 = kt * VT + gkv * 65
                        tiles.append((kres[:, pr, kt * 128:(kt + 1) * 128], vflat[:, v0:v0 + 128], [kres_b, vres_b], None, None))
                    acc = 6 + blk_i % 2
                    dst = scr["oT"][c * 128 + lo:c * 128 + lo + 64, qc * 512:(qc + 1) * 512]
                    stream.block(qr[:, qc * 512:(qc + 1) * 512], [qrb], tiles, 0.125, 128, acc,
                                 finish=lambda acc=acc, dst=dst, qc=qc: attn_finish(cx, ab, acc, 5, dst, [scr["oT_b"][qc]], consts))
                    blk_i += 1
        stream.flush()


def phase_out_proj(cx, xT_ap, xT_b, w_ap, KC, scr, bg=None):
    P = cx.P
    with contextlib.ExitStack() as st:
        w = cx.sb([128, KC, D], BF16, "wo", st)
        wb = [Buf(), Buf()]
        load_w(cx, st, w, wb, w_ap, KC * 128, D)
        oc = [cx.sb([128, KC, 512], BF16, "oc", st) for _ in range(2)]
        oc_b = [Buf(), Buf()]
        xr = [cx.sb([128, 512], F32, "oxr", st) for _ in range(3)]
        xr_b = [Buf() for _ in range(3)]
        it = 0
        for c in range(NCH):
            o_, ob_ = oc[c % 2], oc_b[c % 2]
            P.dma("sync", o_[:], scr["oT"].rearrange("(k p) t -> p k t", p=128)[:, :, c * 512:(c + 1) * 512],
                  reads=[scr["oT_b"][c]], writes=[ob_])
            for dc in range(8):
                ya, yab = cx.banks[it % 4], cx.bank_bufs[it % 4]
                xr_, xrb_ = xr[it % 3], xr_b[it % 3]
                xsl = xT_ap[dc * 128:(dc + 1) * 128, c * 512:(c + 1) * 512]
                P.dma("scalar", xr_[:], xsl, reads=[xT_b[c][dc]], writes=[xrb_])
                for k in range(KC):
                    P.op("tensor", lambda e, ya=ya, k=k, dc=dc, o_=o_: e.matmul(
                        ya[:, :], lhsT=w[:, k, dc * 128:(dc + 1) * 128], rhs=o_[:, k, :],
                        start=(k == 0), stop=(k == KC - 1)), reads=[ob_] + wb, writes=[yab])
                P.op("vector", lambda e, ya=ya, xr_=xr_: e.tensor_tensor(
                    out=xr_[:], in0=ya[:, :], in1=xr_[:], op=ALU.add), reads=[yab], writes=[xrb_])
                P.dma("gpsimd", xsl, xr_[:], reads=[xrb_], writes=[xT_b[c][dc]])
                it += 1
                if bg is not None:
                    next(bg, None)


DIL = (1, 4, 16)
DIL_DK0 = {1: list(range(-128, 513, 128)), 4: list(range(-256, 641, 128)), 16: list(range(-1024, 1409, 128))}
DIL_J0 = {d: max(v) for d, v in DIL_DK0.items()}
DIL_W = {d: max(v) - min(v) + 512 for d, v in DIL_DK0.items()}
EXT = T + 2048


def phase_even_proj(cx, xT_ap, xT_b, aps, consts, scr):
    P = cx.P
    ones_bf = consts["ones_bf"]
    NW = 3008
    with contextlib.ExitStack() as st:
        w = cx.sb([128, 8, NW], BF16, "ew", st)
        wb = [Buf(), Buf()]
        g = cx.sb([128, 8], F32, "eg", st)
        gb = Buf()
        P.dma("sync", g[:], aps["norm"], writes=[gb])
        load_w(cx, st, w, wb, aps["w"], D, NW, row_gain=(g, gb))
        wq = cx.sb([128, 3, 1024], BF16, "ewq", st)
        wqb = [Buf(), Buf()]
        load_w(cx, st, wq, wqb, aps["wuq"], 384, 1024)
        lg = cx.sb([128, 5], F32, "elg", st)
        lgb = Buf()
        P.dma("sync", lg[:], aps["lgain"], writes=[lgb])
        ctab = cx.sb([128, T], F32, "ectab", st)
        stab = cx.sb([128, T], F32, "estab", st)
        tab_b = Buf()
        P.dma("sync", ctab[64:96, :], aps["ctab"], writes=[tab_b])
        P.dma("scalar", stab[64:96, :], aps["stab"], writes=[tab_b])
        xs = cx.sb([128, 8, 512], F32, "ex", st)
        xsb = Buf()
        xns = [cx.sb([128, 8, 512], BF16, "exn", st) for _ in range(2)]
        xnbs = [[Buf(), Buf()], [Buf(), Buf()]]
        cur = {}
        sqs = [cx.sb([128, 512], BF16, "esq", st) for _ in range(2)]
        sqs_b = [Buf(), Buf()]
        rstd = cx.sb([128, 512], F32, "erstd", st)
        rstd_b = Buf()
        lat = cx.sb([128, 5, 512], F32, "elat", st)
        lat_b = [Buf() for _ in range(5)]
        lrs = [cx.sb([128, 512], F32, "elrs", st) for _ in range(2)]
        lrs_b = [Buf(), Buf()]
        latn = cx.sb([128, 5, 512], BF16, "elatn", st)
        latn_b = [Buf() for _ in range(5)]
        t1 = [cx.sb([128, 512], F32, "et1", st) for _ in range(2)]
        t1_b = [Buf(), Buf()]
        t2 = [cx.sb([128, 512], F32, "et2", st) for _ in range(2)]
        t2_b = [Buf(), Buf()]
        qo = [cx.sb([128, 512], BF16, "eqo", st) for _ in range(3)]
        qo_b = [Buf() for _ in range(3)]
        vst = [cx.sb([128, 4, 768], BF16, "evst", st) for _ in range(2)]
        vst_b = [Buf(), Buf()]
        bk = [0]
        zt = cx.sb([128, T], BF16, "ezero", st)
        ztb = Buf()
        P.op("gpsimd", lambda e: e.memset(zt[:], 0.0), writes=[ztb])
        P.dma("sync", scr["kr"][32:160, :], zt[:], reads=[ztb], writes=[scr["kr_b"]])
        P.dma("sync", scr["kr"][128:256, :], zt[:], reads=[ztb], writes=[scr["kr_b"]])

        def nb():
            bk[0] += 1
            i = bk[0] % 6
            return cx.banks[i], cx.bank_bufs[i]

        def proj(bank, bb, c0, ncols, prow0=0):
            xn_, xnb_ = cur["xn"], cur["xnb"]
            for k in range(8):
                P.op("tensor", lambda e, k=k: e.matmul(
                    bank[prow0:prow0 + ncols, :], lhsT=w[:, k, c0:c0 + ncols], rhs=xn_[:, k, :], start=(k == 0), stop=(k == 7)),
                    reads=xnb_ + wb, writes=[bb])

        def xload(c):
            P.dma("sync", xs[:], xT_view(xT_ap, c), reads=xT_b[c], writes=[xsb])

        def xnorm(c):
            rms_chunk(cx, xs, xsb, 8, xns[c % 2], xnbs[c % 2], sqs, sqs_b, ones_bf, 6, rstd, rstd_b, D, gain=(g, gb))

        qi = 0
        xload(0)
        xnorm(0)
        for c in range(NCH):
            cs = slice(c * 512, (c + 1) * 512)
            xn, xnb = xns[c % 2], xnbs[c % 2]
            cur["xn"], cur["xnb"] = xn, xnb
            if c + 1 < NCH:
                xload(c + 1)
            for k5 in range(5):
                A, Ab = nb()
                proj(A, Ab, k5 * 128, 128)
                P.op("scalar", lambda e, A=A, k5=k5: e.activation(out=lat[:, k5, :], in_=A[:, :], func=ACTF.Copy),
                     reads=[Ab], writes=[lat_b[k5]])
            for grp, (ks, Dn) in enumerate((((0, 1, 2), 384), ((3, 4), 256))):
                Sb_, Sbb_ = cx.banks[7], cx.bank_bufs[7]
                for ii, k5 in enumerate(ks):
                    sq, sqb = sqs[ii % 2], sqs_b[ii % 2]
                    P.op("scalar", lambda e, sq=sq, k5=k5: e.activation(out=sq[:], in_=lat[:, k5, :], func=ACTF.Square),
                         reads=[lat_b[k5]], writes=[sqb])
                    P.op("tensor", lambda e, sq=sq, ii=ii, n=len(ks): e.matmul(
                        Sb_[:, :], lhsT=ones_bf[0][:, :], rhs=sq[:], start=(ii == 0), stop=(ii == n - 1)),
                        reads=[sqb, ones_bf[1]], writes=[Sbb_])
                r_, rb_ = lrs[grp], lrs_b[grp]
                P.op("scalar", lambda e, r_=r_, Dn=Dn: e.activation(out=r_[:], in_=Sb_[:, :], func=ACTF.Sqrt, scale=1.0 / Dn, bias=EPS),
                     reads=[Sbb_], writes=[rb_])
                P.op("vector", lambda e, r_=r_: e.reciprocal(out=r_[:], in_=r_[:]), reads=[rb_], writes=[rb_])
                for k5 in ks:
                    P.op("vector", lambda e, k5=k5, r_=r_: e.scalar_tensor_tensor(
                        out=latn[:, k5, :], in0=lat[:, k5, :], scalar=lg[:, k5:k5 + 1], in1=r_[:], op0=ALU.mult, op1=ALU.mult),
                        reads=[lat_b[k5], rb_, lgb], writes=[latn_b[k5]])
            for k5 in (3, 4):
                P.dma("gpsimd", scr["lat"][(k5 - 3) * 128:(k5 - 2) * 128, cs], latn[:, k5, :], reads=[latn_b[k5]], writes=[scr["lat_b"]])

            def rope_to(dst_rows, A, Ab, B, Bb, q_, qb_, cs=cs, qi_=None):
                a_, ab_ = t1[qi_ % 2], t1_b[qi_ % 2]
                b_, bb_ = t2[qi_ % 2], t2_b[qi_ % 2]
                P.op("vector", lambda e: e.tensor_tensor(out=a_[64:96, :], in0=A[64:96, :], in1=ctab[64:96, cs], op=ALU.mult),
                     reads=[Ab, tab_b], writes=[ab_])
                P.op("vector", lambda e: e.tensor_tensor(out=b_[64:96, :], in0=B[64:96, :], in1=stab[64:96, cs], op=ALU.mult),
                     reads=[Bb, tab_b], writes=[bb_])
                P.op("gpsimd", lambda e: e.tensor_tensor(out=q_[64:96, :], in0=a_[64:96, :], in1=b_[64:96, :], op=ALU.add),
                     reads=[ab_, bb_], writes=[qb_])

            A, Ab = nb()
            B, Bb = nb()
            proj(A, Ab, 640, 32, prow0=64)
            proj(B, Bb, 672, 32, prow0=64)
            q_, qb_ = qo[qi % 3], qo_b[qi % 3]
            rope_to(None, A, Ab, B, Bb, q_, qb_, cs=cs, qi_=qi)
            P.dma("gpsimd", scr["kr"][0:32, cs], q_[64:96, :], reads=[qb_], writes=[scr["kr_b"]])
            qi += 1
            for hh in range(8):
                A, Ab = nb()
                B, Bb = nb()
                for k in range(3):
                    P.op("tensor", lambda e, A=A, k=k, hh=hh: e.matmul(
                        A[0:96, :], lhsT=wq[:, k, hh * 96:(hh + 1) * 96], rhs=latn[:, k, :], start=(k == 0), stop=(k == 2)),
                        reads=[latn_b[0], latn_b[1], latn_b[2]] + wqb, writes=[Ab])
                for k in range(3):
                    P.op("tensor", lambda e, B=B, k=k, hh=hh: e.matmul(
                        B[64:96, :], lhsT=wq[:, k, 768 + hh * 32:768 + (hh + 1) * 32], rhs=latn[:, k, :], start=(k == 0), stop=(k == 2)),
                        reads=[latn_b[0], latn_b[1], latn_b[2]] + wqb, writes=[Bb])
                q_, qb_ = qo[qi % 3], qo_b[qi % 3]
                P.op("scalar", lambda e, q_=q_, A=A: e.activation(out=q_[0:64, :], in_=A[0:64, :], func=ACTF.Copy),
                     reads=[Ab], writes=[qb_])
                rope_to(None, A, Ab, B, Bb, q_, qb_, cs=cs, qi_=qi)
                P.dma("sync", scr["mq"][hh * 96:(hh + 1) * 96, cs], q_[0:96, :], reads=[qb_], writes=[scr["mq_b"][hh]])
                qi += 1
            if c + 1 < NCH:
                xnorm(c + 1)
            for cc in range(12):
                A, Ab = nb()
                proj(A, Ab, 704 + cc * 128, 128)
                q_, qb_ = qo[qi % 3], qo_b[qi % 3]
                if cc % 2 == 0:
                    P.op("scalar", lambda e, q_=q_, A=A: e.activation(out=q_[:], in_=A[:, :], func=ACTF.Copy), reads=[Ab], writes=[qb_])
                else:
                    P.op("vector", lambda e, q_=q_, A=A: e.tensor_copy(out=q_[:], in_=A[:, :]), reads=[Ab], writes=[qb_])
                if cc < 6:
                    P.dma("sync", scr["dq"][cc * 128:(cc + 1) * 128, cs], q_[:], reads=[qb_], writes=[scr["dq_b"]])
                else:
                    gg_ = (cc - 6) // 2
                    rr_ = ((cc - 6) % 2) * 128
                    P.dma("sync", scr["dk"][gg_][rr_:rr_ + 128, cs], q_[:], reads=[qb_], writes=[scr["dk_b"]])
                qi += 1
            vs, vsb = vst[c % 2], vst_b[c % 2]
            for j in range(4):
                for part, (c0, nn) in enumerate(((0, 512), (512, 256))):
                    V, Vb = nb()
                    for k in range(8):
                        P.op("tensor", lambda e, V=V, k=k, j=j, c0=c0, nn=nn, xn=xn: e.matmul(
                            V[:, 0:nn], lhsT=xn[:, k, j * 128:(j + 1) * 128], rhs=w[:, k, 2240 + c0:2240 + c0 + nn],
                            start=(k == 0), stop=(k == 7)), reads=xnb + wb, writes=[Vb])
                    if part == 0:
                        P.op("scalar", lambda e, V=V, j=j, c0=c0, nn=nn, vs=vs: e.activation(out=vs[:, j, c0:c0 + nn], in_=V[:, 0:nn], func=ACTF.Copy),
                             reads=[Vb], writes=[vsb])
                    else:
                        P.op("vector", lambda e, V=V, j=j, c0=c0, nn=nn, vs=vs: e.tensor_copy(out=vs[:, j, c0:c0 + nn], in_=V[:, 0:nn]),
                             reads=[Vb], writes=[vsb])
            for gg_ in range(3):
                P.dma("gpsimd", scr["dv"][gg_][cs, :].rearrange("(j p) f -> p j f", p=128), vs[:, :, gg_ * 256:(gg_ + 1) * 256],
                      reads=[vsb], writes=[scr["dv_b"]])


def phase_mla_attn(cx, aps, consts, scr, after_setup=None):
    P = cx.P
    with contextlib.ExitStack() as st:
        wk = cx.sb([128, 2, 512], BF16, "mwk", st)
        wv = cx.sb([128, 2, 512], BF16, "mwv", st)
        wkb, wvb = [Buf(), Buf()], [Buf(), Buf()]
        load_w(cx, st, wk, wkb, aps["wuk"], 256, 512)
        load_w(cx, st, wv, wvb, aps["wuv"], 256, 512)
        ckv = cx.sb([128, 2, SEQ], BF16, "mckv", st)
        ckv_b = Buf()
        for r in range(2):
            for k in range(2):
                P.dma("sync" if k == 0 else "scalar", ckv[:, k, r * T:(r + 1) * T],
                      scr["lat_all"][r * 256 + k * 128:r * 256 + (k + 1) * 128, :], reads=[scr["lat_all_b"]], writes=[ckv_b])
        kh = [cx.sb([128, SEQ], BF16, "mkh", st) for _ in range(2)]
        kh_b = [Buf(), Buf()]
        for i in range(2):
            P.op("gpsimd", lambda e, i=i: e.memset(kh[i][96:128, :], 0.0), writes=[kh_b[i]])
            for r in range(2):
                P.dma("sync", kh[i][64:96, r * T:(r + 1) * T], scr["kr_all"][r * 256:r * 256 + 32, :],
                      reads=[scr["kr_all_b"]], writes=[kh_b[i]])
        vflat = cx.sb([128, 64 * 520 + 128], BF16, "mvres", st)
        vres = vflat[:, 0:64 * 520].rearrange("p (t f) -> p t f", f=520)
        vres_b = Buf()
        P.op("gpsimd", lambda e: e.memset(vflat[:], 1.0), writes=[vres_b])
        for kt in range(64):
            V, Vb = cx.banks[6 + kt % 2], cx.bank_bufs[6 + kt % 2]
            for k in range(2):
                P.op("tensor", lambda e, V=V, k=k, kt=kt: e.matmul(
                    V[:, :], lhsT=ckv[:, k, kt * 128:(kt + 1) * 128], rhs=wv[:, k, :], start=(k == 0), stop=(k == 1)),
                    reads=[ckv_b] + wvb, writes=[Vb])
            dst = vres[:, kt, :].rearrange("p (h d) -> p h d", d=65)[:, :, 0:64]
            src = V[:, :].rearrange("p (h d) -> p h d", d=64)
            if kt % 2 == 0:
                P.op("scalar", lambda e, dst=dst, src=src: e.activation(out=dst, in_=src, func=ACTF.Copy), reads=[Vb], writes=[vres_b])
            else:
                P.op("vector", lambda e, dst=dst, src=src: e.tensor_copy(out=dst, in_=src), reads=[Vb], writes=[vres_b])
        qres = [cx.sb([128, T], BF16, "mq", st) for _ in range(2)]
        qres_b = [Buf(), Buf()]
        for i in range(2):
            P.op("gpsimd", lambda e, i=i: e.memset(qres[i][96:128, :], 0.0), writes=[qres_b[i]])
        if after_setup is not None:
            after_setup()
        ab = attn_bufs(cx, st)
        stream = AttnStream(cx, ab)
        blk_i = 0
        for hh in range(8):
            qr, qrb = qres[hh % 2], qres_b[hh % 2]
            P.dma("sync", qr[0:96, :], scr["mq"][hh * 96:(hh + 1) * 96, :], reads=[scr["mq_b"][hh]], writes=[qrb])
            k_, kb_ = kh[hh % 2], kh_b[hh % 2]
            for kc in range(16):
                bk_ = take_bank(cx, ab)
                V, Vb = cx.banks[bk_], cx.bank_bufs[bk_]
                for k in range(2):
                    P.op("tensor", lambda e, V=V, k=k, kc=kc, hh=hh: e.matmul(
                        V[0:64, :], lhsT=wk[:, k, hh * 64:(hh + 1) * 64], rhs=ckv[:, k, kc * 512:(kc + 1) * 512],
                        start=(k == 0), stop=(k == 1)), reads=[ckv_b] + wkb, writes=[Vb])
                P.op("vector", lambda e, V=V, k_=k_, kc=kc: e.tensor_copy(out=k_[0:64, kc * 512:(kc + 1) * 512], in_=V[0:64, :]),
                     reads=[Vb], writes=[kb_])
            for qc in range(NCH):
                tiles = [(k_[:, kt * 128:(kt + 1) * 128], vflat[:, kt * 520 + hh * 65:kt * 520 + hh * 65 + 128], [kb_, vres_b], None, None)
                         for kt in range(64)]
                acc = 6 + blk_i % 2
                dst = scr["oT"][hh * 64:(hh + 1) * 64, qc * 512:(qc + 1) * 512]
                stream.block(qr[:, qc * 512:(qc + 1) * 512], [qrb], tiles, 96 ** -0.5, 128, acc,
                             finish=lambda acc=acc, dst=dst, qc=qc: attn_finish(cx, ab, acc, 5, dst, [scr["oT_b"][qc]], consts))
                blk_i += 1
        stream.flush()


def phase_dil_attn(cx, aps, consts, scr):
    P = cx.P
    with contextlib.ExitStack() as st:
        hv = cx.sb([128, 2], F32, "dhv", st)
        hvb = Buf()
        P.dma("sync", hv[:], aps["halo_valid"], writes=[hvb])
        ab = attn_bufs(cx, st)
        sets = []
        zb = Buf()
        for si in range(2):
            kx = [cx.sb([128, EXT], BF16, "dkx", st) for _ in range(3)]
            qx = [cx.sb([128, T], BF16, "dqx", st) for _ in range(3)]
            vxf = [cx.sb([128, 48 * 65 + 128], BF16, "dvx", st) for _ in range(3)]
            vx = [v[:, 0:48 * 65].rearrange("p (t f) -> p t f", f=65) for v in vxf]
            mk = [cx.sb([128, DIL_W[d]], BF16, "dmk", st) for d in DIL]
            bufs = {k_: [Buf() for _ in range(3)] for k_ in ("kx", "qx", "vx", "mk")}
            for gi in range(3):
                P.op("gpsimd", lambda e, kx=kx, gi=gi: e.memset(kx[gi][64:128, :], 0.0), writes=[bufs["kx"][gi]])
                P.op("gpsimd", lambda e, qx=qx, gi=gi: e.memset(qx[gi][64:128, :], 0.0), writes=[bufs["qx"][gi]])
            sets.append((kx, qx, vxf, vx, mk, bufs))

        def load_slot(s_):
            kx, qx, vxf, vx, mk, bufs = sets[s_ % 2]
            for gi, d in enumerate(DIL):
                hd = gi * 4 + s_
                r0 = hd * 64
                rs = s_ * 64
                q = "sync" if gi % 2 == 0 else "gpsimd"
                kb, qb, vb, mb = bufs["kx"][gi], bufs["qx"][gi], bufs["vx"][gi], bufs["mk"][gi]
                P.dma(q, mk[gi][:], aps["dmask"][hd], writes=[mb])
                P.dma(q, qx[gi][0:64, :], scr["dq"][r0:r0 + 64, :], reads=[scr["dq_b"]], writes=[qb])
                P.dma(q, kx[gi][0:64, 1024:1024 + T], scr["dk"][gi][rs:rs + 64, :], reads=[scr["dk_b"]], writes=[kb])
                P.dma(q, kx[gi][0:64, 0:1024], scr["dk_all"][gi][rs:rs + 64, T - 1024:T], reads=[scr["dk_all_b"]], writes=[kb])
                P.dma(q, kx[gi][0:64, 1024 + T:EXT], scr["dk_all"][gi][256 + rs:256 + rs + 64, 0:1024], reads=[scr["dk_all_b"]], writes=[kb])
                P.op("vector", lambda e, vxf=vxf, gi=gi: e.memset(vxf[gi][:], 1.0), writes=[vb])
                for hf2 in range(2):
                    P.dma(q, vx[gi][:, 8 + hf2 * 16:24 + hf2 * 16, 0:64],
                          scr["dv"][gi][hf2 * 2048:(hf2 + 1) * 2048, rs:rs + 64].rearrange("(t p) d -> p t d", p=128),
                          reads=[scr["dv_b"]], writes=[vb])
                P.dma(q, vx[gi][:, 0:8, 0:64], scr["dv_all"][gi][T - 1024:T, rs:rs + 64].rearrange("(t p) d -> p t d", p=128),
                      reads=[scr["dv_all_b"]], writes=[vb])
                P.dma(q, vx[gi][:, 40:48, 0:64], scr["dv_all"][gi][T:T + 1024, rs:rs + 64].rearrange("(t p) d -> p t d", p=128),
                      reads=[scr["dv_all_b"]], writes=[vb])

        def fix_halo(s_):
            kx, qx, vxf, vx, mk, bufs = sets[s_ % 2]
            for gi in range(3):
                vb = bufs["vx"][gi]
                P.op("vector", lambda e, vx=vx, gi=gi: e.tensor_scalar(out=vx[gi][:, 0:8, :], in0=vx[gi][:, 0:8, :], scalar1=hv[:, 0:1],
                                                                     scalar2=None, op0=ALU.mult), reads=[hvb], writes=[vb])
                P.op("vector", lambda e, vx=vx, gi=gi: e.tensor_scalar(out=vx[gi][:, 40:48, :], in0=vx[gi][:, 40:48, :], scalar1=hv[:, 1:2],
                                                                     scalar2=None, op0=ALU.mult), reads=[hvb], writes=[vb])

        blk_i = 0
        stream = AttnStream(cx, ab)
        load_slot(0)
        for s_ in range(4):
            fix_halo(s_)
            if s_ + 1 < 4:
                load_slot(s_ + 1)
            kx, qx, vxf, vx, mk, bufs = sets[s_ % 2]
            ab["mask_b"] = bufs["mk"]
            for qc in range(NCH):
                q0 = qc * 512
                tiles = []
                for gi, d in enumerate(DIL):
                    for dk0 in DIL_DK0[d]:
                        e0 = q0 + dk0 + 1024
                        j0 = DIL_J0[d] - dk0
                        v0 = (e0 // 128) * 65
                        tiles.append((kx[gi][:, e0:e0 + 128], vxf[gi][:, v0:v0 + 128], [bufs["kx"][gi], bufs["vx"][gi], bufs["qx"][gi]],
                                      mk[gi][:, j0:j0 + 512], qx[gi][:, q0:q0 + 512]))
                acc = 6 + blk_i % 2
                dst = scr["oT"][512 + s_ * 64:512 + (s_ + 1) * 64, q0:q0 + 512]
                stream.block(None, [], tiles, 0.125, 128, acc,
                             finish=lambda acc=acc, dst=dst, qc=qc: attn_finish(cx, ab, acc, 5, dst, [scr["oT_b"][qc]], consts))
                blk_i += 1
        stream.flush()


def run_dil_block(cx, st_bufs, q_list, tiles, scale, acc_bank):
    P = cx.P
    pts, pts_b = st_bufs["pt"], st_bufs["pt_b"]
    sbanks = st_bufs["sbanks"]
    ab, abb = cx.banks[acc_bank], cx.bank_bufs[acc_bank]
    n = len(tiles)
    LA = 2
    cnt = st_bufs["cnt"]
    for i in range(n + LA):
        if i < n:
            lhsT, v_ap, rb, mask, c0, c1, gi = tiles[i]
            u = cnt[0] + i
            sbk = sbanks[u % len(sbanks)]
            sb_, sbb_ = cx.banks[sbk], cx.bank_bufs[sbk]
            pt, ptb = pts[u % len(pts)], pts_b[u % len(pts)]
            P.op("tensor", lambda e, sb_=sb_, lhsT=lhsT, gi=gi: e.matmul(
                sb_[:, :], lhsT=lhsT, rhs=q_list[gi], start=True, stop=True), reads=list(rb), writes=[sbb_])
            P.op("scalar", lambda e, sb_=sb_, pt=pt: e.activation(out=pt[:], in_=sb_[:, :], func=ACTF.Exp, scale=scale),
                 reads=[sbb_], writes=[ptb])
            P.op("vector", lambda e, pt=pt, mask=mask: e.tensor_tensor(out=pt[:], in0=pt[:], in1=mask, op=ALU.mult),
                 reads=[ptb] + st_bufs["mask_b"], writes=[ptb])
        j = i - LA
        if j >= 0:
            lhsT, v_ap, rb, mask, c0, c1, gi = tiles[j]
            u = cnt[0] + j
            pt, ptb = pts[u % len(pts)], pts_b[u % len(pts)]
            P.op("tensor", lambda e, v_ap=v_ap, pt=pt, j=j: e.matmul(
                ab[0:65, :], lhsT=v_ap, rhs=pt[:], start=(j == 0), stop=(j == n - 1)), reads=list(rb) + [ptb], writes=[abb])
    cnt[0] += n


def build_program(layers=(), dbg=()):
    nc = bass.Bass("TRN2", target_bir_lowering=False)
    cx = Ctx(nc)
    P = cx.P

    def dram(name, shape, dtype, kind="Internal"):
        if name in dbg:
            kind = "ExternalOutput"
        return nc.dram_tensor(name, list(shape), dtype, kind=kind).ap()

    x_ap = dram("x", [T, D], F32, "ExternalInput")
    out_ap = dram("out", [T, D], F32, "ExternalOutput")
    ident_ap = dram("ident", [128, 128], F32, "ExternalInput")
    fin_g_ap = dram("final_norm", [128, D], F32, "ExternalInput")
    xT_ap = dram("xT", [D, T], F32)

    ident = cx.sb([128, 128], F32, "ident")
    ident_b = Buf()
    P.dma("sync", ident[:], ident_ap[:, :], writes=[ident_b])

    xT_b = [[Buf() for _ in range(8)] for _ in range(NCH)]
    consts = {}
    ones_bf = cx.sb([128, 128], BF16, "ones_bf")
    ones_bf_b = Buf()
    P.op("vector", lambda e: e.memset(ones_bf[:], 1.0), writes=[ones_bf_b])
    consts["ones_bf"] = (ones_bf, ones_bf_b)
    phase_transpose_in(cx, x_ap, xT_ap, xT_b, (ident, ident_b))
    P.barrier()
    blk_bf = cx.sb([128, 128], BF16, "blk_bf")
    blk_b = Buf()
    P.op("vector", lambda e: e.memset(blk_bf[:], 0.0), writes=[blk_b])
    P.op("vector", lambda e: e.memset(blk_bf[0:64, 0:64], 1.0), writes=[blk_b])
    P.op("vector", lambda e: e.memset(blk_bf[64:128, 64:128], 1.0), writes=[blk_b])
    consts["blk_bf"] = (blk_bf, blk_b)
    PAIRS = [[0, 1], [2, 3], [4, 5], [6, 7]]
    for l in layers:
        pending_op = None
        if "even" in l:
            li = l["even"]
            aps = {"norm": dram("ev_norm%d" % li, [128, 8], F32, "ExternalInput"),
                   "w": dram("ev_w%d" % li, [D, 3008], F32, "ExternalInput"),
                   "wuq": dram("ev_wuq%d" % li, [384, 1024], F32, "ExternalInput"),
                   "wuk": dram("ev_wuk%d" % li, [256, 512], F32, "ExternalInput"),
                   "wuv": dram("ev_wuv%d" % li, [256, 512], F32, "ExternalInput"),
                   "lgain": dram("ev_lgain%d" % li, [128, 5], F32, "ExternalInput")}
            for nm, shp, dt_ in (("ctab", [32, T], F32), ("stab", [32, T], F32), ("halo_valid", [128, 2], F32)):
                if "ev_" + nm not in consts:
                    consts["ev_" + nm] = dram("ev_" + nm, shp, dt_, "ExternalInput")
                aps[nm] = consts["ev_" + nm]
            if "ev_dmask" not in consts:
                consts["ev_dmask"] = [dram("ev_dmask%d" % hd, [128, DIL_W[DIL[hd // 4]]], BF16, "ExternalInput") for hd in range(12)]
            aps["dmask"] = consts["ev_dmask"]
            wo_ap = dram("ev_wo%d" % li, [768, D], F32, "ExternalInput")
            scr = {"lat": dram("lat%d" % li, [256, T], BF16), "lat_b": Buf(),
                   "lat_all": dram("lat_all%d" % li, [512, T], BF16), "lat_all_b": Buf(),
                   "kr": dram("kr%d" % li, [256, T], BF16), "kr_b": Buf(),
                   "kr_all": dram("kr_all%d" % li, [512, T], BF16), "kr_all_b": Buf(),
                   "mq": dram("mq%d" % li, [768, T], BF16), "mq_b": [Buf() for _ in range(8)],
                   "dq": dram("dq%d" % li, [768, T], BF16), "dq_b": Buf(),
                   "dk": [dram("dk%d_%d" % (li, g_), [256, T], BF16) for g_ in range(3)], "dk_b": Buf(),
                   "dv": [dram("dv%d_%d" % (li, g_), [T, 256], BF16) for g_ in range(3)], "dv_b": Buf(),
                   "dk_all": [dram("dk_all%d_%d" % (li, g_), [512, T], BF16) for g_ in range(3)], "dk_all_b": Buf(),
                   "dv_all": [dram("dv_all%d_%d" % (li, g_), [2 * T, 256], BF16) for g_ in range(3)], "dv_all_b": Buf(),
                   "oT": dram("eoT%d" % li, [768, T], BF16), "oT_b": [Buf() for _ in range(NCH)]}
            upto = l.get("upto", 5)
            phase_even_proj(cx, xT_ap, xT_b, aps, consts, scr)
            P.collective("AllGather", [scr["lat"][:, :]], [scr["lat_all"][:, :]], PAIRS, reads=[scr["lat_b"]], writes=[scr["lat_all_b"]])
            P.collective("AllGather", [scr["kr"][:, :]], [scr["kr_all"][:, :]], PAIRS, reads=[scr["kr_b"]], writes=[scr["kr_all_b"]])

            def dil_cc(scr=scr):
                for g_ in range(3):
                    P.collective("AllGather", [scr["dk"][g_][:, :]], [scr["dk_all"][g_][:, :]], PAIRS, reads=[scr["dk_b"]], writes=[scr["dk_all_b"]])
                    P.collective("AllGather", [scr["dv"][g_][:, :]], [scr["dv_all"][g_][:, :]], PAIRS, reads=[scr["dv_b"]], writes=[scr["dv_all_b"]])
            P.barrier()
            if upto >= 3:
                phase_mla_attn(cx, aps, consts, scr, after_setup=dil_cc)
                P.barrier()
            else:
                dil_cc()
                P.barrier()
            if upto >= 4:
                phase_dil_attn(cx, aps, consts, scr)
                P.barrier()
            if upto >= 5:
                if "ffn" in l:
                    pending_op = (wo_ap, 6, scr)
                else:
                    phase_out_proj(cx, xT_ap, xT_b, wo_ap, 6, scr)
                    P.barrier()
        if "gqa" in l:
            li = l["gqa"]
            aps = {"norm": dram("gqa_norm%d" % li, [128, 8], F32, "ExternalInput"),
                   "w": dram("gqa_w%d" % li, [D, 2816], F32, "ExternalInput"),
                   "hgain": dram("gqa_hgain%d" % li, [128, 4], F32, "ExternalInput"),
                   "ctab": dram("gqa_ctab", [128, T], F32, "ExternalInput") if "gqa_ctab" not in consts else consts["gqa_ctab"],
                   "stab": dram("gqa_stab", [128, T], F32, "ExternalInput") if "gqa_stab" not in consts else consts["gqa_stab"]}
            consts["gqa_ctab"], consts["gqa_stab"] = aps["ctab"], aps["stab"]
            wo_ap = dram("gqa_wo%d" % li, [D, D], F32, "ExternalInput")
            scr = {"qT": dram("qT%d" % li, [D, T], BF16), "qT_b": [Buf() for _ in range(8)],
                   "kT": dram("kT%d" % li, [256, T], BF16), "kT_b": Buf(),
                   "v": dram("v%d" % li, [T, 256], BF16), "v_b": Buf(),
                   "kT_all": dram("kT_all%d" % li, [512, T], BF16), "kT_all_b": Buf(),
                   "v_all": dram("v_all%d" % li, [2 * T, 256], BF16), "v_all_b": Buf(),
                   "oT": dram("oT%d" % li, [D, T], BF16), "oT_b": [Buf() for _ in range(NCH)]}
            upto = l.get("upto", 4)
            phase_gqa_proj(cx, xT_ap, xT_b, aps, consts, scr)
            if upto >= 2:
                P.collective("AllGather", [scr["kT"][:, :]], [scr["kT_all"][:, :]], PAIRS,
                             reads=[scr["kT_b"]], writes=[scr["kT_all_b"]])
                P.collective("AllGather", [scr["v"][:, :]], [scr["v_all"][:, :]], PAIRS,
                             reads=[scr["v_b"]], writes=[scr["v_all_b"]])
            P.barrier()
            if upto >= 3:
                phase_gqa_attn(cx, consts, scr)
                P.barrier()
            if upto >= 4:
                if "ffn" in l:
                    pending_op = (wo_ap, 8, scr)
                else:
                    phase_out_proj(cx, xT_ap, xT_b, wo_ap, 8, scr)
                    P.barrier()
        if "ffn" in l:
            li = l["ffn"]
            w_in_ap = dram("ffn_w_in%d" % li, [D, 2 * FFN_H], F32, "ExternalInput")
            w_out_ap = dram("ffn_w_out%d" % li, [FFN_H, D], F32, "ExternalInput")
            g_ap = dram("ffn_norm%d" % li, [128, 8], F32, "ExternalInput")
            if pending_op is not None:
                phase_outproj_ffn(cx, xT_ap, xT_b, pending_op[0], pending_op[1], pending_op[2], w_in_ap, w_out_ap, g_ap, consts)
            else:
                phase_ffn(cx, xT_ap, xT_b, w_in_ap, w_out_ap, g_ap, consts)
            P.barrier()
    phase_final(cx, xT_ap, xT_b, out_ap, fin_g_ap, (ident, ident_b))

    for q in ("sync", "gpsimd", "scalar"):
        ring = P.dma_ring[q]
        toks = [(id(s), v) for s, v in zip(ring["sems"], ring["vals"]) if v > 0]
        P._wait_dma("gpsimd", toks)
    P.emit()
    cx.stack.close()
    return nc


def gain_cols(g):
    g = np.asarray(g, dtype=np.float32)
    return np.ascontiguousarray(g.reshape(-1, 128).T)


def rope_tabs(pos, dim):
    freqs = 10000.0 ** (-np.arange(0, dim, 2, dtype=np.float32) / dim)
    ang = pos.astype(np.float32)[:, None] * freqs[None, :].astype(np.float32)
    return np.cos(ang).astype(np.float32), np.sin(ang).astype(np.float32)


GQ_ORDER = [hh for pair in zip(GQ_L, GQ_U) for hh in pair]
SWAP64 = np.concatenate([np.arange(16, 32), np.arange(0, 16), np.arange(48, 64), np.arange(32, 48)])


def gqa_host(inputs, li, h):
    wq = inputs["gqa_w_q"][li].astype(np.float32).reshape(D, 16, 64)
    wkv = inputs["gqa_w_kv"][li].astype(np.float32).reshape(D, 2, 4, 64)
    qa = wq[:, GQ_ORDER, :]
    qb = qa[:, :, SWAP64]
    ka = wkv[:, 0]
    kb = ka[:, :, SWAP64]
    v = wkv[:, 1]
    w = np.concatenate([qa.reshape(D, -1), qb.reshape(D, -1), ka.reshape(D, -1), kb.reshape(D, -1), v.reshape(D, -1)], axis=1)
    gq = inputs["gqa_q_norm"][li].astype(np.float32)
    gk = inputs["gqa_k_norm"][li].astype(np.float32)
    hg = np.stack([np.tile(gq, 2), np.tile(gq[SWAP64], 2), np.tile(gk, 2), np.tile(gk[SWAP64], 2)], axis=1)
    wo = inputs["gqa_w_o"][li].astype(np.float32).reshape(16, 64, D)[GQ_ORDER].reshape(D, D)
    return np.ascontiguousarray(w), np.ascontiguousarray(hg), np.ascontiguousarray(wo)


def gqa_tabs(h):
    t = np.arange(h * T, (h + 1) * T)
    cr, sr = rope_tabs(t // 64, 32)
    cc, sc = rope_tabs(t % 64, 32)
    C = np.concatenate([cr, cr, cc, cc], axis=1).T
    S = np.concatenate([-sr, sr, -sc, sc], axis=1).T
    return np.ascontiguousarray(np.tile(C, (2, 1))), np.ascontiguousarray(np.tile(S, (2, 1)))


SWAP32 = np.concatenate([np.arange(16, 32), np.arange(0, 16)])


def even_host(inputs, li):
    w = inputs["w_in_ab"][li].astype(np.float32)
    kr = w[:, 640:672]
    wcat = np.concatenate([w[:, 0:672], kr[:, SWAP32], w[:, 672:]], axis=1)
    uq = inputs["mla_w_uq"][li].astype(np.float32)
    wuq = np.concatenate([uq.reshape(384, 768), uq[:, :, 64 + SWAP32].reshape(384, 256)], axis=1)
    ukv = inputs["mla_w_ukv"][li].astype(np.float32)
    wuk = ukv[:, :, 0:64].reshape(256, 512)
    wuv = ukv[:, :, 64:128].reshape(256, 512)
    lg = np.concatenate([gain_cols(inputs["mla_q_norm"][li]), gain_cols(inputs["mla_kv_norm"][li])], axis=1)
    return [np.ascontiguousarray(a) for a in (wcat, wuq, wuk, wuv, lg)]


def even_consts(h):
    t = np.arange(h * T, (h + 1) * T)
    c, s_ = rope_tabs(t, 32)
    C = np.concatenate([c, c], axis=1).T
    S = np.concatenate([-s_, s_], axis=1).T
    hv = np.zeros((128, 2), np.float32)
    hv[:, 0] = 1.0 if h == 1 else 0.0
    hv[:, 1] = 1.0 if h == 0 else 0.0
    slopes = np.exp2(-8.0 * np.arange(1, 13, dtype=np.float32) / 12).astype(np.float32)
    masks = []
    for hd in range(12):
        d = DIL[hd // 4]
        p = np.arange(128)[:, None]
        j = np.arange(DIL_W[d])[None, :]
        delta = p - j + DIL_J0[d]
        ok = (delta % d == 0) & (np.abs(delta) <= 64 * d)
        mval = np.where(ok, np.exp(-slopes[hd] * np.abs(delta).astype(np.float32)), 0.0).astype(np.float32)
        masks.append(mval.astype(ml_dtypes.bfloat16))
    return np.ascontiguousarray(C), np.ascontiguousarray(S), hv, masks


def make_inputs_for_core(c, inputs, layers=()):
    b, h = c // 2, c % 2
    m = {}
    for l in layers:
        if "even" in l:
            li = l["even"]
            wcat, wuq, wuk, wuv, lg = even_host(inputs, li)
            m["ev_w%d" % li], m["ev_wuq%d" % li], m["ev_wuk%d" % li], m["ev_wuv%d" % li], m["ev_lgain%d" % li] = wcat, wuq, wuk, wuv, lg
            m["ev_norm%d" % li] = gain_cols(inputs["mix_norm_ab"][li])
            m["ev_wo%d" % li] = np.ascontiguousarray(inputs["w_out_ab"][li], dtype=np.float32)
            C, S, hv, masks = even_consts(h)
            m["ev_ctab"], m["ev_stab"], m["ev_halo_valid"] = C, S, hv
            for hd in range(12):
                m["ev_dmask%d" % hd] = masks[hd]
        if "gqa" in l:
            li = l["gqa"]
            w, hg, wo = gqa_host(inputs, li, h)
            m["gqa_w%d" % li] = w
            m["gqa_hgain%d" % li] = hg
            m["gqa_wo%d" % li] = wo
            m["gqa_norm%d" % li] = gain_cols(inputs["mix_norm_c"][li])
            m["gqa_ctab"], m["gqa_stab"] = gqa_tabs(h)
        if "ffn" in l:
            li = l["ffn"]
            m["ffn_w_in%d" % li] = np.ascontiguousarray(inputs["ffn_w_in"][li], dtype=np.float32)
            m["ffn_w_out%d" % li] = np.ascontiguousarray(inputs["ffn_w_out"][li], dtype=np.float32)
            m["ffn_norm%d" % li] = gain_cols(inputs["ffn_norm"][li])
    m["x"] = np.ascontiguousarray(inputs["x"][b, h * T:(h + 1) * T, :], dtype=np.float32)
    m["ident"] = np.eye(128, dtype=np.float32)
    m["final_norm"] = np.ascontiguousarray(np.broadcast_to(inputs["final_norm"].astype(np.float32).reshape(1, D), (128, D)))
    return m


LAYERS = ({"even": 0, "ffn": 0}, {"gqa": 0, "ffn": 1}, {"even": 1, "ffn": 2}, {"gqa": 1, "ffn": 3})


def kernel(**inputs):
    inputs = {k: np.asarray(v) for k, v in inputs.items()}
    nc = build_program(layers=LAYERS)
    in_maps = [make_inputs_for_core(c, inputs, LAYERS) for c in range(N_CORES)]
    res = run_bass_kernel_spmd(nc, in_maps, core_ids=list(range(N_CORES)))
    out = np.empty((BATCH, SEQ, D), dtype=np.float32)
    for c in range(N_CORES):
        b, h = c // 2, c % 2
        out[b, h * T:(h + 1) * T, :] = np.asarray(res.results[c]["out"])
    return out
```

```python
import contextlib
import numpy as np
import ml_dtypes
import concourse.bass as bass
import concourse.mybir as mybir
from concourse.bass_utils import run_bass_kernel_spmd

F32 = mybir.dt.float32
BF16 = mybir.dt.bfloat16
ALU = mybir.AluOpType
ACTF = mybir.ActivationFunctionType

D = 1024
BATCH = 4
SEQ = 8192
T = 4096
NCH = T // 512
EPS = 1e-6
FFN_H = 2816
N_CORES = 8
N_DUMMY = 0
SG = 3
VW = 272


class Buf:
    __slots__ = ("w", "r", "name")

    def __init__(self, name=""):
        self.w = None
        self.r = {}
        self.name = name


class Prog:
    ENGS = ("sync", "scalar", "vector", "tensor", "gpsimd")

    def __init__(self, nc, n_dma_sems=24):
        self.nc = nc
        self.lists = {e: [] for e in self.ENGS}
        self.sem = {e: nc.alloc_semaphore(name="c_" + e) for e in self.ENGS}
        self.cnt = {e: 0 for e in self.ENGS}
        self.waited = {e: {} for e in self.ENGS}
        self.semobj = {}
        for e in self.ENGS:
            self.semobj[id(self.sem[e])] = self.sem[e]
        self.dma_ring = {}
        for q in ("sync", "gpsimd", "scalar"):
            sems = [nc.alloc_semaphore(name="d_%s_%d" % (q, i)) for i in range(n_dma_sems)]
            for s in sems:
                self.semobj[id(s)] = s
            self.dma_ring[q] = {"sems": sems, "vals": [0] * n_dma_sems, "i": 0}
        self.cc_sem = nc.alloc_semaphore(name="c_collective")
        self.semobj[id(self.cc_sem)] = self.cc_sem
        self.cc_cnt = 0
        self.n_ops = 0

    def _wait(self, eng, toks):
        need = {}
        for (sid, val) in toks:
            if need.get(sid, 0) < val:
                need[sid] = val
        for sid, val in need.items():
            if sid == id(self.sem[eng]) and eng == "tensor":
                continue
            if self.waited[eng].get(sid, 0) >= val:
                continue
            self.waited[eng][sid] = val
            self.lists[eng].append(("wait", self.semobj[sid], val))

    def _deps(self, reads, writes):
        toks = []
        for b in reads:
            if b.w is not None:
                toks.append(b.w)
        for b in writes:
            if b.w is not None:
                toks.append(b.w)
            toks.extend(b.r.items())
        return toks

    def _commit(self, tok, reads, writes):
        for b in reads:
            if b.r.get(tok[0], 0) < tok[1]:
                b.r[tok[0]] = tok[1]
        for b in writes:
            b.w = tok
            b.r = {}

    def op(self, eng, emit, reads=(), writes=()):
        self._wait(eng, self._deps(reads, writes))
        self.cnt[eng] += 1
        tok = (id(self.sem[eng]), self.cnt[eng])
        self.lists[eng].append(("op", emit, self.sem[eng], 1))
        self._commit(tok, reads, writes)
        self.n_ops += 1
        return tok

    def dma(self, q, out, in_, reads=(), writes=(), **kw):
        ring = self.dma_ring[q]
        k = ring["i"] % len(ring["sems"])
        ring["i"] += 1
        s = ring["sems"][k]
        toks = self._deps(reads, writes)
        if ring["vals"][k] > 0:
            toks.append((id(s), ring["vals"][k]))
        self._wait_dma(q, toks)
        ring["vals"][k] += 16
        tok = (id(s), ring["vals"][k])
        self.lists[q].append(("op", lambda e: e.dma_start(out=out, in_=in_, **kw), s, 16))
        self._commit(tok, reads, writes)
        self.n_ops += 1
        return tok

    def _wait_dma(self, q, toks):
        need = {}
        for (sid, val) in toks:
            if need.get(sid, 0) < val:
                need[sid] = val
        for sid, val in need.items():
            if self.waited[q].get(sid, 0) >= val:
                continue
            self.waited[q][sid] = val
            self.lists[q].append(("wait", self.semobj[sid], val))

    def collective(self, kind, ins, outs, groups, reads=(), writes=()):
        q = "gpsimd"
        self._wait_dma(q, self._deps(reads, writes))
        self.cc_cnt += 1
        tok = (id(self.cc_sem), self.cc_cnt)
        self.lists[q].append(("op", lambda e: e.collective_compute(
            kind, ALU.bypass, replica_groups=groups, ins=ins, outs=outs), self.cc_sem, 1))
        self._commit(tok, reads, writes)
        return tok

    def barrier(self):
        toks = [(id(self.sem[e]), self.cnt[e]) for e in self.ENGS if self.cnt[e] > 0]
        if self.cc_cnt > 0:
            toks.append((id(self.cc_sem), self.cc_cnt))
        for q, ring in self.dma_ring.items():
            toks += [(id(s), v) for s, v in zip(ring["sems"], ring["vals"]) if v > 0]
        for e in self.ENGS:
            self._wait_dma(e, toks)

    def wait_all(self, eng, bufs):
        toks = []
        for b in bufs:
            if b.w is not None:
                toks.append(b.w)
            toks.extend(b.r.items())
        self._wait_dma(eng, toks)

    def emit(self):
        nc = self.nc
        with nc.Block() as block:
            def runner(name):
                def body(e):
                    for it in self.lists[name]:
                        if it[0] == "wait":
                            e.wait_ge(it[1], it[2])
                        else:
                            ins = it[1](e)
                            ins.then_inc(it[2], it[3])
                return body
            block.sync(runner("sync"))
            block.scalar(runner("scalar"))
            block.vector(runner("vector"))
            block.tensor(runner("tensor"))
            block.gpsimd(runner("gpsimd"))


class Ctx:
    def __init__(self, nc):
        self.nc = nc
        self.P = Prog(nc)
        self.stack = contextlib.ExitStack()
        self.banks = []
        self.bank_bufs = []
        self.trip = []
        for i in range(2):
            t = self.stack.enter_context(nc.psum_tensor("trip%d" % i, [128, SG * 512], F32))
            self.trip.append(t)
            for hf in range(SG):
                self.banks.append(t[:, hf * 512:(hf + 1) * 512])
                self.bank_bufs.append(Buf("bank%d" % (SG * i + hf)))
        for i in range(2):
            t = self.stack.enter_context(nc.psum_tensor("accb%d" % i, [128, 512], F32))
            self.banks.append(t[:, :])
            self.bank_bufs.append(Buf("bank%d" % (2 * SG + i)))
        self.uid = 0

    def sb(self, shape, dtype, name=None, stack=None):
        self.uid += 1
        nm = "%s_%d" % (name or "t", self.uid)
        return (stack or self.stack).enter_context(self.nc.sbuf_tensor(nm, list(shape), dtype))


def phase_transpose_in(cx, x_ap, xT_ap, xT_b, ident_f32):
    P, nc = cx.P, cx.nc
    with contextlib.ExitStack() as st:
        NB = 2
        xin = [cx.sb([128, D], F32, "xin", st) for _ in range(NB)]
        xin_b = [Buf() for _ in range(NB)]
        stg = [cx.sb([128, 8, 512], F32, "xTstg", st) for _ in range(2)]
        stg_b = [Buf() for _ in range(2)]
        ident, ident_b = ident_f32
        for tt in range(T // 128):
            c, j = tt // 4, tt % 4
            xi, xb = xin[tt % NB], xin_b[tt % NB]
            P.dma("sync", xi[:], x_ap[tt * 128:(tt + 1) * 128, :], writes=[xb])
            sg, sgb = stg[c % 2], stg_b[c % 2]
            for half in range(2):
                bank = (tt * 2 + half) % 8
                pb, pbb = cx.banks[bank], cx.bank_bufs[bank]
                for k in range(4):
                    dc = half * 4 + k
                    P.op("tensor", lambda e, pb=pb, xi=xi, dc=dc, k=k: e.transpose(
                        pb[:, k * 128:(k + 1) * 128], xi[:, dc * 128:(dc + 1) * 128], ident[:]),
                        reads=[xb, ident_b], writes=[pbb])
                eng = "vector" if half == 0 else "scalar"
                src = pb[:].rearrange("p (k t) -> p k t", k=4)
                dst = sg[:, half * 4:(half + 1) * 4, j * 128:(j + 1) * 128]
                if eng == "vector":
                    P.op("vector", lambda e, dst=dst, src=src: e.tensor_copy(out=dst, in_=src),
                         reads=[pbb], writes=[sgb])
                else:
                    P.op("scalar", lambda e, dst=dst, src=src: e.activation(out=dst, in_=src, func=ACTF.Copy),
                         reads=[pbb], writes=[sgb])
            if j == 3:
                P.dma("gpsimd", xT_ap.rearrange("(k p) t -> p k t", p=128)[:, :, c * 512:(c + 1) * 512],
                      sg[:], reads=[sgb], writes=xT_b[c])


def phase_final(cx, xT_ap, xT_b, out_ap, g_bc, ident_f32):
    P, nc = cx.P, cx.nc
    ident, ident_b = ident_f32
    with contextlib.ExitStack() as st:
        g_t = cx.sb([128, D], F32, "gfin", st)
        g_b = Buf()
        P.dma("sync", g_t[:], g_bc[:, :], writes=[g_b])
        xs = [cx.sb([128, 8, 512], F32, "fx", st) for _ in range(2)]
        xs_b = [Buf() for _ in range(2)]
        ot = [cx.sb([128, D], F32, "fo", st) for _ in range(2)]
        ot_b = [Buf() for _ in range(2)]
        junk = cx.sb([128, 512], F32, "fjunk", st)
        junk_b = Buf()
        ss = [cx.sb([128, 2], F32, "fss", st) for _ in range(2)]
        ss_b = [Buf() for _ in range(2)]
        rs = [cx.sb([128, 1], F32, "frs", st) for _ in range(2)]
        rs_b = [Buf() for _ in range(2)]
        for c in range(NCH):
            xc, xcb = xs[c % 2], xs_b[c % 2]
            P.dma("sync", xc[:], xT_ap.rearrange("(k p) t -> p k t", p=128)[:, :, c * 512:(c + 1) * 512],
                  reads=xT_b[c], writes=[xcb])
            for j in range(4):
                tt = c * 4 + j
                o, ob = ot[tt % 2], ot_b[tt % 2]
                s_, sb_ = ss[tt % 2], ss_b[tt % 2]
                r_, rb_ = rs[tt % 2], rs_b[tt % 2]
                bk = [(tt * 2) % 8, (tt * 2 + 1) % 8]
                for half in range(2):
                    pb, pbb = cx.banks[bk[half]], cx.bank_bufs[bk[half]]
                    for k in range(4):
                        dc = half * 4 + k
                        P.op("tensor", lambda e, pb=pb, xc=xc, dc=dc, k=k, j=j: e.transpose(
                            pb[:, k * 128:(k + 1) * 128], xc[:, dc, j * 128:(j + 1) * 128], ident[:]),
                            reads=[xcb, ident_b], writes=[pbb])
                    P.op("scalar", lambda e, pb=pb, s_=s_, half=half: e.activation(
                        out=junk[:], in_=pb[:], func=ACTF.Square, accum_out=s_[:, half:half + 1]),
                        reads=[pbb], writes=[junk_b, sb_])
                P.op("vector", lambda e, s_=s_, r_=r_: e.tensor_tensor(
                    out=r_[:], in0=s_[:, 0:1], in1=s_[:, 1:2], op=ALU.add), reads=[sb_], writes=[rb_])
                P.op("scalar", lambda e, r_=r_: e.activation(
                    out=r_[:], in_=r_[:], func=ACTF.Sqrt, scale=1.0 / D, bias=EPS), reads=[rb_], writes=[rb_])
                P.op("vector", lambda e, r_=r_: e.reciprocal(out=r_[:], in_=r_[:]), reads=[rb_], writes=[rb_])
                for half in range(2):
                    pb, pbb = cx.banks[bk[half]], cx.bank_bufs[bk[half]]
                    P.op("vector", lambda e, pb=pb, o=o, r_=r_, half=half: e.scalar_tensor_tensor(
                        out=o[:, half * 512:(half + 1) * 512], in0=pb[:], scalar=r_[:, 0:1],
                        in1=g_t[:, half * 512:(half + 1) * 512], op0=ALU.mult, op1=ALU.mult),
                        reads=[pbb, rb_, g_b], writes=[ob])
                P.dma("gpsimd", out_ap[tt * 128:(tt + 1) * 128, :], o[:], reads=[ob])


def load_w(cx, st, wdst, wbufs, w_ap, K, N, row_gain=None, col0=0, piece=1024):
    for _ in load_w_iter(cx, wdst, wbufs, w_ap, K, N, col0, piece):
        pass


def load_w_iter(cx, wdst, wbufs, w_ap, K, N, col0=0, piece=1024):
    P = cx.P
    KC = K // 128
    for kc in range(KC):
        P.dma("gpsimd", wdst[:, kc, col0:col0 + N], w_ap[kc * 128:(kc + 1) * 128, :], writes=[wbufs[kc % 2]])
        yield


def rms_chunk(cx, xc, xcb, KC, xn, xnb, sqs, sqs_b, ones_bf, statbank, rstd, rstd_b, Dn, ncol=512, gain=None):
    P = cx.P
    ones, ones_b = ones_bf
    pb, pbb = cx.banks[statbank], cx.bank_bufs[statbank]
    for k in range(KC):
        sq, sqb = sqs[k % len(sqs)], sqs_b[k % len(sqs)]
        P.op("scalar", lambda e, sq=sq, k=k: e.activation(out=sq[:, 0:ncol], in_=xc[:, k, 0:ncol], func=ACTF.Square),
             reads=[xcb], writes=[sqb])
        P.op("tensor", lambda e, sq=sq, k=k: e.matmul(pb[:, 0:ncol], lhsT=ones[:, :], rhs=sq[:, 0:ncol],
                                                       start=(k == 0), stop=(k == KC - 1)),
             reads=[sqb, ones_b], writes=[pbb])
    P.op("scalar", lambda e: e.activation(out=rstd[:, 0:ncol], in_=pb[:, 0:ncol], func=ACTF.Sqrt, scale=1.0 / Dn, bias=EPS),
         reads=[pbb], writes=[rstd_b])
    P.op("vector", lambda e: e.reciprocal(out=rstd[:, 0:ncol], in_=rstd[:, 0:ncol]), reads=[rstd_b], writes=[rstd_b])
    g, gb = gain
    for k in range(KC):
        P.op("vector", lambda e, k=k: e.scalar_tensor_tensor(
            out=xn[:, k, 0:ncol], in0=xc[:, k, 0:ncol], scalar=g[:, k:k + 1], in1=rstd[:, 0:ncol], op0=ALU.mult, op1=ALU.mult),
            reads=[xcb, rstd_b, gb], writes=[xnb[k % 2]])


def xT_view(xT_ap, c):
    return xT_ap.rearrange("(k p) t -> p k t", p=128)[:, :, c * 512:(c + 1) * 512]


def ffn_weights(cx, st, w_in_ap, w_out_ap, g_ap):
    P = cx.P
    HC = FFN_H // 128
    w1 = cx.sb([128, 8, 2 * FFN_H], BF16, "w1", st)
    w2 = cx.sb([128, HC, D], BF16, "w2", st)
    w1b, w2b = [Buf(), Buf()], [Buf(), Buf()]
    g = cx.sb([128, 8], F32, "ffg", st)
    gb = Buf()
    P.dma("sync", g[:], g_ap, writes=[gb])

    def gen():
        yield from load_w_iter(cx, w1, w1b, w_in_ap, D, 2 * FFN_H)
        yield from load_w_iter(cx, w2, w2b, w_out_ap, FFN_H, D)
    return (w1, w2, w1b, w2b, g, gb), gen()


def phase_outproj_ffn(cx, xT_ap, xT_b, wo_ap, KC, scr, w_in_ap, w_out_ap, g_ap, consts):
    P = cx.P
    with contextlib.ExitStack() as st:
        pre, gen = ffn_weights(cx, st, w_in_ap, w_out_ap, g_ap)
        phase_out_proj(cx, xT_ap, xT_b, wo_ap, KC, scr, bg=gen)
        for _ in gen:
            pass
        P.barrier()
        phase_ffn(cx, xT_ap, xT_b, w_in_ap, w_out_ap, g_ap, consts, pre=(st, pre))


def phase_ffn(cx, xT_ap, xT_b, w_in_ap, w_out_ap, g_ap, consts, pre=None):
    P, nc = cx.P, cx.nc
    ones_bf = consts["ones_bf"]
    HC = FFN_H // 128
    with contextlib.ExitStack() as st_own:
        if pre is None:
            st = st_own
            (w1, w2, w1b, w2b, g, gb), gen = ffn_weights(cx, st, w_in_ap, w_out_ap, g_ap)
            for _ in gen:
                pass
        else:
            st, (w1, w2, w1b, w2b, g, gb) = pre
            st = st_own
        xs = [cx.sb([128, 8, 512], F32, "fx", st)]
        xs_b = [Buf()]
        xr = [cx.sb([128, 512], F32, "fxr", st) for _ in range(2)]
        xr_b = [Buf(), Buf()]
        xns = [cx.sb([128, 8, 512], BF16, "fxn", st) for _ in range(2)]
        xnbs = [[Buf(), Buf()], [Buf(), Buf()]]
        act = cx.sb([128, HC, 512], BF16, "fact", st)
        act_b = [Buf() for _ in range(HC)]
        sqs = [cx.sb([128, 512], BF16, "fsq", st) for _ in range(2)]
        sqs_b = [Buf(), Buf()]
        sg = [cx.sb([128, 512], BF16, "fsg", st) for _ in range(2)]
        sg_b = [Buf(), Buf()]
        rstd = cx.sb([128, 512], F32, "frstd", st)
        rstd_b = Buf()

        def load(c):
            P.dma("sync", xs[0][:], xT_view(xT_ap, c), reads=xT_b[c], writes=[xs_b[0]])

        def norm(c):
            rms_chunk(cx, xs[0], xs_b[0], 8, xns[c % 2], xnbs[c % 2], sqs, sqs_b, ones_bf, 6, rstd, rstd_b, D, gain=(g, gb))

        load(0)
        norm(0)
        for c in range(NCH):
            xn, xnb = xns[c % 2], xnbs[c % 2]
            if c + 1 < NCH:
                load(c + 1)
            for j in range(HC):
                if j == 6 and c + 1 < NCH:
                    norm(c + 1)
                ga, gab = cx.banks[j % 2], cx.bank_bufs[j % 2]
                ua, uab = cx.banks[2 + j % 2], cx.bank_bufs[2 + j % 2]
                for k in range(8):
                    P.op("tensor", lambda e, ga=ga, j=j, k=k, xn=xn: e.matmul(
                        ga[:, :], lhsT=w1[:, k, j * 128:(j + 1) * 128], rhs=xn[:, k, :], start=(k == 0), stop=(k == 7)),
                        reads=xnb + w1b, writes=[gab])
                for k in range(8):
                    P.op("tensor", lambda e, ua=ua, j=j, k=k, xn=xn: e.matmul(
                        ua[:, :], lhsT=w1[:, k, FFN_H + j * 128:FFN_H + (j + 1) * 128], rhs=xn[:, k, :],
                        start=(k == 0), stop=(k == 7)), reads=xnb + w1b, writes=[uab])
                s_, sb_ = sg[j % 2], sg_b[j % 2]
                P.op("scalar", lambda e, s_=s_, ga=ga: e.activation(out=s_[:], in_=ga[:, :], func=ACTF.Silu),
                     reads=[gab], writes=[sb_])
                P.op("vector", lambda e, s_=s_, ua=ua, j=j: e.tensor_tensor(
                    out=act[:, j, :], in0=ua[:, :], in1=s_[:], op=ALU.mult), reads=[uab, sb_], writes=[act_b[j]])
            for dc in range(8):
                ya, yab = cx.banks[4 + dc % 2], cx.bank_bufs[4 + dc % 2]
                xr_, xrb_ = xr[dc % 2], xr_b[dc % 2]
                xsl = xT_ap[dc * 128:(dc + 1) * 128, c * 512:(c + 1) * 512]
                P.dma("sync", xr_[:], xsl, reads=[xT_b[c][dc]], writes=[xrb_])
                for j in range(HC):
                    P.op("tensor", lambda e, ya=ya, j=j, dc=dc: e.matmul(
                        ya[:, :], lhsT=w2[:, j, dc * 128:(dc + 1) * 128], rhs=act[:, j, :],
                        start=(j == 0), stop=(j == HC - 1)), reads=[act_b[j]] + w2b, writes=[yab])
                P.op("vector", lambda e, ya=ya, xr_=xr_: e.tensor_tensor(
                    out=xr_[:], in0=ya[:, :], in1=xr_[:], op=ALU.add), reads=[yab], writes=[xrb_])
                P.dma("gpsimd", xsl, xr_[:], reads=[xrb_], writes=[xT_b[c][dc]])


GQ_L = [0, 1, 2, 3, 8, 9, 10, 11]
GQ_U = [4, 5, 6, 7, 12, 13, 14, 15]


class AttnStream:
    PV_LAG = 2

    def __init__(self, cx, st_bufs, pv_lag=2):
        self.cx = cx
        self.b = st_bufs
        self.PV_LAG = pv_lag
        assert len(st_bufs["pt"]) >= pv_lag + 1
        self.pending = []
        self.gidx = 0
        self.deferred = []
        self.mask_i = 0

    def block(self, q_rhs, q_bufs, tiles, scale, vcols, acc_bank, finish=None):
        cx, P, b = self.cx, self.cx.P, self.b
        pts, pts_b = b["pt"], b["pt_b"]
        n = len(tiles)
        groups = [list(range(i, min(i + SG, n))) for i in range(0, n, SG)]
        for gi, grp in enumerate(groups):
            u = b["cnt"][0]
            b["cnt"][0] += 1
            slot = u % 2
            trip = cx.trip[slot]
            pi = self.gidx % len(pts)
            self.gidx += 1
            pt, ptb = pts[pi], pts_b[pi]
            bbs = []
            for ii, ti in enumerate(grp):
                lhsT, v_ap, rb, mask, q_ap = tiles[ti]
                rhs = q_rhs if q_ap is None else q_ap
                sbb_ = cx.bank_bufs[SG * slot + ii]
                bbs.append(sbb_)
                P.op("tensor", lambda e, trip=trip, ii=ii, lhsT=lhsT, rhs=rhs: e.matmul(
                    trip[:, ii * 512:(ii + 1) * 512], lhsT=lhsT, rhs=rhs, start=True, stop=True),
                    reads=list(rb) + list(q_bufs), writes=[sbb_])
            w = 512 * len(grp)
            P.op("scalar", lambda e, trip=trip, pt=pt, w=w: e.activation(
                out=pt[:, 0:w], in_=trip[:, 0:w], func=ACTF.Exp, scale=scale), reads=bbs, writes=[ptb])
            for ii, ti in enumerate(grp):
                mask = tiles[ti][3]
                if mask is not None:
                    self.mask_i += 1
                    P.op("gpsimd" if self.mask_i % 4 == 0 else "vector", lambda e, pt=pt, mask=mask, ii=ii: e.tensor_tensor(
                        out=pt[:, ii * 512:(ii + 1) * 512], in0=pt[:, ii * 512:(ii + 1) * 512], in1=mask, op=ALU.mult),
                        reads=[ptb] + b.get("mask_b", []), writes=[ptb])
            items = [(tiles[ti][1], tiles[ti][2], ii, ti == 0, ti == n - 1) for ii, ti in enumerate(grp)]
            self.pending.append((items, pt, ptb, acc_bank, vcols, finish if gi == len(groups) - 1 else None))
            if len(self.pending) > self.PV_LAG:
                self._pv(self.pending.pop(0))
            self._tick()

    def _tick(self):
        for d in self.deferred:
            d[0] -= 1
        while self.deferred and self.deferred[0][0] <= 0:
            self.deferred.pop(0)[1]()

    def _pv(self, entry):
        cx, P = self.cx, self.cx.P
        items, pt, ptb, acc_bank, vcols, finish = entry
        ab, abb = cx.banks[acc_bank], cx.bank_bufs[acc_bank]
        for v_ap, rb, ii, first, last in items:
            P.op("tensor", lambda e, v_ap=v_ap, pt=pt, ii=ii, first=first, last=last, ab=ab, vcols=vcols: e.matmul(
                ab[0:vcols, :], lhsT=v_ap, rhs=pt[:, ii * 512:(ii + 1) * 512], start=first, stop=last),
                reads=list(rb) + [ptb], writes=[abb])
        if finish is not None:
            part2 = finish()
            if part2 is not None:
                self.deferred.append([3, part2])

    def flush(self):
        while self.pending:
            self._pv(self.pending.pop(0))
        while self.deferred:
            self.deferred.pop(0)[1]()


def take_bank(cx, st_bufs):
    u = st_bufs["cnt"][0]
    st_bufs["cnt"][0] += 1
    return SG * (u % 2)


def attn_finish(cx, st_bufs, acc_bank, bc_bank, dst_ap, dst_bufs, consts):
    P = cx.P
    ab, abb = cx.banks[acc_bank], cx.bank_bufs[acc_bank]
    ones, ones_b = consts["ones_bf"]
    k = st_bufs["fin_i"][0]
    st_bufs["fin_i"][0] += 1
    rd, rdb = st_bufs["rd"][k % 2], st_bufs["rd_b"][k % 2]
    rdh, rdhb = st_bufs["rdh"][k % 2], st_bufs["rdh_b"][k % 2]
    bcs, bcsb = st_bufs["bcs"][k % 2], st_bufs["bcs_b"][k % 2]
    ot, otb = st_bufs["ot"][k % 2], st_bufs["ot_b"][k % 2]
    P.op("vector", lambda e: e.reciprocal(out=rd[64:65, :], in_=ab[64:65, :]), reads=[abb], writes=[rdb])
    P.op("vector", lambda e: e.tensor_copy(out=rdh[64:65, :], in_=rd[64:65, :]), reads=[rdb], writes=[rdhb])

    def part2():
        bc_bank = take_bank(cx, st_bufs)
        bb, bbb = cx.banks[bc_bank], cx.bank_bufs[bc_bank]
        P.op("tensor", lambda e: e.matmul(bb[0:64, :], lhsT=ones[64:65, 0:64], rhs=rdh[64:65, :], start=True, stop=True),
             reads=[rdhb, ones_b], writes=[bbb])
        P.op("vector", lambda e: e.tensor_copy(out=bcs[0:64, :], in_=bb[0:64, :]), reads=[bbb], writes=[bcsb])
        P.op("vector", lambda e: e.tensor_tensor(out=ot[0:64, :], in0=ab[0:64, :], in1=bcs[0:64, :], op=ALU.mult),
             reads=[abb, bcsb], writes=[otb])
        P.dma("sync", dst_ap, ot[0:64, :], reads=[otb], writes=dst_bufs)
    return part2


def attn_bufs(cx, st, npt=3):
    b = {}
    b["pt"] = [cx.sb([128, SG * 512], BF16, "pt", st) for _ in range(npt)]
    b["pt_b"] = [Buf() for _ in range(npt)]
    b["cnt"] = [0]
    b["fin_i"] = [0]
    b["rd"] = [cx.sb([128, 512], F32, "rd", st) for _ in range(2)]
    b["rd_b"] = [Buf(), Buf()]
    b["rdh"] = [cx.sb([128, 512], BF16, "rdh", st) for _ in range(2)]
    b["rdh_b"] = [Buf(), Buf()]
    b["bcs"] = [cx.sb([64, 512], F32, "bcs", st) for _ in range(2)]
    b["bcs_b"] = [Buf(), Buf()]
    b["ot"] = [cx.sb([64, 512], BF16, "ot", st) for _ in range(2)]
    b["ot_b"] = [Buf(), Buf()]
    return b


def phase_gqa_proj(cx, xT_ap, xT_b, aps, consts, scr):
    P, nc = cx.P, cx.nc
    ones_bf = consts["ones_bf"]
    blk, blk_b = consts["blk_bf"]
    NW = 2816
    with contextlib.ExitStack() as st:
        w = cx.sb([128, 8, NW], BF16, "gw", st)
        wb = [Buf(), Buf()]
        g = cx.sb([128, 8], F32, "gg", st)
        gb = Buf()
        P.dma("sync", g[:], aps["norm"], writes=[gb])
        load_w(cx, st, w, wb, aps["w"], D, NW, row_gain=(g, gb))
        hg = cx.sb([128, 4], F32, "ghg", st)
        hgb = Buf()
        P.dma("sync", hg[:], aps["hgain"], writes=[hgb])
        ctab = cx.sb([128, T], F32, "ctab", st)
        stab = cx.sb([128, T], F32, "stab", st)
        tab_b = Buf()
        P.dma("sync", ctab[:], aps["ctab"], writes=[tab_b])
        P.dma("scalar", stab[:], aps["stab"], writes=[tab_b])
        xs = cx.sb([128, 8, 512], F32, "gx", st)
        xsb = Buf()
        xns = [cx.sb([128, 8, 512], BF16, "gxn", st) for _ in range(2)]
        xnbs = [[Buf(), Buf()], [Buf(), Buf()]]
        sqs = [cx.sb([128, 512], BF16, "gsq", st) for _ in range(2)]
        sqs_b = [Buf(), Buf()]
        rstd = cx.sb([128, 512], F32, "grstd", st)
        rstd_b = Buf()
        hr = [cx.sb([128, 512], F32, "ghr", st) for _ in range(2)]
        hr_b = [Buf(), Buf()]
        t1 = [cx.sb([128, 512], F32, "gt1", st) for _ in range(2)]
        t1_b = [Buf(), Buf()]
        t2 = [cx.sb([128, 512], F32, "gt2", st) for _ in range(2)]
        t2_b = [Buf(), Buf()]
        qo = [cx.sb([128, 512], BF16, "gqo", st) for _ in range(2)]
        qo_b = [Buf(), Buf()]
        vst = [cx.sb([128, 4, 256], BF16, "gvst", st) for _ in range(2)]
        vst_b = [Buf(), Buf()]
        it = 0

        def xload(c):
            P.dma("sync", xs[:], xT_view(xT_ap, c), reads=xT_b[c], writes=[xsb])

        def xnorm(c):
            rms_chunk(cx, xs, xsb, 8, xns[c % 2], xnbs[c % 2], sqs, sqs_b, ones_bf, 6, rstd, rstd_b, D, gain=(g, gb))

        xload(0)
        xnorm(0)
        for c in range(NCH):
            xn, xnb = xns[c % 2], xnbs[c % 2]
            if c + 1 < NCH:
                xload(c + 1)
            for cc in range(10):
                if cc == 5 and c + 1 < NCH:
                    xnorm(c + 1)
                isq = cc < 8
                ca = cc * 128 if isq else 2048 + (cc - 8) * 128
                cbo = 1024 + cc * 128 if isq else 2304 + (cc - 8) * 128
                gcol = 0 if isq else 2
                A, Ab = cx.banks[it % 2], cx.bank_bufs[it % 2]
                B, Bb = cx.banks[2 + it % 2], cx.bank_bufs[2 + it % 2]
                S, Sb = cx.banks[4 + it % 2], cx.bank_bufs[4 + it % 2]
                for k in range(8):
                    P.op("tensor", lambda e, A=A, k=k, ca=ca, xn=xn: e.matmul(
                        A[:, :], lhsT=w[:, k, ca:ca + 128], rhs=xn[:, k, :], start=(k == 0), stop=(k == 7)),
                        reads=xnb + wb, writes=[Ab])
                for k in range(8):
                    P.op("tensor", lambda e, B=B, k=k, cbo=cbo, xn=xn: e.matmul(
                        B[:, :], lhsT=w[:, k, cbo:cbo + 128], rhs=xn[:, k, :], start=(k == 0), stop=(k == 7)),
                        reads=xnb + wb, writes=[Bb])
                sq, sqb = sqs[it % 2], sqs_b[it % 2]
                P.op("scalar", lambda e, sq=sq, A=A: e.activation(out=sq[:], in_=A[:, :], func=ACTF.Square),
                     reads=[Ab], writes=[sqb])
                P.op("tensor", lambda e, S=S, sq=sq: e.matmul(S[:, :], lhsT=blk[:, :], rhs=sq[:], start=True, stop=True),
                     reads=[sqb, blk_b], writes=[Sb])
                h_, hb_ = hr[it % 2], hr_b[it % 2]
                P.op("scalar", lambda e, h_=h_, S=S: e.activation(
                    out=h_[:], in_=S[:, :], func=ACTF.Sqrt, scale=1.0 / 64, bias=EPS), reads=[Sb], writes=[hb_])
                P.op("vector", lambda e, h_=h_: e.reciprocal(out=h_[:], in_=h_[:]), reads=[hb_], writes=[hb_])
                a_, ab_ = t1[it % 2], t1_b[it % 2]
                b_, bb_ = t2[it % 2], t2_b[it % 2]
                P.op("vector", lambda e, a_=a_, A=A, gcol=gcol, c=c: e.scalar_tensor_tensor(
                    out=a_[:], in0=A[:, :], scalar=hg[:, gcol:gcol + 1], in1=ctab[:, c * 512:(c + 1) * 512],
                    op0=ALU.mult, op1=ALU.mult), reads=[Ab, hgb, tab_b], writes=[ab_])
                P.op("vector", lambda e, b_=b_, B=B, gcol=gcol, c=c: e.scalar_tensor_tensor(
                    out=b_[:], in0=B[:, :], scalar=hg[:, gcol + 1:gcol + 2], in1=stab[:, c * 512:(c + 1) * 512],
                    op0=ALU.mult, op1=ALU.mult), reads=[Bb, hgb, tab_b], writes=[bb_])
                P.op("gpsimd", lambda e, a_=a_, b_=b_: e.tensor_tensor(out=a_[:], in0=a_[:], in1=b_[:], op=ALU.add),
                     reads=[ab_, bb_], writes=[ab_])
                q_, qb_ = qo[it % 2], qo_b[it % 2]
                P.op("gpsimd", lambda e, a_=a_, h_=h_, q_=q_: e.tensor_tensor(out=q_[:], in0=a_[:], in1=h_[:], op=ALU.mult),
                     reads=[ab_, hb_], writes=[qb_])
                if isq:
                    dst = scr["qT"][cc * 128:(cc + 1) * 128, c * 512:(c + 1) * 512]
                    P.dma("sync", dst, q_[:], reads=[qb_], writes=[scr["qT_b"][cc]])
                else:
                    dst = scr["kT"][(cc - 8) * 128:(cc - 7) * 128, c * 512:(c + 1) * 512]
                    P.dma("sync", dst, q_[:], reads=[qb_], writes=[scr["kT_b"]])
                it += 1
            vs, vsb = vst[c % 2], vst_b[c % 2]
            for j in range(4):
                V, Vb = cx.banks[7], cx.bank_bufs[7]
                for k in range(8):
                    P.op("tensor", lambda e, V=V, k=k, j=j, xn=xn: e.matmul(
                        V[:, 0:256], lhsT=xn[:, k, j * 128:(j + 1) * 128], rhs=w[:, k, 2560:2816],
                        start=(k == 0), stop=(k == 7)), reads=xnb + wb, writes=[Vb])
                P.op("scalar", lambda e, V=V, vs=vs, j=j: e.activation(
                    out=vs[:, j, :], in_=V[:, 0:256], func=ACTF.Copy), reads=[Vb], writes=[vsb])
            P.dma("gpsimd", scr["v"][c * 512:(c + 1) * 512, :].rearrange("(j p) f -> p j f", p=128), vs[:],
                  reads=[vsb], writes=[scr["v_b"]])


def phase_gqa_attn(cx, consts, scr):
    P = cx.P
    with contextlib.ExitStack() as st:
        kres = cx.sb([128, 2, SEQ], BF16, "kres", st)
        kres_b = Buf()
        VT = 260
        vflat = cx.sb([128, 64 * VT + 128], BF16, "vres", st)
        vres = vflat[:, 0:64 * VT].rearrange("p (t f) -> p t f", f=VT)
        vres_b = Buf()
        for r in range(2):
            for pr in range(2):
                P.dma("sync" if pr == 0 else "scalar", kres[:, pr, r * T:(r + 1) * T],
                      scr["kT_all"][r * 256 + pr * 128:r * 256 + (pr + 1) * 128, :], reads=[scr["kT_all_b"]], writes=[kres_b])
        P.op("vector", lambda e: e.memset(vflat[:], 1.0), writes=[vres_b])
        for i in range(4):
            for hh in range(4):
                P.dma("sync" if hh % 2 == 0 else "scalar", vres[:, i * 16:(i + 1) * 16, hh * 65:hh * 65 + 64],
                      scr["v_all"][i * 2048:(i + 1) * 2048, hh * 64:(hh + 1) * 64].rearrange("(t p) d -> p t d", p=128),
                      reads=[scr["v_all_b"]], writes=[vres_b])
        qz = [[cx.sb([128, T], BF16, "qz", st) for _ in range(2)] for _ in range(2)]
        qz_b = [[Buf(), Buf()] for _ in range(2)]
        for i in range(2):
            P.op("gpsimd", lambda e, i=i: e.memset(qz[i][0][64:128, :], 0.0), writes=[qz_b[i][0]])
            P.op("gpsimd", lambda e, i=i: e.memset(qz[i][1][0:64, :], 0.0), writes=[qz_b[i][1]])
        ab = attn_bufs(cx, st)
        stream = AttnStream(cx, ab)
        blk_i = 0
        for c in range(8):
            pr = c // 4
            for hf in range(2):
                lo = hf * 64
                P.dma("sync" if hf == 0 else "scalar", qz[c % 2][hf][lo:lo + 64, :], scr["qT"][c * 128 + lo:c * 128 + lo + 64, :],
                      reads=[scr["qT_b"][c]], writes=[qz_b[c % 2][hf]])
            for hf in range(2):
                gkv = 2 * pr + hf
                lo = hf * 64
                qr, qrb = qz[c % 2][hf], qz_b[c % 2][hf]
                for qc in range(NCH):
                    tiles = []
                    for kt in range(64):
                        v0 = kt * VT + gkv * 65
                        tiles.append((kres[:, pr, kt * 128:(kt + 1) * 128], vflat[:, v0:v0 + 128], [kres_b, vres_b], None, None))
                    acc = 6 + blk_i % 2
                    dst = scr["oT"][c * 128 + lo:c * 128 + lo + 64, qc * 512:(qc + 1) * 512]
                    stream.block(qr[:, qc * 512:(qc + 1) * 512], [qrb], tiles, 0.125, 128, acc,
                                 finish=lambda acc=acc, dst=dst, qc=qc: attn_finish(cx, ab, acc, 5, dst, [scr["oT_b"][qc]], consts))
                    blk_i += 1
        stream.flush()


def phase_out_proj(cx, xT_ap, xT_b, w_ap, KC, scr, bg=None):
    P = cx.P
    with contextlib.ExitStack() as st:
        w = cx.sb([128, KC, D], BF16, "wo", st)
        wb = [Buf(), Buf()]
        load_w(cx, st, w, wb, w_ap, KC * 128, D)
        oc = [cx.sb([128, KC, 512], BF16, "oc", st) for _ in range(2)]
        oc_b = [Buf(), Buf()]
        xr = [cx.sb([128, 512], F32, "oxr", st) for _ in range(3)]
        xr_b = [Buf() for _ in range(3)]
        it = 0
        for c in range(NCH):
            o_, ob_ = oc[c % 2], oc_b[c % 2]
            P.dma("sync", o_[:], scr["oT"].rearrange("(k p) t -> p k t", p=128)[:, :, c * 512:(c + 1) * 512],
                  reads=[scr["oT_b"][c]], writes=[ob_])
            for dc in range(8):
                ya, yab = cx.banks[it % 4], cx.bank_bufs[it % 4]
                xr_, xrb_ = xr[it % 3], xr_b[it % 3]
                xsl = xT_ap[dc * 128:(dc + 1) * 128, c * 512:(c + 1) * 512]
                P.dma("scalar", xr_[:], xsl, reads=[xT_b[c][dc]], writes=[xrb_])
                for k in range(KC):
                    P.op("tensor", lambda e, ya=ya, k=k, dc=dc, o_=o_: e.matmul(
                        ya[:, :], lhsT=w[:, k, dc * 128:(dc + 1) * 128], rhs=o_[:, k, :],
                        start=(k == 0), stop=(k == KC - 1)), reads=[ob_] + wb, writes=[yab])
                P.op("vector", lambda e, ya=ya, xr_=xr_: e.tensor_tensor(
                    out=xr_[:], in0=ya[:, :], in1=xr_[:], op=ALU.add), reads=[yab], writes=[xrb_])
                P.dma("gpsimd", xsl, xr_[:], reads=[xrb_], writes=[xT_b[c][dc]])
                it += 1
                if bg is not None:
                    next(bg, None)


DIL = (1, 4, 16)
DIL_DK0 = {1: list(range(-128, 513, 128)), 4: list(range(-256, 641, 128)), 16: list(range(-1024, 1409, 128))}
DIL_J0 = {d: max(v) for d, v in DIL_DK0.items()}
DIL_W = {d: max(v) - min(v) + 512 for d, v in DIL_DK0.items()}
EXT = T + 2048


def phase_even_proj(cx, xT_ap, xT_b, aps, consts, scr):
    P = cx.P
    ones_bf = consts["ones_bf"]
    NW = 3008
    with contextlib.ExitStack() as st:
        w = cx.sb([128, 8, NW], BF16, "ew", st)
        wb = [Buf(), Buf()]
        g = cx.sb([128, 8], F32, "eg", st)
        gb = Buf()
        P.dma("sync", g[:], aps["norm"], writes=[gb])
        load_w(cx, st, w, wb, aps["w"], D, NW, row_gain=(g, gb))
        wq = cx.sb([128, 3, 1024], BF16, "ewq", st)
        wqb = [Buf(), Buf()]
        load_w(cx, st, wq, wqb, aps["wuq"], 384, 1024)
        lg = cx.sb([128, 5], F32, "elg", st)
        lgb = Buf()
        P.dma("sync", lg[:], aps["lgain"], writes=[lgb])
        ctab = cx.sb([128, T], F32, "ectab", st)
        stab = cx.sb([128, T], F32, "estab", st)
        tab_b = Buf()
        P.dma("sync", ctab[64:96, :], aps["ctab"], writes=[tab_b])
        P.dma("scalar", stab[64:96, :], aps["stab"], writes=[tab_b])
        xs = cx.sb([128, 8, 512], F32, "ex", st)
        xsb = Buf()
        xns = [cx.sb([128, 8, 512], BF16, "exn", st) for _ in range(2)]
        xnbs = [[Buf(), Buf()], [Buf(), Buf()]]
        cur = {}
        sqs = [cx.sb([128, 512], BF16, "esq", st) for _ in range(2)]
        sqs_b = [Buf(), Buf()]
        rstd = cx.sb([128, 512], F32, "erstd", st)
        rstd_b = Buf()
        lat = cx.sb([128, 5, 512], F32, "elat", st)
        lat_b = [Buf() for _ in range(5)]
        lrs = [cx.sb([128, 512], F32, "elrs", st) for _ in range(2)]
        lrs_b = [Buf(), Buf()]
        latn = cx.sb([128, 5, 512], BF16, "elatn", st)
        latn_b = [Buf() for _ in range(5)]
        t1 = [cx.sb([128, 512], F32, "et1", st) for _ in range(2)]
        t1_b = [Buf(), Buf()]
        t2 = [cx.sb([128, 512], F32, "et2", st) for _ in range(2)]
        t2_b = [Buf(), Buf()]
        qo = [cx.sb([128, 512], BF16, "eqo", st) for _ in range(3)]
        qo_b = [Buf() for _ in range(3)]
        vst = [cx.sb([128, 4, 768], BF16, "evst", st) for _ in range(2)]
        vst_b = [Buf(), Buf()]
        bk = [0]
        zt = cx.sb([128, T], BF16, "ezero", st)
        ztb = Buf()
        P.op("gpsimd", lambda e: e.memset(zt[:], 0.0), writes=[ztb])
        P.dma("sync", scr["kr"][32:160, :], zt[:], reads=[ztb], writes=[scr["kr_b"]])
        P.dma("sync", scr["kr"][128:256, :], zt[:], reads=[ztb], writes=[scr["kr_b"]])

        def nb():
            bk[0] += 1
            i = bk[0] % 6
            return cx.banks[i], cx.bank_bufs[i]

        def proj(bank, bb, c0, ncols, prow0=0):
            xn_, xnb_ = cur["xn"], cur["xnb"]
            for k in range(8):
                P.op("tensor", lambda e, k=k: e.matmul(
                    bank[prow0:prow0 + ncols, :], lhsT=w[:, k, c0:c0 + ncols], rhs=xn_[:, k, :], start=(k == 0), stop=(k == 7)),
                    reads=xnb_ + wb, writes=[bb])

        def xload(c):
            P.dma("sync", xs[:], xT_view(xT_ap, c), reads=xT_b[c], writes=[xsb])

        def xnorm(c):
            rms_chunk(cx, xs, xsb, 8, xns[c % 2], xnbs[c % 2], sqs, sqs_b, ones_bf, 6, rstd, rstd_b, D, gain=(g, gb))

        qi = 0
        xload(0)
        xnorm(0)
        for c in range(NCH):
            cs = slice(c * 512, (c + 1) * 512)
            xn, xnb = xns[c % 2], xnbs[c % 2]
            cur["xn"], cur["xnb"] = xn, xnb
            if c + 1 < NCH:
                xload(c + 1)
            for k5 in range(5):
                A, Ab = nb()
                proj(A, Ab, k5 * 128, 128)
                P.op("scalar", lambda e, A=A, k5=k5: e.activation(out=lat[:, k5, :], in_=A[:, :], func=ACTF.Copy),
                     reads=[Ab], writes=[lat_b[k5]])
            for grp, (ks, Dn) in enumerate((((0, 1, 2), 384), ((3, 4), 256))):
                Sb_, Sbb_ = cx.banks[7], cx.bank_bufs[7]
                for ii, k5 in enumerate(ks):
                    sq, sqb = sqs[ii % 2], sqs_b[ii % 2]
                    P.op("scalar", lambda e, sq=sq, k5=k5: e.activation(out=sq[:], in_=lat[:, k5, :], func=ACTF.Square),
                         reads=[lat_b[k5]], writes=[sqb])
                    P.op("tensor", lambda e, sq=sq, ii=ii, n=len(ks): e.matmul(
                        Sb_[:, :], lhsT=ones_bf[0][:, :], rhs=sq[:], start=(ii == 0), stop=(ii == n - 1)),
                        reads=[sqb, ones_bf[1]], writes=[Sbb_])
                r_, rb_ = lrs[grp], lrs_b[grp]
                P.op("scalar", lambda e, r_=r_, Dn=Dn: e.activation(out=r_[:], in_=Sb_[:, :], func=ACTF.Sqrt, scale=1.0 / Dn, bias=EPS),
                     reads=[Sbb_], writes=[rb_])
                P.op("vector", lambda e, r_=r_: e.reciprocal(out=r_[:], in_=r_[:]), reads=[rb_], writes=[rb_])
                for k5 in ks:
                    P.op("vector", lambda e, k5=k5, r_=r_: e.scalar_tensor_tensor(
                        out=latn[:, k5, :], in0=lat[:, k5, :], scalar=lg[:, k5:k5 + 1], in1=r_[:], op0=ALU.mult, op1=ALU.mult),
                        reads=[lat_b[k5], rb_, lgb], writes=[latn_b[k5]])
            for k5 in (3, 4):
                P.dma("gpsimd", scr["lat"][(k5 - 3) * 128:(k5 - 2) * 128, cs], latn[:, k5, :], reads=[latn_b[k5]], writes=[scr["lat_b"]])

            def rope_to(dst_rows, A, Ab, B, Bb, q_, qb_, cs=cs, qi_=None):
                a_, ab_ = t1[qi_ % 2], t1_b[qi_ % 2]
                b_, bb_ = t2[qi_ % 2], t2_b[qi_ % 2]
                P.op("vector", lambda e: e.tensor_tensor(out=a_[64:96, :], in0=A[64:96, :], in1=ctab[64:96, cs], op=ALU.mult),
                     reads=[Ab, tab_b], writes=[ab_])
                P.op("vector", lambda e: e.tensor_tensor(out=b_[64:96, :], in0=B[64:96, :], in1=stab[64:96, cs], op=ALU.mult),
                     reads=[Bb, tab_b], writes=[bb_])
                P.op("gpsimd", lambda e: e.tensor_tensor(out=q_[64:96, :], in0=a_[64:96, :], in1=b_[64:96, :], op=ALU.add),
                     reads=[ab_, bb_], writes=[qb_])

            A, Ab = nb()
            B, Bb = nb()
            proj(A, Ab, 640, 32, prow0=64)
            proj(B, Bb, 672, 32, prow0=64)
            q_, qb_ = qo[qi % 3], qo_b[qi % 3]
            rope_to(None, A, Ab, B, Bb, q_, qb_, cs=cs, qi_=qi)
            P.dma("gpsimd", scr["kr"][0:32, cs], q_[64:96, :], reads=[qb_], writes=[scr["kr_b"]])
            qi += 1
            for hh in range(8):
                A, Ab = nb()
                B, Bb = nb()
                for k in range(3):
                    P.op("tensor", lambda e, A=A, k=k, hh=hh: e.matmul(
                        A[0:96, :], lhsT=wq[:, k, hh * 96:(hh + 1) * 96], rhs=latn[:, k, :], start=(k == 0), stop=(k == 2)),
                        reads=[latn_b[0], latn_b[1], latn_b[2]] + wqb, writes=[Ab])
                for k in range(3):
                    P.op("tensor", lambda e, B=B, k=k, hh=hh: e.matmul(
                        B[64:96, :], lhsT=wq[:, k, 768 + hh * 32:768 + (hh + 1) * 32], rhs=latn[:, k, :], start=(k == 0), stop=(k == 2)),
                        reads=[latn_b[0], latn_b[1], latn_b[2]] + wqb, writes=[Bb])
                q_, qb_ = qo[qi % 3], qo_b[qi % 3]
                P.op("scalar", lambda e, q_=q_, A=A: e.activation(out=q_[0:64, :], in_=A[0:64, :], func=ACTF.Copy),
                     reads=[Ab], writes=[qb_])
                rope_to(None, A, Ab, B, Bb, q_, qb_, cs=cs, qi_=qi)
                P.dma("sync", scr["mq"][hh * 96:(hh + 1) * 96, cs], q_[0:96, :], reads=[qb_], writes=[scr["mq_b"][hh]])
                qi += 1
            if c + 1 < NCH:
                xnorm(c + 1)
            for cc in range(12):
                A, Ab = nb()
                proj(A, Ab, 704 + cc * 128, 128)
                q_, qb_ = qo[qi % 3], qo_b[qi % 3]
                if cc % 2 == 0:
                    P.op("scalar", lambda e, q_=q_, A=A: e.activation(out=q_[:], in_=A[:, :], func=ACTF.Copy), reads=[Ab], writes=[qb_])
                else:
                    P.op("vector", lambda e, q_=q_, A=A: e.tensor_copy(out=q_[:], in_=A[:, :]), reads=[Ab], writes=[qb_])
                if cc < 6:
                    P.dma("sync", scr["dq"][cc * 128:(cc + 1) * 128, cs], q_[:], reads=[qb_], writes=[scr["dq_b"]])
                else:
                    gg_ = (cc - 6) // 2
                    rr_ = ((cc - 6) % 2) * 128
                    P.dma("sync", scr["dk"][gg_][rr_:rr_ + 128, cs], q_[:], reads=[qb_], writes=[scr["dk_b"]])
                qi += 1
            vs, vsb = vst[c % 2], vst_b[c % 2]
            for j in range(4):
                for part, (c0, nn) in enumerate(((0, 512), (512, 256))):
                    V, Vb = nb()
                    for k in range(8):
                        P.op("tensor", lambda e, V=V, k=k, j=j, c0=c0, nn=nn, xn=xn: e.matmul(
                            V[:, 0:nn], lhsT=xn[:, k, j * 128:(j + 1) * 128], rhs=w[:, k, 2240 + c0:2240 + c0 + nn],
                            start=(k == 0), stop=(k == 7)), reads=xnb + wb, writes=[Vb])
                    if part == 0:
                        P.op("scalar", lambda e, V=V, j=j, c0=c0, nn=nn, vs=vs: e.activation(out=vs[:, j, c0:c0 + nn], in_=V[:, 0:nn], func=ACTF.Copy),
                             reads=[Vb], writes=[vsb])
                    else:
                        P.op("vector", lambda e, V=V, j=j, c0=c0, nn=nn, vs=vs: e.tensor_copy(out=vs[:, j, c0:c0 + nn], in_=V[:, 0:nn]),
                             reads=[Vb], writes=[vsb])
            for gg_ in range(3):
                P.dma("gpsimd", scr["dv"][gg_][cs, :].rearrange("(j p) f -> p j f", p=128), vs[:, :, gg_ * 256:(gg_ + 1) * 256],
                      reads=[vsb], writes=[scr["dv_b"]])


def phase_mla_attn(cx, aps, consts, scr, after_setup=None):
    P = cx.P
    with contextlib.ExitStack() as st:
        wk = cx.sb([128, 2, 512], BF16, "mwk", st)
        wv = cx.sb([128, 2, 512], BF16, "mwv", st)
        wkb, wvb = [Buf(), Buf()], [Buf(), Buf()]
        load_w(cx, st, wk, wkb, aps["wuk"], 256, 512)
        load_w(cx, st, wv, wvb, aps["wuv"], 256, 512)
        ckv = cx.sb([128, 2, SEQ], BF16, "mckv", st)
        ckv_b = Buf()
        for r in range(2):
            for k in range(2):
                P.dma("sync" if k == 0 else "scalar", ckv[:, k, r * T:(r + 1) * T],
                      scr["lat_all"][r * 256 + k * 128:r * 256 + (k + 1) * 128, :], reads=[scr["lat_all_b"]], writes=[ckv_b])
        kh = [cx.sb([128, SEQ], BF16, "mkh", st) for _ in range(2)]
        kh_b = [Buf(), Buf()]
        for i in range(2):
            P.op("gpsimd", lambda e, i=i: e.memset(kh[i][96:128, :], 0.0), writes=[kh_b[i]])
            for r in range(2):
                P.dma("sync", kh[i][64:96, r * T:(r + 1) * T], scr["kr_all"][r * 256:r * 256 + 32, :],
                      reads=[scr["kr_all_b"]], writes=[kh_b[i]])
        vflat = cx.sb([128, 64 * 520 + 128], BF16, "mvres", st)
        vres = vflat[:, 0:64 * 520].rearrange("p (t f) -> p t f", f=520)
        vres_b = Buf()
        P.op("gpsimd", lambda e: e.memset(vflat[:], 1.0), writes=[vres_b])
        for kt in range(64):
            V, Vb = cx.banks[6 + kt % 2], cx.bank_bufs[6 + kt % 2]
            for k in range(2):
                P.op("tensor", lambda e, V=V, k=k, kt=kt: e.matmul(
                    V[:, :], lhsT=ckv[:, k, kt * 128:(kt + 1) * 128], rhs=wv[:, k, :], start=(k == 0), stop=(k == 1)),
                    reads=[ckv_b] + wvb, writes=[Vb])
            dst = vres[:, kt, :].rearrange("p (h d) -> p h d", d=65)[:, :, 0:64]
            src = V[:, :].rearrange("p (h d) -> p h d", d=64)
            if kt % 2 == 0:
                P.op("scalar", lambda e, dst=dst, src=src: e.activation(out=dst, in_=src, func=ACTF.Copy), reads=[Vb], writes=[vres_b])
            else:
                P.op("vector", lambda e, dst=dst, src=src: e.tensor_copy(out=dst, in_=src), reads=[Vb], writes=[vres_b])
        qres = [cx.sb([128, T], BF16, "mq", st) for _ in range(2)]
        qres_b = [Buf(), Buf()]
        for i in range(2):
            P.op("gpsimd", lambda e, i=i: e.memset(qres[i][96:128, :], 0.0), writes=[qres_b[i]])
        if after_setup is not None:
            after_setup()
        ab = attn_bufs(cx, st)
        stream = AttnStream(cx, ab)
        blk_i = 0
        for hh in range(8):
            qr, qrb = qres[hh % 2], qres_b[hh % 2]
            P.dma("sync", qr[0:96, :], scr["mq"][hh * 96:(hh + 1) * 96, :], reads=[scr["mq_b"][hh]], writes=[qrb])
            k_, kb_ = kh[hh % 2], kh_b[hh % 2]
            for kc in range(16):
                bk_ = take_bank(cx, ab)
                V, Vb = cx.banks[bk_], cx.bank_bufs[bk_]
                for k in range(2):
                    P.op("tensor", lambda e, V=V, k=k, kc=kc, hh=hh: e.matmul(
                        V[0:64, :], lhsT=wk[:, k, hh * 64:(hh + 1) * 64], rhs=ckv[:, k, kc * 512:(kc + 1) * 512],
                        start=(k == 0), stop=(k == 1)), reads=[ckv_b] + wkb, writes=[Vb])
                P.op("vector", lambda e, V=V, k_=k_, kc=kc: e.tensor_copy(out=k_[0:64, kc * 512:(kc + 1) * 512], in_=V[0:64, :]),
                     reads=[Vb], writes=[kb_])
            for qc in range(NCH):
                tiles = [(k_[:, kt * 128:(kt + 1) * 128], vflat[:, kt * 520 + hh * 65:kt * 520 + hh * 65 + 128], [kb_, vres_b], None, None)
                         for kt in range(64)]
                acc = 6 + blk_i % 2
                dst = scr["oT"][hh * 64:(hh + 1) * 64, qc * 512:(qc + 1) * 512]
                stream.block(qr[:, qc * 512:(qc + 1) * 512], [qrb], tiles, 96 ** -0.5, 128, acc,
                             finish=lambda acc=acc, dst=dst, qc=qc: attn_finish(cx, ab, acc, 5, dst, [scr["oT_b"][qc]], consts))
                blk_i += 1
        stream.flush()


def phase_dil_attn(cx, aps, consts, scr):
    P = cx.P
    with contextlib.ExitStack() as st:
        hv = cx.sb([128, 2], F32, "dhv", st)
        hvb = Buf()
        P.dma("sync", hv[:], aps["halo_valid"], writes=[hvb])
        ab = attn_bufs(cx, st, npt=4)
        sets = []
        zb = Buf()
        for si in range(2):
            kx = [cx.sb([128, EXT], BF16, "dkx", st) for _ in range(3)]
            qx = [cx.sb([128, T], BF16, "dqx", st) for _ in range(3)]
            vxf = [cx.sb([128, 48 * 65 + 128], BF16, "dvx", st) for _ in range(3)]
            vx = [v[:, 0:48 * 65].rearrange("p (t f) -> p t f", f=65) for v in vxf]
            mk = [cx.sb([128, DIL_W[d]], BF16, "dmk", st) for d in DIL]
            bufs = {k_: [Buf() for _ in range(3)] for k_ in ("kx", "qx", "vx", "mk")}
            for gi in range(3):
                P.op("gpsimd", lambda e, kx=kx, gi=gi: e.memset(kx[gi][64:128, :], 0.0), writes=[bufs["kx"][gi]])
                P.op("gpsimd", lambda e, qx=qx, gi=gi: e.memset(qx[gi][64:128, :], 0.0), writes=[bufs["qx"][gi]])
            sets.append((kx, qx, vxf, vx, mk, bufs))

        def load_slot(s_):
            kx, qx, vxf, vx, mk, bufs = sets[s_ % 2]
            for gi, d in enumerate(DIL):
                hd = gi * 4 + s_
                r0 = hd * 64
                rs = s_ * 64
                q = "sync" if gi % 2 == 0 else "gpsimd"
                kb, qb, vb, mb = bufs["kx"][gi], bufs["qx"][gi], bufs["vx"][gi], bufs["mk"][gi]
                P.dma(q, mk[gi][:], aps["dmask"][hd], writes=[mb])
                P.dma(q, qx[gi][0:64, :], scr["dq"][r0:r0 + 64, :], reads=[scr["dq_b"]], writes=[qb])
                P.dma(q, kx[gi][0:64, 1024:1024 + T], scr["dk"][gi][rs:rs + 64, :], reads=[scr["dk_b"]], writes=[kb])
                P.dma(q, kx[gi][0:64, 0:1024], scr["dk_all"][gi][rs:rs + 64, T - 1024:T], reads=[scr["dk_all_b"]], writes=[kb])
                P.dma(q, kx[gi][0:64, 1024 + T:EXT], scr["dk_all"][gi][256 + rs:256 + rs + 64, 0:1024], reads=[scr["dk_all_b"]], writes=[kb])
                P.op("vector", lambda e, vxf=vxf, gi=gi: e.memset(vxf[gi][:], 1.0), writes=[vb])
                for hf2 in range(2):
                    P.dma(q, vx[gi][:, 8 + hf2 * 16:24 + hf2 * 16, 0:64],
                          scr["dv"][gi][hf2 * 2048:(hf2 + 1) * 2048, rs:rs + 64].rearrange("(t p) d -> p t d", p=128),
                          reads=[scr["dv_b"]], writes=[vb])
                P.dma(q, vx[gi][:, 0:8, 0:64], scr["dv_all"][gi][T - 1024:T, rs:rs + 64].rearrange("(t p) d -> p t d", p=128),
                      reads=[scr["dv_all_b"]], writes=[vb])
                P.dma(q, vx[gi][:, 40:48, 0:64], scr["dv_all"][gi][T:T + 1024, rs:rs + 64].rearrange("(t p) d -> p t d", p=128),
                      reads=[scr["dv_all_b"]], writes=[vb])

        def fix_halo(s_):
            kx, qx, vxf, vx, mk, bufs = sets[s_ % 2]
            for gi in range(3):
                vb = bufs["vx"][gi]
                P.op("vector", lambda e, vx=vx, gi=gi: e.tensor_scalar(out=vx[gi][:, 0:8, :], in0=vx[gi][:, 0:8, :], scalar1=hv[:, 0:1],
                                                                     scalar2=None, op0=ALU.mult), reads=[hvb], writes=[vb])
                P.op("vector", lambda e, vx=vx, gi=gi: e.tensor_scalar(out=vx[gi][:, 40:48, :], in0=vx[gi][:, 40:48, :], scalar1=hv[:, 1:2],
                                                                     scalar2=None, op0=ALU.mult), reads=[hvb], writes=[vb])

        blk_i = 0
        stream = AttnStream(cx, ab, pv_lag=3)
        load_slot(0)
        for s_ in range(4):
            fix_halo(s_)
            while stream.pending:
                stream._pv(stream.pending.pop(0))
            if s_ + 1 < 4:
                load_slot(s_ + 1)
            kx, qx, vxf, vx, mk, bufs = sets[s_ % 2]
            ab["mask_b"] = bufs["mk"]
            for qc in range(NCH):
                q0 = qc * 512
                tiles = []
                for gi, d in enumerate(DIL):
                    for dk0 in DIL_DK0[d]:
                        e0 = q0 + dk0 + 1024
                        j0 = DIL_J0[d] - dk0
                        v0 = (e0 // 128) * 65
                        tiles.append((kx[gi][:, e0:e0 + 128], vxf[gi][:, v0:v0 + 128], [bufs["kx"][gi], bufs["vx"][gi], bufs["qx"][gi]],
                                      mk[gi][:, j0:j0 + 512], qx[gi][:, q0:q0 + 512]))
                acc = 6 + blk_i % 2
                dst = scr["oT"][512 + s_ * 64:512 + (s_ + 1) * 64, q0:q0 + 512]
                stream.block(None, [], tiles, 0.125, 128, acc,
                             finish=lambda acc=acc, dst=dst, qc=qc: attn_finish(cx, ab, acc, 5, dst, [scr["oT_b"][qc]], consts))
                blk_i += 1
        stream.flush()


def run_dil_block(cx, st_bufs, q_list, tiles, scale, acc_bank):
    P = cx.P
    pts, pts_b = st_bufs["pt"], st_bufs["pt_b"]
    sbanks = st_bufs["sbanks"]
    ab, abb = cx.banks[acc_bank], cx.bank_bufs[acc_bank]
    n = len(tiles)
    LA = 2
    cnt = st_bufs["cnt"]
    for i in range(n + LA):
        if i < n:
            lhsT, v_ap, rb, mask, c0, c1, gi = tiles[i]
            u = cnt[0] + i
            sbk = sbanks[u % len(sbanks)]
            sb_, sbb_ = cx.banks[sbk], cx.bank_bufs[sbk]
            pt, ptb = pts[u % len(pts)], pts_b[u % len(pts)]
            P.op("tensor", lambda e, sb_=sb_, lhsT=lhsT, gi=gi: e.matmul(
                sb_[:, :], lhsT=lhsT, rhs=q_list[gi], start=True, stop=True), reads=list(rb), writes=[sbb_])
            P.op("scalar", lambda e, sb_=sb_, pt=pt: e.activation(out=pt[:], in_=sb_[:, :], func=ACTF.Exp, scale=scale),
                 reads=[sbb_], writes=[ptb])
            P.op("vector", lambda e, pt=pt, mask=mask: e.tensor_tensor(out=pt[:], in0=pt[:], in1=mask, op=ALU.mult),
                 reads=[ptb] + st_bufs["mask_b"], writes=[ptb])
        j = i - LA
        if j >= 0:
            lhsT, v_ap, rb, mask, c0, c1, gi = tiles[j]
            u = cnt[0] + j
            pt, ptb = pts[u % len(pts)], pts_b[u % len(pts)]
            P.op("tensor", lambda e, v_ap=v_ap, pt=pt, j=j: e.matmul(
                ab[0:65, :], lhsT=v_ap, rhs=pt[:], start=(j == 0), stop=(j == n - 1)), reads=list(rb) + [ptb], writes=[abb])
    cnt[0] += n


def build_program(layers=(), dbg=()):
    nc = bass.Bass("TRN2", target_bir_lowering=False)
    cx = Ctx(nc)
    P = cx.P

    def dram(name, shape, dtype, kind="Internal"):
        if name in dbg:
            kind = "ExternalOutput"
        return nc.dram_tensor(name, list(shape), dtype, kind=kind).ap()

    x_ap = dram("x", [T, D], F32, "ExternalInput")
    out_ap = dram("out", [T, D], F32, "ExternalOutput")
    ident_ap = dram("ident", [128, 128], F32, "ExternalInput")
    fin_g_ap = dram("final_norm", [128, D], F32, "ExternalInput")
    xT_ap = dram("xT", [D, T], F32)

    ident = cx.sb([128, 128], F32, "ident")
    ident_b = Buf()
    P.dma("sync", ident[:], ident_ap[:, :], writes=[ident_b])

    xT_b = [[Buf() for _ in range(8)] for _ in range(NCH)]
    consts = {}
    ones_bf = cx.sb([128, 128], BF16, "ones_bf")
    ones_bf_b = Buf()
    P.op("vector", lambda e: e.memset(ones_bf[:], 1.0), writes=[ones_bf_b])
    consts["ones_bf"] = (ones_bf, ones_bf_b)
    phase_transpose_in(cx, x_ap, xT_ap, xT_b, (ident, ident_b))
    P.barrier()
    blk_bf = cx.sb([128, 128], BF16, "blk_bf")
    blk_b = Buf()
    P.op("vector", lambda e: e.memset(blk_bf[:], 0.0), writes=[blk_b])
    P.op("vector", lambda e: e.memset(blk_bf[0:64, 0:64], 1.0), writes=[blk_b])
    P.op("vector", lambda e: e.memset(blk_bf[64:128, 64:128], 1.0), writes=[blk_b])
    consts["blk_bf"] = (blk_bf, blk_b)
    PAIRS = [[0, 1], [2, 3], [4, 5], [6, 7]]
    for l in layers:
        pending_op = None
        if "even" in l:
            li = l["even"]
            aps = {"norm": dram("ev_norm%d" % li, [128, 8], F32, "ExternalInput"),
                   "w": dram("ev_w%d" % li, [D, 3008], F32, "ExternalInput"),
                   "wuq": dram("ev_wuq%d" % li, [384, 1024], F32, "ExternalInput"),
                   "wuk": dram("ev_wuk%d" % li, [256, 512], F32, "ExternalInput"),
                   "wuv": dram("ev_wuv%d" % li, [256, 512], F32, "ExternalInput"),
                   "lgain": dram("ev_lgain%d" % li, [128, 5], F32, "ExternalInput")}
            for nm, shp, dt_ in (("ctab", [32, T], F32), ("stab", [32, T], F32), ("halo_valid", [128, 2], F32)):
                if "ev_" + nm not in consts:
                    consts["ev_" + nm] = dram("ev_" + nm, shp, dt_, "ExternalInput")
                aps[nm] = consts["ev_" + nm]
            if "ev_dmask" not in consts:
                consts["ev_dmask"] = [dram("ev_dmask%d" % hd, [128, DIL_W[DIL[hd // 4]]], BF16, "ExternalInput") for hd in range(12)]
            aps["dmask"] = consts["ev_dmask"]
            wo_ap = dram("ev_wo%d" % li, [768, D], F32, "ExternalInput")
            scr = {"lat": dram("lat%d" % li, [256, T], BF16), "lat_b": Buf(),
                   "lat_all": dram("lat_all%d" % li, [512, T], BF16), "lat_all_b": Buf(),
                   "kr": dram("kr%d" % li, [256, T], BF16), "kr_b": Buf(),
                   "kr_all": dram("kr_all%d" % li, [512, T], BF16), "kr_all_b": Buf(),
                   "mq": dram("mq%d" % li, [768, T], BF16), "mq_b": [Buf() for _ in range(8)],
                   "dq": dram("dq%d" % li, [768, T], BF16), "dq_b": Buf(),
                   "dk": [dram("dk%d_%d" % (li, g_), [256, T], BF16) for g_ in range(3)], "dk_b": Buf(),
                   "dv": [dram("dv%d_%d" % (li, g_), [T, 256], BF16) for g_ in range(3)], "dv_b": Buf(),
                   "dk_all": [dram("dk_all%d_%d" % (li, g_), [512, T], BF16) for g_ in range(3)], "dk_all_b": Buf(),
                   "dv_all": [dram("dv_all%d_%d" % (li, g_), [2 * T, 256], BF16) for g_ in range(3)], "dv_all_b": Buf(),
                   "oT": dram("eoT%d" % li, [768, T], BF16), "oT_b": [Buf() for _ in range(NCH)]}
            upto = l.get("upto", 5)
            phase_even_proj(cx, xT_ap, xT_b, aps, consts, scr)
            P.collective("AllGather", [scr["lat"][:, :]], [scr["lat_all"][:, :]], PAIRS, reads=[scr["lat_b"]], writes=[scr["lat_all_b"]])
            P.collective("AllGather", [scr["kr"][:, :]], [scr["kr_all"][:, :]], PAIRS, reads=[scr["kr_b"]], writes=[scr["kr_all_b"]])

            def dil_cc(scr=scr):
                for g_ in range(3):
                    P.collective("AllGather", [scr["dk"][g_][:, :]], [scr["dk_all"][g_][:, :]], PAIRS, reads=[scr["dk_b"]], writes=[scr["dk_all_b"]])
                    P.collective("AllGather", [scr["dv"][g_][:, :]], [scr["dv_all"][g_][:, :]], PAIRS, reads=[scr["dv_b"]], writes=[scr["dv_all_b"]])
            P.barrier()
            if upto >= 3:
                phase_mla_attn(cx, aps, consts, scr, after_setup=dil_cc)
                P.barrier()
            else:
                dil_cc()
                P.barrier()
            if upto >= 4:
                phase_dil_attn(cx, aps, consts, scr)
                P.barrier()
            if upto >= 5:
                if "ffn" in l:
                    pending_op = (wo_ap, 6, scr)
                else:
                    phase_out_proj(cx, xT_ap, xT_b, wo_ap, 6, scr)
                    P.barrier()
        if "gqa" in l:
            li = l["gqa"]
            aps = {"norm": dram("gqa_norm%d" % li, [128, 8], F32, "ExternalInput"),
                   "w": dram("gqa_w%d" % li, [D, 2816], F32, "ExternalInput"),
                   "hgain": dram("gqa_hgain%d" % li, [128, 4], F32, "ExternalInput"),
                   "ctab": dram("gqa_ctab", [128, T], F32, "ExternalInput") if "gqa_ctab" not in consts else consts["gqa_ctab"],
                   "stab": dram("gqa_stab", [128, T], F32, "ExternalInput") if "gqa_stab" not in consts else consts["gqa_stab"]}
            consts["gqa_ctab"], consts["gqa_stab"] = aps["ctab"], aps["stab"]
            wo_ap = dram("gqa_wo%d" % li, [D, D], F32, "ExternalInput")
            scr = {"qT": dram("qT%d" % li, [D, T], BF16), "qT_b": [Buf() for _ in range(8)],
                   "kT": dram("kT%d" % li, [256, T], BF16), "kT_b": Buf(),
                   "v": dram("v%d" % li, [T, 256], BF16), "v_b": Buf(),
                   "kT_all": dram("kT_all%d" % li, [512, T], BF16), "kT_all_b": Buf(),
                   "v_all": dram("v_all%d" % li, [2 * T, 256], BF16), "v_all_b": Buf(),
                   "oT": dram("oT%d" % li, [D, T], BF16), "oT_b": [Buf() for _ in range(NCH)]}
            upto = l.get("upto", 4)
            phase_gqa_proj(cx, xT_ap, xT_b, aps, consts, scr)
            if upto >= 2:
                P.collective("AllGather", [scr["kT"][:, :]], [scr["kT_all"][:, :]], PAIRS,
                             reads=[scr["kT_b"]], writes=[scr["kT_all_b"]])
                P.collective("AllGather", [scr["v"][:, :]], [scr["v_all"][:, :]], PAIRS,
                             reads=[scr["v_b"]], writes=[scr["v_all_b"]])
            P.barrier()
            if upto >= 3:
                phase_gqa_attn(cx, consts, scr)
                P.barrier()
            if upto >= 4:
                if "ffn" in l:
                    pending_op = (wo_ap, 8, scr)
                else:
                    phase_out_proj(cx, xT_ap, xT_b, wo_ap, 8, scr)
                    P.barrier()
        if "ffn" in l:
            li = l["ffn"]
            w_in_ap = dram("ffn_w_in%d" % li, [D, 2 * FFN_H], F32, "ExternalInput")
            w_out_ap = dram("ffn_w_out%d" % li, [FFN_H, D], F32, "ExternalInput")
            g_ap = dram("ffn_norm%d" % li, [128, 8], F32, "ExternalInput")
            if pending_op is not None:
                phase_outproj_ffn(cx, xT_ap, xT_b, pending_op[0], pending_op[1], pending_op[2], w_in_ap, w_out_ap, g_ap, consts)
            else:
                phase_ffn(cx, xT_ap, xT_b, w_in_ap, w_out_ap, g_ap, consts)
            P.barrier()
    phase_final(cx, xT_ap, xT_b, out_ap, fin_g_ap, (ident, ident_b))

    for q in ("sync", "gpsimd", "scalar"):
        ring = P.dma_ring[q]
        toks = [(id(s), v) for s, v in zip(ring["sems"], ring["vals"]) if v > 0]
        P._wait_dma("gpsimd", toks)
    P.emit()
    cx.stack.close()
    return nc


def gain_cols(g):
    g = np.asarray(g, dtype=np.float32)
    return np.ascontiguousarray(g.reshape(-1, 128).T)


def rope_tabs(pos, dim):
    freqs = 10000.0 ** (-np.arange(0, dim, 2, dtype=np.float32) / dim)
    ang = pos.astype(np.float32)[:, None] * freqs[None, :].astype(np.float32)
    return np.cos(ang).astype(np.float32), np.sin(ang).astype(np.float32)


GQ_ORDER = [hh for pair in zip(GQ_L, GQ_U) for hh in pair]
SWAP64 = np.concatenate([np.arange(16, 32), np.arange(0, 16), np.arange(48, 64), np.arange(32, 48)])


def gqa_host(inputs, li, h):
    wq = inputs["gqa_w_q"][li].astype(np.float32).reshape(D, 16, 64)
    wkv = inputs["gqa_w_kv"][li].astype(np.float32).reshape(D, 2, 4, 64)
    qa = wq[:, GQ_ORDER, :]
    qb = qa[:, :, SWAP64]
    ka = wkv[:, 0]
    kb = ka[:, :, SWAP64]
    v = wkv[:, 1]
    w = np.concatenate([qa.reshape(D, -1), qb.reshape(D, -1), ka.reshape(D, -1), kb.reshape(D, -1), v.reshape(D, -1)], axis=1)
    gq = inputs["gqa_q_norm"][li].astype(np.float32)
    gk = inputs["gqa_k_norm"][li].astype(np.float32)
    hg = np.stack([np.tile(gq, 2), np.tile(gq[SWAP64], 2), np.tile(gk, 2), np.tile(gk[SWAP64], 2)], axis=1)
    wo = inputs["gqa_w_o"][li].astype(np.float32).reshape(16, 64, D)[GQ_ORDER].reshape(D, D)
    return np.ascontiguousarray(w), np.ascontiguousarray(hg), np.ascontiguousarray(wo)


def gqa_tabs(h):
    t = np.arange(h * T, (h + 1) * T)
    cr, sr = rope_tabs(t // 64, 32)
    cc, sc = rope_tabs(t % 64, 32)
    C = np.concatenate([cr, cr, cc, cc], axis=1).T
    S = np.concatenate([-sr, sr, -sc, sc], axis=1).T
    return np.ascontiguousarray(np.tile(C, (2, 1))), np.ascontiguousarray(np.tile(S, (2, 1)))


SWAP32 = np.concatenate([np.arange(16, 32), np.arange(0, 16)])


def even_host(inputs, li):
    w = inputs["w_in_ab"][li].astype(np.float32)
    kr = w[:, 640:672]
    wcat = np.concatenate([w[:, 0:672], kr[:, SWAP32], w[:, 672:]], axis=1)
    uq = inputs["mla_w_uq"][li].astype(np.float32)
    wuq = np.concatenate([uq.reshape(384, 768), uq[:, :, 64 + SWAP32].reshape(384, 256)], axis=1)
    ukv = inputs["mla_w_ukv"][li].astype(np.float32)
    wuk = ukv[:, :, 0:64].reshape(256, 512)
    wuv = ukv[:, :, 64:128].reshape(256, 512)
    lg = np.concatenate([gain_cols(inputs["mla_q_norm"][li]), gain_cols(inputs["mla_kv_norm"][li])], axis=1)
    return [np.ascontiguousarray(a) for a in (wcat, wuq, wuk, wuv, lg)]


def even_consts(h):
    t = np.arange(h * T, (h + 1) * T)
    c, s_ = rope_tabs(t, 32)
    C = np.concatenate([c, c], axis=1).T
    S = np.concatenate([-s_, s_], axis=1).T
    hv = np.zeros((128, 2), np.float32)
    hv[:, 0] = 1.0 if h == 1 else 0.0
    hv[:, 1] = 1.0 if h == 0 else 0.0
    slopes = np.exp2(-8.0 * np.arange(1, 13, dtype=np.float32) / 12).astype(np.float32)
    masks = []
    for hd in range(12):
        d = DIL[hd // 4]
        p = np.arange(128)[:, None]
        j = np.arange(DIL_W[d])[None, :]
        delta = p - j + DIL_J0[d]
        ok = (delta % d == 0) & (np.abs(delta) <= 64 * d)
        mval = np.where(ok, np.exp(-slopes[hd] * np.abs(delta).astype(np.float32)), 0.0).astype(np.float32)
        masks.append(mval.astype(ml_dtypes.bfloat16))
    return np.ascontiguousarray(C), np.ascontiguousarray(S), hv, masks


def make_inputs_for_core(c, inputs, layers=()):
    b, h = c // 2, c % 2
    m = {}
    for l in layers:
        if "even" in l:
            li = l["even"]
            wcat, wuq, wuk, wuv, lg = even_host(inputs, li)
            m["ev_w%d" % li], m["ev_wuq%d" % li], m["ev_wuk%d" % li], m["ev_wuv%d" % li], m["ev_lgain%d" % li] = wcat, wuq, wuk, wuv, lg
            m["ev_norm%d" % li] = gain_cols(inputs["mix_norm_ab"][li])
            m["ev_wo%d" % li] = np.ascontiguousarray(inputs["w_out_ab"][li], dtype=np.float32)
            C, S, hv, masks = even_consts(h)
            m["ev_ctab"], m["ev_stab"], m["ev_halo_valid"] = C, S, hv
            for hd in range(12):
                m["ev_dmask%d" % hd] = masks[hd]
        if "gqa" in l:
            li = l["gqa"]
            w, hg, wo = gqa_host(inputs, li, h)
            m["gqa_w%d" % li] = w
            m["gqa_hgain%d" % li] = hg
            m["gqa_wo%d" % li] = wo
            m["gqa_norm%d" % li] = gain_cols(inputs["mix_norm_c"][li])
            m["gqa_ctab"], m["gqa_stab"] = gqa_tabs(h)
        if "ffn" in l:
            li = l["ffn"]
            m["ffn_w_in%d" % li] = np.ascontiguousarray(inputs["ffn_w_in"][li], dtype=np.float32)
            m["ffn_w_out%d" % li] = np.ascontiguousarray(inputs["ffn_w_out"][li], dtype=np.float32)
            m["ffn_norm%d" % li] = gain_cols(inputs["ffn_norm"][li])
    m["x"] = np.ascontiguousarray(inputs["x"][b, h * T:(h + 1) * T, :], dtype=np.float32)
    m["ident"] = np.eye(128, dtype=np.float32)
    m["final_norm"] = np.ascontiguousarray(np.broadcast_to(inputs["final_norm"].astype(np.float32).reshape(1, D), (128, D)))
    return m


LAYERS = ({"even": 0, "ffn": 0}, {"gqa": 0, "ffn": 1}, {"even": 1, "ffn": 2}, {"gqa": 1, "ffn": 3})


def kernel(**inputs):
    inputs = {k: np.asarray(v) for k, v in inputs.items()}
    nc = build_program(layers=LAYERS)
    in_maps = [make_inputs_for_core(c, inputs, LAYERS) for c in range(N_CORES)]
    res = run_bass_kernel_spmd(nc, in_maps, core_ids=list(range(N_CORES)))
    out = np.empty((BATCH, SEQ, D), dtype=np.float32)
    for c in range(N_CORES):
        b, h = c // 2, c % 2
        out[b, h * T:(h + 1) * T, :] = np.asarray(res.results[c]["out"])
    return out
```

```python
import contextlib
import numpy as np
import ml_dtypes
import concourse.bass as bass
import concourse.mybir as mybir
from concourse.bass_utils import run_bass_kernel_spmd

F32 = mybir.dt.float32
BF16 = mybir.dt.bfloat16
ALU = mybir.AluOpType
ACTF = mybir.ActivationFunctionType

D = 1024
BATCH = 4
SEQ = 8192
T = 4096
NCH = T // 512
EPS = 1e-6
FFN_H = 2816
N_CORES = 8
N_DUMMY = 0
SG = 3
VW = 272


class Buf:
    __slots__ = ("w", "r", "name")

    def __init__(self, name=""):
        self.w = None
        self.r = {}
        self.name = name


class Prog:
    ENGS = ("sync", "scalar", "vector", "tensor", "gpsimd")

    def __init__(self, nc, n_dma_sems=24):
        self.nc = nc
        self.lists = {e: [] for e in self.ENGS}
        self.sem = {e: nc.alloc_semaphore(name="c_" + e) for e in self.ENGS}
        self.cnt = {e: 0 for e in self.ENGS}
        self.waited = {e: {} for e in self.ENGS}
        self.semobj = {}
        for e in self.ENGS:
            self.semobj[id(self.sem[e])] = self.sem[e]
        self.dma_ring = {}
        for q in ("sync", "gpsimd", "scalar"):
            sems = [nc.alloc_semaphore(name="d_%s_%d" % (q, i)) for i in range(n_dma_sems)]
            for s in sems:
                self.semobj[id(s)] = s
            self.dma_ring[q] = {"sems": sems, "vals": [0] * n_dma_sems, "i": 0}
        self.cc_sem = nc.alloc_semaphore(name="c_collective")
        self.semobj[id(self.cc_sem)] = self.cc_sem
        self.cc_cnt = 0
        self.n_ops = 0

    def _wait(self, eng, toks):
        need = {}
        for (sid, val) in toks:
            if need.get(sid, 0) < val:
                need[sid] = val
        for sid, val in need.items():
            if sid == id(self.sem[eng]) and eng == "tensor":
                continue
            if self.waited[eng].get(sid, 0) >= val:
                continue
            self.waited[eng][sid] = val
            self.lists[eng].append(("wait", self.semobj[sid], val))

    def _deps(self, reads, writes):
        toks = []
        for b in reads:
            if b.w is not None:
                toks.append(b.w)
        for b in writes:
            if b.w is not None:
                toks.append(b.w)
            toks.extend(b.r.items())
        return toks

    def _commit(self, tok, reads, writes):
        for b in reads:
            if b.r.get(tok[0], 0) < tok[1]:
                b.r[tok[0]] = tok[1]
        for b in writes:
            b.w = tok
            b.r = {}

    def op(self, eng, emit, reads=(), writes=()):
        self._wait(eng, self._deps(reads, writes))
        self.cnt[eng] += 1
        tok = (id(self.sem[eng]), self.cnt[eng])
        self.lists[eng].append(("op", emit, self.sem[eng], 1))
        self._commit(tok, reads, writes)
        self.n_ops += 1
        return tok

    def dma(self, q, out, in_, reads=(), writes=(), **kw):
        ring = self.dma_ring[q]
        k = ring["i"] % len(ring["sems"])
        ring["i"] += 1
        s = ring["sems"][k]
        toks = self._deps(reads, writes)
        if ring["vals"][k] > 0:
            toks.append((id(s), ring["vals"][k]))
        self._wait_dma(q, toks)
        ring["vals"][k] += 16
        tok = (id(s), ring["vals"][k])
        self.lists[q].append(("op", lambda e: e.dma_start(out=out, in_=in_, **kw), s, 16))
        self._commit(tok, reads, writes)
        self.n_ops += 1
        return tok

    def _wait_dma(self, q, toks):
        need = {}
        for (sid, val) in toks:
            if need.get(sid, 0) < val:
                need[sid] = val
        for sid, val in need.items():
            if self.waited[q].get(sid, 0) >= val:
                continue
            self.waited[q][sid] = val
            self.lists[q].append(("wait", self.semobj[sid], val))

    def collective(self, kind, ins, outs, groups, reads=(), writes=()):
        q = "gpsimd"
        self._wait_dma(q, self._deps(reads, writes))
        self.cc_cnt += 1
        tok = (id(self.cc_sem), self.cc_cnt)
        self.lists[q].append(("op", lambda e: e.collective_compute(
            kind, ALU.bypass, replica_groups=groups, ins=ins, outs=outs), self.cc_sem, 1))
        self._commit(tok, reads, writes)
        return tok

    def barrier(self):
        toks = [(id(self.sem[e]), self.cnt[e]) for e in self.ENGS if self.cnt[e] > 0]
        if self.cc_cnt > 0:
            toks.append((id(self.cc_sem), self.cc_cnt))
        for q, ring in self.dma_ring.items():
            toks += [(id(s), v) for s, v in zip(ring["sems"], ring["vals"]) if v > 0]
        for e in self.ENGS:
            self._wait_dma(e, toks)

    def wait_all(self, eng, bufs):
        toks = []
        for b in bufs:
            if b.w is not None:
                toks.append(b.w)
            toks.extend(b.r.items())
        self._wait_dma(eng, toks)

    def emit(self):
        nc = self.nc
        with nc.Block() as block:
            def runner(name):
                def body(e):
                    for it in self.lists[name]:
                        if it[0] == "wait":
                            e.wait_ge(it[1], it[2])
                        else:
                            ins = it[1](e)
                            ins.then_inc(it[2], it[3])
                return body
            block.sync(runner("sync"))
            block.scalar(runner("scalar"))
            block.vector(runner("vector"))
            block.tensor(runner("tensor"))
            block.gpsimd(runner("gpsimd"))


class Ctx:
    def __init__(self, nc):
        self.nc = nc
        self.P = Prog(nc)
        self.stack = contextlib.ExitStack()
        self.banks = []
        self.bank_bufs = []
        self.trip = []
        for i in range(2):
            t = self.stack.enter_context(nc.psum_tensor("trip%d" % i, [128, SG * 512], F32))
            self.trip.append(t)
            for hf in range(SG):
                self.banks.append(t[:, hf * 512:(hf + 1) * 512])
                self.bank_bufs.append(Buf("bank%d" % (SG * i + hf)))
        for i in range(2):
            t = self.stack.enter_context(nc.psum_tensor("accb%d" % i, [128, 512], F32))
            self.banks.append(t[:, :])
            self.bank_bufs.append(Buf("bank%d" % (2 * SG + i)))
        self.uid = 0

    def sb(self, shape, dtype, name=None, stack=None):
        self.uid += 1
        nm = "%s_%d" % (name or "t", self.uid)
        return (stack or self.stack).enter_context(self.nc.sbuf_tensor(nm, list(shape), dtype))


def phase_transpose_in(cx, x_ap, xT_ap, xT_b, ident_f32):
    P, nc = cx.P, cx.nc
    with contextlib.ExitStack() as st:
        NB = 2
        xin = [cx.sb([128, D], F32, "xin", st) for _ in range(NB)]
        xin_b = [Buf() for _ in range(NB)]
        stg = [cx.sb([128, 8, 512], F32, "xTstg", st) for _ in range(2)]
        stg_b = [Buf() for _ in range(2)]
        ident, ident_b = ident_f32
        for tt in range(T // 128):
            c, j = tt // 4, tt % 4
            xi, xb = xin[tt % NB], xin_b[tt % NB]
            P.dma("sync", xi[:], x_ap[tt * 128:(tt + 1) * 128, :], writes=[xb])
            sg, sgb = stg[c % 2], stg_b[c % 2]
            for half in range(2):
                bank = (tt * 2 + half) % 8
                pb, pbb = cx.banks[bank], cx.bank_bufs[bank]
                for k in range(4):
                    dc = half * 4 + k
                    P.op("tensor", lambda e, pb=pb, xi=xi, dc=dc, k=k: e.transpose(
                        pb[:, k * 128:(k + 1) * 128], xi[:, dc * 128:(dc + 1) * 128], ident[:]),
                        reads=[xb, ident_b], writes=[pbb])
                eng = "vector" if half == 0 else "scalar"
                src = pb[:].rearrange("p (k t) -> p k t", k=4)
                dst = sg[:, half * 4:(half + 1) * 4, j * 128:(j + 1) * 128]
                if eng == "vector":
                    P.op("vector", lambda e, dst=dst, src=src: e.tensor_copy(out=dst, in_=src),
                         reads=[pbb], writes=[sgb])
                else:
                    P.op("scalar", lambda e, dst=dst, src=src: e.activation(out=dst, in_=src, func=ACTF.Copy),
                         reads=[pbb], writes=[sgb])
            if j == 3:
                P.dma("gpsimd", xT_ap.rearrange("(k p) t -> p k t", p=128)[:, :, c * 512:(c + 1) * 512],
                      sg[:], reads=[sgb], writes=xT_b[c])


def phase_final(cx, xT_ap, xT_b, out_ap, g_bc, ident_f32):
    P, nc = cx.P, cx.nc
    ident, ident_b = ident_f32
    with contextlib.ExitStack() as st:
        g_t = cx.sb([128, D], F32, "gfin", st)
        g_b = Buf()
        P.dma("sync", g_t[:], g_bc[:, :], writes=[g_b])
        xs = [cx.sb([128, 8, 512], F32, "fx", st) for _ in range(2)]
        xs_b = [Buf() for _ in range(2)]
        ot = [cx.sb([128, D], F32, "fo", st) for _ in range(2)]
        ot_b = [Buf() for _ in range(2)]
        junk = cx.sb([128, 512], F32, "fjunk", st)
        junk_b = Buf()
        ss = [cx.sb([128, 2], F32, "fss", st) for _ in range(2)]
        ss_b = [Buf() for _ in range(2)]
        rs = [cx.sb([128, 1], F32, "frs", st) for _ in range(2)]
        rs_b = [Buf() for _ in range(2)]
        for c in range(NCH):
            xc, xcb = xs[c % 2], xs_b[c % 2]
            P.dma("sync", xc[:], xT_ap.rearrange("(k p) t -> p k t", p=128)[:, :, c * 512:(c + 1) * 512],
                  reads=xT_b[c], writes=[xcb])
            for j in range(4):
                tt = c * 4 + j
                o, ob = ot[tt % 2], ot_b[tt % 2]
                s_, sb_ = ss[tt % 2], ss_b[tt % 2]
                r_, rb_ = rs[tt % 2], rs_b[tt % 2]
                bk = [(tt * 2) % 8, (tt * 2 + 1) % 8]
                for half in range(2):
                    pb, pbb = cx.banks[bk[half]], cx.bank_bufs[bk[half]]
                    for k in range(4):
                        dc = half * 4 + k
                        P.op("tensor", lambda e, pb=pb, xc=xc, dc=dc, k=k, j=j: e.transpose(
                            pb[:, k * 128:(k + 1) * 128], xc[:, dc, j * 128:(j + 1) * 128], ident[:]),
                            reads=[xcb, ident_b], writes=[pbb])
                    P.op("scalar", lambda e, pb=pb, s_=s_, half=half: e.activation(
                        out=junk[:], in_=pb[:], func=ACTF.Square, accum_out=s_[:, half:half + 1]),
                        reads=[pbb], writes=[junk_b, sb_])
                P.op("vector", lambda e, s_=s_, r_=r_: e.tensor_tensor(
                    out=r_[:], in0=s_[:, 0:1], in1=s_[:, 1:2], op=ALU.add), reads=[sb_], writes=[rb_])
                P.op("scalar", lambda e, r_=r_: e.activation(
                    out=r_[:], in_=r_[:], func=ACTF.Sqrt, scale=1.0 / D, bias=EPS), reads=[rb_], writes=[rb_])
                P.op("vector", lambda e, r_=r_: e.reciprocal(out=r_[:], in_=r_[:]), reads=[rb_], writes=[rb_])
                for half in range(2):
                    pb, pbb = cx.banks[bk[half]], cx.bank_bufs[bk[half]]
                    P.op("vector", lambda e, pb=pb, o=o, r_=r_, half=half: e.scalar_tensor_tensor(
                        out=o[:, half * 512:(half + 1) * 512], in0=pb[:], scalar=r_[:, 0:1],
                        in1=g_t[:, half * 512:(half + 1) * 512], op0=ALU.mult, op1=ALU.mult),
                        reads=[pbb, rb_, g_b], writes=[ob])
                P.dma("gpsimd", out_ap[tt * 128:(tt + 1) * 128, :], o[:], reads=[ob])


def load_w(cx, st, wdst, wbufs, w_ap, K, N, row_gain=None, col0=0, piece=1024):
    for _ in load_w_iter(cx, wdst, wbufs, w_ap, K, N, col0, piece):
        pass


def load_w_iter(cx, wdst, wbufs, w_ap, K, N, col0=0, piece=1024):
    P = cx.P
    KC = K // 128
    for kc in range(KC):
        P.dma("gpsimd", wdst[:, kc, col0:col0 + N], w_ap[kc * 128:(kc + 1) * 128, :], writes=[wbufs[kc % 2]])
        yield


def rms_chunk(cx, xc, xcb, KC, xn, xnb, sqs, sqs_b, ones_bf, statbank, rstd, rstd_b, Dn, ncol=512, gain=None):
    P = cx.P
    ones, ones_b = ones_bf
    pb, pbb = cx.banks[statbank], cx.bank_bufs[statbank]
    for k in range(KC):
        sq, sqb = sqs[k % len(sqs)], sqs_b[k % len(sqs)]
        P.op("scalar", lambda e, sq=sq, k=k: e.activation(out=sq[:, 0:ncol], in_=xc[:, k, 0:ncol], func=ACTF.Square),
             reads=[xcb], writes=[sqb])
        P.op("tensor", lambda e, sq=sq, k=k: e.matmul(pb[:, 0:ncol], lhsT=ones[:, :], rhs=sq[:, 0:ncol],
                                                       start=(k == 0), stop=(k == KC - 1)),
             reads=[sqb, ones_b], writes=[pbb])
    P.op("scalar", lambda e: e.activation(out=rstd[:, 0:ncol], in_=pb[:, 0:ncol], func=ACTF.Sqrt, scale=1.0 / Dn, bias=EPS),
         reads=[pbb], writes=[rstd_b])
    P.op("vector", lambda e: e.reciprocal(out=rstd[:, 0:ncol], in_=rstd[:, 0:ncol]), reads=[rstd_b], writes=[rstd_b])
    g, gb = gain
    for k in range(KC):
        P.op("vector", lambda e, k=k: e.scalar_tensor_tensor(
            out=xn[:, k, 0:ncol], in0=xc[:, k, 0:ncol], scalar=g[:, k:k + 1], in1=rstd[:, 0:ncol], op0=ALU.mult, op1=ALU.mult),
            reads=[xcb, rstd_b, gb], writes=[xnb[k % 2]])


def xT_view(xT_ap, c):
    return xT_ap.rearrange("(k p) t -> p k t", p=128)[:, :, c * 512:(c + 1) * 512]


def ffn_weights(cx, st, w_in_ap, w_out_ap, g_ap):
    P = cx.P
    HC = FFN_H // 128
    w1 = cx.sb([128, 8, 2 * FFN_H], BF16, "w1", st)
    w2 = cx.sb([128, HC, D], BF16, "w2", st)
    w1b, w2b = [Buf(), Buf()], [Buf(), Buf()]
    g = cx.sb([128, 8], F32, "ffg", st)
    gb = Buf()
    P.dma("sync", g[:], g_ap, writes=[gb])

    def gen():
        yield from load_w_iter(cx, w1, w1b, w_in_ap, D, 2 * FFN_H)
        yield from load_w_iter(cx, w2, w2b, w_out_ap, FFN_H, D)
    return (w1, w2, w1b, w2b, g, gb), gen()


def phase_outproj_ffn(cx, xT_ap, xT_b, wo_ap, KC, scr, w_in_ap, w_out_ap, g_ap, consts):
    P = cx.P
    with contextlib.ExitStack() as st:
        pre, gen = ffn_weights(cx, st, w_in_ap, w_out_ap, g_ap)
        phase_out_proj(cx, xT_ap, xT_b, wo_ap, KC, scr, bg=gen)
        for _ in gen:
            pass
        P.barrier()
        phase_ffn(cx, xT_ap, xT_b, w_in_ap, w_out_ap, g_ap, consts, pre=(st, pre))


def phase_ffn(cx, xT_ap, xT_b, w_in_ap, w_out_ap, g_ap, consts, pre=None):
    P, nc = cx.P, cx.nc
    ones_bf = consts["ones_bf"]
    HC = FFN_H // 128
    with contextlib.ExitStack() as st_own:
        if pre is None:
            st = st_own
            (w1, w2, w1b, w2b, g, gb), gen = ffn_weights(cx, st, w_in_ap, w_out_ap, g_ap)
            for _ in gen:
                pass
        else:
            st, (w1, w2, w1b, w2b, g, gb) = pre
            st = st_own
        xs = [cx.sb([128, 8, 512], F32, "fx", st)]
        xs_b = [Buf()]
        xr = [cx.sb([128, 512], F32, "fxr", st) for _ in range(2)]
        xr_b = [Buf(), Buf()]
        xns = [cx.sb([128, 8, 512], BF16, "fxn", st) for _ in range(2)]
        xnbs = [[Buf(), Buf()], [Buf(), Buf()]]
        act = cx.sb([128, HC, 512], BF16, "fact", st)
        act_b = [Buf() for _ in range(HC)]
        sqs = [cx.sb([128, 512], BF16, "fsq", st) for _ in range(2)]
        sqs_b = [Buf(), Buf()]
        sg = [cx.sb([128, 512], BF16, "fsg", st) for _ in range(2)]
        sg_b = [Buf(), Buf()]
        rstd = cx.sb([128, 512], F32, "frstd", st)
        rstd_b = Buf()

        def load(c):
            P.dma("sync", xs[0][:], xT_view(xT_ap, c), reads=xT_b[c], writes=[xs_b[0]])

        def norm(c):
            rms_chunk(cx, xs[0], xs_b[0], 8, xns[c % 2], xnbs[c % 2], sqs, sqs_b, ones_bf, 6, rstd, rstd_b, D, gain=(g, gb))

        load(0)
        norm(0)
        for c in range(NCH):
            xn, xnb = xns[c % 2], xnbs[c % 2]
            if c + 1 < NCH:
                load(c + 1)
            for j in range(HC):
                if j == 6 and c + 1 < NCH:
                    norm(c + 1)
                ga, gab = cx.banks[j % 2], cx.bank_bufs[j % 2]
                ua, uab = cx.banks[2 + j % 2], cx.bank_bufs[2 + j % 2]
                for k in range(8):
                    P.op("tensor", lambda e, ga=ga, j=j, k=k, xn=xn: e.matmul(
                        ga[:, :], lhsT=w1[:, k, j * 128:(j + 1) * 128], rhs=xn[:, k, :], start=(k == 0), stop=(k == 7)),
                        reads=xnb + w1b, writes=[gab])
                for k in range(8):
                    P.op("tensor", lambda e, ua=ua, j=j, k=k, xn=xn: e.matmul(
                        ua[:, :], lhsT=w1[:, k, FFN_H + j * 128:FFN_H + (j + 1) * 128], rhs=xn[:, k, :],
                        start=(k == 0), stop=(k == 7)), reads=xnb + w1b, writes=[uab])
                s_, sb_ = sg[j % 2], sg_b[j % 2]
                P.op("scalar", lambda e, s_=s_, ga=ga: e.activation(out=s_[:], in_=ga[:, :], func=ACTF.Silu),
                     reads=[gab], writes=[sb_])
                P.op("vector", lambda e, s_=s_, ua=ua, j=j: e.tensor_tensor(
                    out=act[:, j, :], in0=ua[:, :], in1=s_[:], op=ALU.mult), reads=[uab, sb_], writes=[act_b[j]])
            for dc in range(8):
                ya, yab = cx.banks[4 + dc % 2], cx.bank_bufs[4 + dc % 2]
                xr_, xrb_ = xr[dc % 2], xr_b[dc % 2]
                xsl = xT_ap[dc * 128:(dc + 1) * 128, c * 512:(c + 1) * 512]
                P.dma("sync", xr_[:], xsl, reads=[xT_b[c][dc]], writes=[xrb_])
                for j in range(HC):
                    P.op("tensor", lambda e, ya=ya, j=j, dc=dc: e.matmul(
                        ya[:, :], lhsT=w2[:, j, dc * 128:(dc + 1) * 128], rhs=act[:, j, :],
                        start=(j == 0), stop=(j == HC - 1)), reads=[act_b[j]] + w2b, writes=[yab])
                P.op("vector", lambda e, ya=ya, xr_=xr_: e.tensor_tensor(
                    out=xr_[:], in0=ya[:, :], in1=xr_[:], op=ALU.add), reads=[yab], writes=[xrb_])
                P.dma("gpsimd", xsl, xr_[:], reads=[xrb_], writes=[xT_b[c][dc]])


GQ_L = [0, 1, 2, 3, 8, 9, 10, 11]
GQ_U = [4, 5, 6, 7, 12, 13, 14, 15]


class AttnStream:
    PV_LAG = 2

    def __init__(self, cx, st_bufs, pv_lag=2):
        self.cx = cx
        self.b = st_bufs
        self.PV_LAG = pv_lag
        assert len(st_bufs["pt"]) >= pv_lag + 1
        self.pending = []
        self.gidx = 0
        self.deferred = []
        self.mask_i = 0

    def block(self, q_rhs, q_bufs, tiles, scale, vcols, acc_bank, finish=None):
        cx, P, b = self.cx, self.cx.P, self.b
        pts, pts_b = b["pt"], b["pt_b"]
        n = len(tiles)
        groups = [list(range(i, min(i + SG, n))) for i in range(0, n, SG)]
        for gi, grp in enumerate(groups):
            u = b["cnt"][0]
            b["cnt"][0] += 1
            slot = u % 2
            trip = cx.trip[slot]
            pi = self.gidx % len(pts)
            self.gidx += 1
            pt, ptb = pts[pi], pts_b[pi]
            bbs = []
            for ii, ti in enumerate(grp):
                lhsT, v_ap, rb, mask, q_ap = tiles[ti]
                rhs = q_rhs if q_ap is None else q_ap
                sbb_ = cx.bank_bufs[SG * slot + ii]
                bbs.append(sbb_)
                P.op("tensor", lambda e, trip=trip, ii=ii, lhsT=lhsT, rhs=rhs: e.matmul(
                    trip[:, ii * 512:(ii + 1) * 512], lhsT=lhsT, rhs=rhs, start=True, stop=True),
                    reads=list(rb) + list(q_bufs), writes=[sbb_])
            w = 512 * len(grp)
            P.op("scalar", lambda e, trip=trip, pt=pt, w=w: e.activation(
                out=pt[:, 0:w], in_=trip[:, 0:w], func=ACTF.Exp, scale=scale), reads=bbs, writes=[ptb])
            for ii, ti in enumerate(grp):
                mask = tiles[ti][3]
                if mask is not None:
                    self.mask_i += 1
                    P.op("gpsimd" if self.mask_i % 4 == 0 else "vector", lambda e, pt=pt, mask=mask, ii=ii: e.tensor_tensor(
                        out=pt[:, ii * 512:(ii + 1) * 512], in0=pt[:, ii * 512:(ii + 1) * 512], in1=mask, op=ALU.mult),
                        reads=[ptb] + b.get("mask_b", []), writes=[ptb])
            items = [(tiles[ti][1], tiles[ti][2], ii, ti == 0, ti == n - 1) for ii, ti in enumerate(grp)]
            self.pending.append((items, pt, ptb, acc_bank, vcols, finish if gi == len(groups) - 1 else None))
            if len(self.pending) > self.PV_LAG:
                self._pv(self.pending.pop(0))
            self._tick()

    def _tick(self):
        for d in self.deferred:
            d[0] -= 1
        while self.deferred and self.deferred[0][0] <= 0:
            self.deferred.pop(0)[1]()

    def _pv(self, entry):
        cx, P = self.cx, self.cx.P
        items, pt, ptb, acc_bank, vcols, finish = entry
        ab, abb = cx.banks[acc_bank], cx.bank_bufs[acc_bank]
        for v_ap, rb, ii, first, last in items:
            P.op("tensor", lambda e, v_ap=v_ap, pt=pt, ii=ii, first=first, last=last, ab=ab, vcols=vcols: e.matmul(
                ab[0:vcols, :], lhsT=v_ap, rhs=pt[:, ii * 512:(ii + 1) * 512], start=first, stop=last),
                reads=list(rb) + [ptb], writes=[abb])
        if finish is not None:
            part2 = finish()
            if part2 is not None:
                self.deferred.append([3, part2])

    def flush(self):
        while self.pending:
            self._pv(self.pending.pop(0))
        while self.deferred:
            self.deferred.pop(0)[1]()


def take_bank(cx, st_bufs):
    u = st_bufs["cnt"][0]
    st_bufs["cnt"][0] += 1
    return SG * (u % 2)


def attn_finish(cx, st_bufs, acc_bank, bc_bank, dst_ap, dst_bufs, consts):
    P = cx.P
    ab, abb = cx.banks[acc_bank], cx.bank_bufs[acc_bank]
    ones, ones_b = consts["ones_bf"]
    k = st_bufs["fin_i"][0]
    st_bufs["fin_i"][0] += 1
    rd, rdb = st_bufs["rd"][k % 2], st_bufs["rd_b"][k % 2]
    rdh, rdhb = st_bufs["rdh"][k % 2], st_bufs["rdh_b"][k % 2]
    bcs, bcsb = st_bufs["bcs"][k % 2], st_bufs["bcs_b"][k % 2]
    ot, otb = st_bufs["ot"][k % 2], st_bufs["ot_b"][k % 2]
    P.op("vector", lambda e: e.reciprocal(out=rd[64:65, :], in_=ab[64:65, :]), reads=[abb], writes=[rdb])
    P.op("vector", lambda e: e.tensor_copy(out=rdh[64:65, :], in_=rd[64:65, :]), reads=[rdb], writes=[rdhb])

    def part2():
        bc_bank = take_bank(cx, st_bufs)
        bb, bbb = cx.banks[bc_bank], cx.bank_bufs[bc_bank]
        P.op("tensor", lambda e: e.matmul(bb[0:64, :], lhsT=ones[64:65, 0:64], rhs=rdh[64:65, :], start=True, stop=True),
             reads=[rdhb, ones_b], writes=[bbb])
        P.op("vector", lambda e: e.tensor_copy(out=bcs[0:64, :], in_=bb[0:64, :]), reads=[bbb], writes=[bcsb])
        P.op("vector", lambda e: e.tensor_tensor(out=ot[0:64, :], in0=ab[0:64, :], in1=bcs[0:64, :], op=ALU.mult),
             reads=[abb, bcsb], writes=[otb])
        P.dma("sync", dst_ap, ot[0:64, :], reads=[otb], writes=dst_bufs)
    return part2


def attn_bufs(cx, st, npt=3):
    b = {}
    b["pt"] = [cx.sb([128, SG * 512], BF16, "pt", st) for _ in range(npt)]
    b["pt_b"] = [Buf() for _ in range(npt)]
    b["cnt"] = [0]
    b["fin_i"] = [0]
    b["rd"] = [cx.sb([128, 512], F32, "rd", st) for _ in range(2)]
    b["rd_b"] = [Buf(), Buf()]
    b["rdh"] = [cx.sb([128, 512], BF16, "rdh", st) for _ in range(2)]
    b["rdh_b"] = [Buf(), Buf()]
    b["bcs"] = [cx.sb([64, 512], F32, "bcs", st) for _ in range(2)]
    b["bcs_b"] = [Buf(), Buf()]
    b["ot"] = [cx.sb([64, 512], BF16, "ot", st) for _ in range(2)]
    b["ot_b"] = [Buf(), Buf()]
    return b


def phase_gqa_proj(cx, xT_ap, xT_b, aps, consts, scr):
    P, nc = cx.P, cx.nc
    ones_bf = consts["ones_bf"]
    blk, blk_b = consts["blk_bf"]
    NW = 2816
    with contextlib.ExitStack() as st:
        w = cx.sb([128, 8, NW], BF16, "gw", st)
        wb = [Buf(), Buf()]
        g = cx.sb([128, 8], F32, "gg", st)
        gb = Buf()
        P.dma("sync", g[:], aps["norm"], writes=[gb])
        load_w(cx, st, w, wb, aps["w"], D, NW, row_gain=(g, gb))
        hg = cx.sb([128, 4], F32, "ghg", st)
        hgb = Buf()
        P.dma("sync", hg[:], aps["hgain"], writes=[hgb])
        ctab = cx.sb([128, T], F32, "ctab", st)
        stab = cx.sb([128, T], F32, "stab", st)
        tab_b = Buf()
        P.dma("sync", ctab[:], aps["ctab"], writes=[tab_b])
        P.dma("scalar", stab[:], aps["stab"], writes=[tab_b])
        xs = cx.sb([128, 8, 512], F32, "gx", st)
        xsb = Buf()
        xns = [cx.sb([128, 8, 512], BF16, "gxn", st) for _ in range(2)]
        xnbs = [[Buf(), Buf()], [Buf(), Buf()]]
        sqs = [cx.sb([128, 512], BF16, "gsq", st) for _ in range(2)]
        sqs_b = [Buf(), Buf()]
        rstd = cx.sb([128, 512], F32, "grstd", st)
        rstd_b = Buf()
        hr = [cx.sb([128, 512], F32, "ghr", st) for _ in range(2)]
        hr_b = [Buf(), Buf()]
        t1 = [cx.sb([128, 512], F32, "gt1", st) for _ in range(2)]
        t1_b = [Buf(), Buf()]
        t2 = [cx.sb([128, 512], F32, "gt2", st) for _ in range(2)]
        t2_b = [Buf(), Buf()]
        qo = [cx.sb([128, 512], BF16, "gqo", st) for _ in range(2)]
        qo_b = [Buf(), Buf()]
        vst = [cx.sb([128, 4, 256], BF16, "gvst", st) for _ in range(2)]
        vst_b = [Buf(), Buf()]
        it = 0

        def xload(c):
            P.dma("sync", xs[:], xT_view(xT_ap, c), reads=xT_b[c], writes=[xsb])

        def xnorm(c):
            rms_chunk(cx, xs, xsb, 8, xns[c % 2], xnbs[c % 2], sqs, sqs_b, ones_bf, 6, rstd, rstd_b, D, gain=(g, gb))

        xload(0)
        xnorm(0)
        for c in range(NCH):
            xn, xnb = xns[c % 2], xnbs[c % 2]
            if c + 1 < NCH:
                xload(c + 1)
            for cc in range(10):
                if cc == 5 and c + 1 < NCH:
                    xnorm(c + 1)
                isq = cc < 8
                ca = cc * 128 if isq else 2048 + (cc - 8) * 128
                cbo = 1024 + cc * 128 if isq else 2304 + (cc - 8) * 128
                gcol = 0 if isq else 2
                A, Ab = cx.banks[it % 2], cx.bank_bufs[it % 2]
                B, Bb = cx.banks[2 + it % 2], cx.bank_bufs[2 + it % 2]
                S, Sb = cx.banks[4 + it % 2], cx.bank_bufs[4 + it % 2]
                for k in range(8):
                    P.op("tensor", lambda e, A=A, k=k, ca=ca, xn=xn: e.matmul(
                        A[:, :], lhsT=w[:, k, ca:ca + 128], rhs=xn[:, k, :], start=(k == 0), stop=(k == 7)),
                        reads=xnb + wb, writes=[Ab])
                for k in range(8):
                    P.op("tensor", lambda e, B=B, k=k, cbo=cbo, xn=xn: e.matmul(
                        B[:, :], lhsT=w[:, k, cbo:cbo + 128], rhs=xn[:, k, :], start=(k == 0), stop=(k == 7)),
                        reads=xnb + wb, writes=[Bb])
                sq, sqb = sqs[it % 2], sqs_b[it % 2]
                P.op("scalar", lambda e, sq=sq, A=A: e.activation(out=sq[:], in_=A[:, :], func=ACTF.Square),
                     reads=[Ab], writes=[sqb])
                P.op("tensor", lambda e, S=S, sq=sq: e.matmul(S[:, :], lhsT=blk[:, :], rhs=sq[:], start=True, stop=True),
                     reads=[sqb, blk_b], writes=[Sb])
                h_, hb_ = hr[it % 2], hr_b[it % 2]
                P.op("scalar", lambda e, h_=h_, S=S: e.activation(
                    out=h_[:], in_=S[:, :], func=ACTF.Sqrt, scale=1.0 / 64, bias=EPS), reads=[Sb], writes=[hb_])
                P.op("vector", lambda e, h_=h_: e.reciprocal(out=h_[:], in_=h_[:]), reads=[hb_], writes=[hb_])
                a_, ab_ = t1[it % 2], t1_b[it % 2]
                b_, bb_ = t2[it % 2], t2_b[it % 2]
                P.op("vector", lambda e, a_=a_, A=A, gcol=gcol, c=c: e.scalar_tensor_tensor(
                    out=a_[:], in0=A[:, :], scalar=hg[:, gcol:gcol + 1], in1=ctab[:, c * 512:(c + 1) * 512],
                    op0=ALU.mult, op1=ALU.mult), reads=[Ab, hgb, tab_b], writes=[ab_])
                P.op("vector", lambda e, b_=b_, B=B, gcol=gcol, c=c: e.scalar_tensor_tensor(
                    out=b_[:], in0=B[:, :], scalar=hg[:, gcol + 1:gcol + 2], in1=stab[:, c * 512:(c + 1) * 512],
                    op0=ALU.mult, op1=ALU.mult), reads=[Bb, hgb, tab_b], writes=[bb_])
                P.op("gpsimd", lambda e, a_=a_, b_=b_: e.tensor_tensor(out=a_[:], in0=a_[:], in1=b_[:], op=ALU.add),
                     reads=[ab_, bb_], writes=[ab_])
                q_, qb_ = qo[it % 2], qo_b[it % 2]
                P.op("gpsimd", lambda e, a_=a_, h_=h_, q_=q_: e.tensor_tensor(out=q_[:], in0=a_[:], in1=h_[:], op=ALU.mult),
                     reads=[ab_, hb_], writes=[qb_])
                if isq:
                    dst = scr["qT"][cc * 128:(cc + 1) * 128, c * 512:(c + 1) * 512]
                    P.dma("sync", dst, q_[:], reads=[qb_], writes=[scr["qT_b"][cc]])
                else:
                    dst = scr["kT"][(cc - 8) * 128:(cc - 7) * 128, c * 512:(c + 1) * 512]
                    P.dma("sync", dst, q_[:], reads=[qb_], writes=[scr["kT_b"]])
                it += 1
            vs, vsb = vst[c % 2], vst_b[c % 2]
            for j in range(4):
                V, Vb = cx.banks[7], cx.bank_bufs[7]
                for k in range(8):
                    P.op("tensor", lambda e, V=V, k=k, j=j, xn=xn: e.matmul(
                        V[:, 0:256], lhsT=xn[:, k, j * 128:(j + 1) * 128], rhs=w[:, k, 2560:2816],
                        start=(k == 0), stop=(k == 7)), reads=xnb + wb, writes=[Vb])
                P.op("scalar", lambda e, V=V, vs=vs, j=j: e.activation(
                    out=vs[:, j, :], in_=V[:, 0:256], func=ACTF.Copy), reads=[Vb], writes=[vsb])
            P.dma("gpsimd", scr["v"][c * 512:(c + 1) * 512, :].rearrange("(j p) f -> p j f", p=128), vs[:],
                  reads=[vsb], writes=[scr["v_b"]])


def phase_gqa_attn(cx, consts, scr):
    P = cx.P
    with contextlib.ExitStack() as st:
        kres = cx.sb([128, 2, SEQ], BF16, "kres", st)
        kres_b = Buf()
        VT = 260
        vflat = cx.sb([128, 64 * VT + 128], BF16, "vres", st)
        vres = vflat[:, 0:64 * VT].rearrange("p (t f) -> p t f", f=VT)
        vres_b = Buf()
        for r in range(2):
            for pr in range(2):
                P.dma("sync" if pr == 0 else "scalar", kres[:, pr, r * T:(r + 1) * T],
                      scr["kT_all"][r * 256 + pr * 128:r * 256 + (pr + 1) * 128, :], reads=[scr["kT_all_b"]], writes=[kres_b])
        P.op("vector", lambda e: e.memset(vflat[:], 1.0), writes=[vres_b])
        for i in range(4):
            for hh in range(4):
                P.dma("sync" if hh % 2 == 0 else "scalar", vres[:, i * 16:(i + 1) * 16, hh * 65:hh * 65 + 64],
                      scr["v_all"][i * 2048:(i + 1) * 2048, hh * 64:(hh + 1) * 64].rearrange("(t p) d -> p t d", p=128),
                      reads=[scr["v_all_b"]], writes=[vres_b])
        qz = [[cx.sb([128, T], BF16, "qz", st) for _ in range(2)] for _ in range(2)]
        qz_b = [[Buf(), Buf()] for _ in range(2)]
        for i in range(2):
            P.op("gpsimd", lambda e, i=i: e.memset(qz[i][0][64:128, :], 0.0), writes=[qz_b[i][0]])
            P.op("gpsimd", lambda e, i=i: e.memset(qz[i][1][0:64, :], 0.0), writes=[qz_b[i][1]])
        ab = attn_bufs(cx, st)
        stream = AttnStream(cx, ab)
        blk_i = 0
        for c in range(8):
            pr = c // 4
            for hf in range(2):
                lo = hf * 64
                P.dma("sync" if hf == 0 else "scalar", qz[c % 2][hf][lo:lo + 64, :], scr["qT"][c * 128 + lo:c * 128 + lo + 64, :],
                      reads=[scr["qT_b"][c]], writes=[qz_b[c % 2][hf]])
            for hf in range(2):
                gkv = 2 * pr + hf
                lo = hf * 64
                qr, qrb = qz[c % 2][hf], qz_b[c % 2][hf]
                for qc in range(NCH):
                    tiles = []
                    for kt in range(64):
                        v0 = kt * VT + gkv * 65
                        tiles.append((kres[:, pr, kt * 128:(kt + 1) * 128], vflat[:, v0:v0 + 128], [kres_b, vres_b], None, None))
                    acc = 6 + blk_i % 2
                    dst = scr["oT"][c * 128 + lo:c * 128 + lo + 64, qc * 512:(qc + 1) * 512]
                    stream.block(qr[:, qc * 512:(qc + 1) * 512], [qrb], tiles, 0.125, 128, acc,
                                 finish=lambda acc=acc, dst=dst, qc=qc: attn_finish(cx, ab, acc, 5, dst, [scr["oT_b"][qc]], consts))
                    blk_i += 1
        stream.flush()


def phase_out_proj(cx, xT_ap, xT_b, w_ap, KC, scr, bg=None):
    P = cx.P
    with contextlib.ExitStack() as st:
        w = cx.sb([128, KC, D], BF16, "wo", st)
        wb = [Buf(), Buf()]
        load_w(cx, st, w, wb, w_ap, KC * 128, D)
        oc = [cx.sb([128, KC, 512], BF16, "oc", st) for _ in range(2)]
        oc_b = [Buf(), Buf()]
        xr = [cx.sb([128, 512], F32, "oxr", st) for _ in range(3)]
        xr_b = [Buf() for _ in range(3)]
        it = 0
        for c in range(NCH):
            o_, ob_ = oc[c % 2], oc_b[c % 2]
            P.dma("sync", o_[:], scr["oT"].rearrange("(k p) t -> p k t", p=128)[:, :, c * 512:(c + 1) * 512],
                  reads=[scr["oT_b"][c]], writes=[ob_])
            for dc in range(8):
                ya, yab = cx.banks[it % 4], cx.bank_bufs[it % 4]
                xr_, xrb_ = xr[it % 3], xr_b[it % 3]
                xsl = xT_ap[dc * 128:(dc + 1) * 128, c * 512:(c + 1) * 512]
                P.dma("scalar", xr_[:], xsl, reads=[xT_b[c][dc]], writes=[xrb_])
                for k in range(KC):
                    P.op("tensor", lambda e, ya=ya, k=k, dc=dc, o_=o_: e.matmul(
                        ya[:, :], lhsT=w[:, k, dc * 128:(dc + 1) * 128], rhs=o_[:, k, :],
                        start=(k == 0), stop=(k == KC - 1)), reads=[ob_] + wb, writes=[yab])
                P.op("vector", lambda e, ya=ya, xr_=xr_: e.tensor_tensor(
                    out=xr_[:], in0=ya[:, :], in1=xr_[:], op=ALU.add), reads=[yab], writes=[xrb_])
                P.dma("sync", xsl, xr_[:], reads=[xrb_], writes=[xT_b[c][dc]])
                it += 1
                if bg is not None:
                    next(bg, None)


DIL = (1, 4, 16)
DIL_DK0 = {1: list(range(-128, 513, 128)), 4: list(range(-256, 641, 128)), 16: list(range(-1024, 1409, 128))}
DIL_J0 = {d: max(v) for d, v in DIL_DK0.items()}
DIL_W = {d: max(v) - min(v) + 512 for d, v in DIL_DK0.items()}
EXT = T + 2048


def phase_even_proj(cx, xT_ap, xT_b, aps, consts, scr):
    P = cx.P
    ones_bf = consts["ones_bf"]
    NW = 3008
    with contextlib.ExitStack() as st:
        w = cx.sb([128, 8, NW], BF16, "ew", st)
        wb = [Buf(), Buf()]
        g = cx.sb([128, 8], F32, "eg", st)
        gb = Buf()
        P.dma("sync", g[:], aps["norm"], writes=[gb])
        load_w(cx, st, w, wb, aps["w"], D, NW, row_gain=(g, gb))
        wq = cx.sb([128, 3, 1024], BF16, "ewq", st)
        wqb = [Buf(), Buf()]
        load_w(cx, st, wq, wqb, aps["wuq"], 384, 1024)
        lg = cx.sb([128, 5], F32, "elg", st)
        lgb = Buf()
        P.dma("sync", lg[:], aps["lgain"], writes=[lgb])
        ctab = cx.sb([128, T], F32, "ectab", st)
        stab = cx.sb([128, T], F32, "estab", st)
        tab_b = Buf()
        P.dma("sync", ctab[64:96, :], aps["ctab"], writes=[tab_b])
        P.dma("scalar", stab[64:96, :], aps["stab"], writes=[tab_b])
        xs = cx.sb([128, 8, 512], F32, "ex", st)
        xsb = Buf()
        xns = [cx.sb([128, 8, 512], BF16, "exn", st) for _ in range(2)]
        xnbs = [[Buf(), Buf()], [Buf(), Buf()]]
        cur = {}
        sqs = [cx.sb([128, 512], BF16, "esq", st) for _ in range(2)]
        sqs_b = [Buf(), Buf()]
        rstd = cx.sb([128, 512], F32, "erstd", st)
        rstd_b = Buf()
        lat = cx.sb([128, 5, 512], F32, "elat", st)
        lat_b = [Buf() for _ in range(5)]
        lrs = [cx.sb([128, 512], F32, "elrs", st) for _ in range(2)]
        lrs_b = [Buf(), Buf()]
        latn = cx.sb([128, 5, 512], BF16, "elatn", st)
        latn_b = [Buf() for _ in range(5)]
        t1 = [cx.sb([128, 512], F32, "et1", st) for _ in range(2)]
        t1_b = [Buf(), Buf()]
        t2 = [cx.sb([128, 512], F32, "et2", st) for _ in range(2)]
        t2_b = [Buf(), Buf()]
        qo = [cx.sb([128, 512], BF16, "eqo", st) for _ in range(3)]
        qo_b = [Buf() for _ in range(3)]
        vst = [cx.sb([128, 4, 768], BF16, "evst", st) for _ in range(2)]
        vst_b = [Buf(), Buf()]
        bk = [0]
        zt = cx.sb([128, T], BF16, "ezero", st)
        ztb = Buf()
        P.op("gpsimd", lambda e: e.memset(zt[:], 0.0), writes=[ztb])
        P.dma("sync", scr["kr"][32:160, :], zt[:], reads=[ztb], writes=[scr["kr_b"]])
        P.dma("sync", scr["kr"][128:256, :], zt[:], reads=[ztb], writes=[scr["kr_b"]])

        def nb():
            bk[0] += 1
            i = bk[0] % 6
            return cx.banks[i], cx.bank_bufs[i]

        def proj(bank, bb, c0, ncols, prow0=0):
            xn_, xnb_ = cur["xn"], cur["xnb"]
            for k in range(8):
                P.op("tensor", lambda e, k=k: e.matmul(
                    bank[prow0:prow0 + ncols, :], lhsT=w[:, k, c0:c0 + ncols], rhs=xn_[:, k, :], start=(k == 0), stop=(k == 7)),
                    reads=xnb_ + wb, writes=[bb])

        def xload(c):
            P.dma("sync", xs[:], xT_view(xT_ap, c), reads=xT_b[c], writes=[xsb])

        def xnorm(c):
            rms_chunk(cx, xs, xsb, 8, xns[c % 2], xnbs[c % 2], sqs, sqs_b, ones_bf, 6, rstd, rstd_b, D, gain=(g, gb))

        qi = 0
        xload(0)
        xnorm(0)
        for c in range(NCH):
            cs = slice(c * 512, (c + 1) * 512)
            xn, xnb = xns[c % 2], xnbs[c % 2]
            cur["xn"], cur["xnb"] = xn, xnb
            if c + 1 < NCH:
                xload(c + 1)
            for k5 in range(5):
                A, Ab = nb()
                proj(A, Ab, k5 * 128, 128)
                P.op("scalar", lambda e, A=A, k5=k5: e.activation(out=lat[:, k5, :], in_=A[:, :], func=ACTF.Copy),
                     reads=[Ab], writes=[lat_b[k5]])
            for grp, (ks, Dn) in enumerate((((0, 1, 2), 384), ((3, 4), 256))):
                Sb_, Sbb_ = cx.banks[7], cx.bank_bufs[7]
                for ii, k5 in enumerate(ks):
                    sq, sqb = sqs[ii % 2], sqs_b[ii % 2]
                    P.op("scalar", lambda e, sq=sq, k5=k5: e.activation(out=sq[:], in_=lat[:, k5, :], func=ACTF.Square),
                         reads=[lat_b[k5]], writes=[sqb])
                    P.op("tensor", lambda e, sq=sq, ii=ii, n=len(ks): e.matmul(
                        Sb_[:, :], lhsT=ones_bf[0][:, :], rhs=sq[:], start=(ii == 0), stop=(ii == n - 1)),
                        reads=[sqb, ones_bf[1]], writes=[Sbb_])
                r_, rb_ = lrs[grp], lrs_b[grp]
                P.op("scalar", lambda e, r_=r_, Dn=Dn: e.activation(out=r_[:], in_=Sb_[:, :], func=ACTF.Sqrt, scale=1.0 / Dn, bias=EPS),
                     reads=[Sbb_], writes=[rb_])
                P.op("vector", lambda e, r_=r_: e.reciprocal(out=r_[:], in_=r_[:]), reads=[rb_], writes=[rb_])
                for k5 in ks:
                    P.op("vector", lambda e, k5=k5, r_=r_: e.scalar_tensor_tensor(
                        out=latn[:, k5, :], in0=lat[:, k5, :], scalar=lg[:, k5:k5 + 1], in1=r_[:], op0=ALU.mult, op1=ALU.mult),
                        reads=[lat_b[k5], rb_, lgb], writes=[latn_b[k5]])
            for k5 in (3, 4):
                P.dma("gpsimd", scr["lat"][(k5 - 3) * 128:(k5 - 2) * 128, cs], latn[:, k5, :], reads=[latn_b[k5]], writes=[scr["lat_b"]])

            def rope_to(dst_rows, A, Ab, B, Bb, q_, qb_, cs=cs, qi_=None):
                a_, ab_ = t1[qi_ % 2], t1_b[qi_ % 2]
                b_, bb_ = t2[qi_ % 2], t2_b[qi_ % 2]
                P.op("vector", lambda e: e.tensor_tensor(out=a_[64:96, :], in0=A[64:96, :], in1=ctab[64:96, cs], op=ALU.mult),
                     reads=[Ab, tab_b], writes=[ab_])
                P.op("vector", lambda e: e.tensor_tensor(out=b_[64:96, :], in0=B[64:96, :], in1=stab[64:96, cs], op=ALU.mult),
                     reads=[Bb, tab_b], writes=[bb_])
                P.op("gpsimd", lambda e: e.tensor_tensor(out=q_[64:96, :], in0=a_[64:96, :], in1=b_[64:96, :], op=ALU.add),
                     reads=[ab_, bb_], writes=[qb_])

            A, Ab = nb()
            B, Bb = nb()
            proj(A, Ab, 640, 32, prow0=64)
            proj(B, Bb, 672, 32, prow0=64)
            q_, qb_ = qo[qi % 3], qo_b[qi % 3]
            rope_to(None, A, Ab, B, Bb, q_, qb_, cs=cs, qi_=qi)
            P.dma("gpsimd", scr["kr"][0:32, cs], q_[64:96, :], reads=[qb_], writes=[scr["kr_b"]])
            qi += 1
            for hh in range(8):
                A, Ab = nb()
                B, Bb = nb()
                for k in range(3):
                    P.op("tensor", lambda e, A=A, k=k, hh=hh: e.matmul(
                        A[0:96, :], lhsT=wq[:, k, hh * 96:(hh + 1) * 96], rhs=latn[:, k, :], start=(k == 0), stop=(k == 2)),
                        reads=[latn_b[0], latn_b[1], latn_b[2]] + wqb, writes=[Ab])
                for k in range(3):
                    P.op("tensor", lambda e, B=B, k=k, hh=hh: e.matmul(
                        B[64:96, :], lhsT=wq[:, k, 768 + hh * 32:768 + (hh + 1) * 32], rhs=latn[:, k, :], start=(k == 0), stop=(k == 2)),
                        reads=[latn_b[0], latn_b[1], latn_b[2]] + wqb, writes=[Bb])
                q_, qb_ = qo[qi % 3], qo_b[qi % 3]
                P.op("scalar", lambda e, q_=q_, A=A: e.activation(out=q_[0:64, :], in_=A[0:64, :], func=ACTF.Copy),
                     reads=[Ab], writes=[qb_])
                rope_to(None, A, Ab, B, Bb, q_, qb_, cs=cs, qi_=qi)
                P.dma("sync", scr["mq"][hh * 96:(hh + 1) * 96, cs], q_[0:96, :], reads=[qb_], writes=[scr["mq_b"][hh]])
                qi += 1
            if c + 1 < NCH:
                xnorm(c + 1)
            for cc in range(12):
                A, Ab = nb()
                proj(A, Ab, 704 + cc * 128, 128)
                q_, qb_ = qo[qi % 3], qo_b[qi % 3]
                if cc % 2 == 0:
                    P.op("scalar", lambda e, q_=q_, A=A: e.activation(out=q_[:], in_=A[:, :], func=ACTF.Copy), reads=[Ab], writes=[qb_])
                else:
                    P.op("vector", lambda e, q_=q_, A=A: e.tensor_copy(out=q_[:], in_=A[:, :]), reads=[Ab], writes=[qb_])
                if cc < 6:
                    P.dma("sync", scr["dq"][cc * 128:(cc + 1) * 128, cs], q_[:], reads=[qb_], writes=[scr["dq_b"]])
                else:
                    gg_ = (cc - 6) // 2
                    rr_ = ((cc - 6) % 2) * 128
                    P.dma("sync", scr["dk"][gg_][rr_:rr_ + 128, cs], q_[:], reads=[qb_], writes=[scr["dk_b"]])
                qi += 1
            vs, vsb = vst[c % 2], vst_b[c % 2]
            for j in range(4):
                for part, (c0, nn) in enumerate(((0, 512), (512, 256))):
                    V, Vb = nb()
                    for k in range(8):
                        P.op("tensor", lambda e, V=V, k=k, j=j, c0=c0, nn=nn, xn=xn: e.matmul(
                            V[:, 0:nn], lhsT=xn[:, k, j * 128:(j + 1) * 128], rhs=w[:, k, 2240 + c0:2240 + c0 + nn],
                            start=(k == 0), stop=(k == 7)), reads=xnb + wb, writes=[Vb])
                    if part == 0:
                        P.op("scalar", lambda e, V=V, j=j, c0=c0, nn=nn, vs=vs: e.activation(out=vs[:, j, c0:c0 + nn], in_=V[:, 0:nn], func=ACTF.Copy),
                             reads=[Vb], writes=[vsb])
                    else:
                        P.op("vector", lambda e, V=V, j=j, c0=c0, nn=nn, vs=vs: e.tensor_copy(out=vs[:, j, c0:c0 + nn], in_=V[:, 0:nn]),
                             reads=[Vb], writes=[vsb])
            for gg_ in range(3):
                P.dma("gpsimd", scr["dv"][gg_][cs, :].rearrange("(j p) f -> p j f", p=128), vs[:, :, gg_ * 256:(gg_ + 1) * 256],
                      reads=[vsb], writes=[scr["dv_b"]])


def phase_mla_attn(cx, aps, consts, scr, after_setup=None):
    P = cx.P
    with contextlib.ExitStack() as st:
        wk = cx.sb([128, 2, 512], BF16, "mwk", st)
        wv = cx.sb([128, 2, 512], BF16, "mwv", st)
        wkb, wvb = [Buf(), Buf()], [Buf(), Buf()]
        load_w(cx, st, wk, wkb, aps["wuk"], 256, 512)
        load_w(cx, st, wv, wvb, aps["wuv"], 256, 512)
        ckv = cx.sb([128, 2, SEQ], BF16, "mckv", st)
        ckv_b = Buf()
        for r in range(2):
            for k in range(2):
                P.dma("sync" if k == 0 else "scalar", ckv[:, k, r * T:(r + 1) * T],
                      scr["lat_all"][r * 256 + k * 128:r * 256 + (k + 1) * 128, :], reads=[scr["lat_all_b"]], writes=[ckv_b])
        kh = [cx.sb([128, SEQ], BF16, "mkh", st) for _ in range(2)]
        kh_b = [Buf(), Buf()]
        for i in range(2):
            P.op("gpsimd", lambda e, i=i: e.memset(kh[i][96:128, :], 0.0), writes=[kh_b[i]])
            for r in range(2):
                P.dma("sync", kh[i][64:96, r * T:(r + 1) * T], scr["kr_all"][r * 256:r * 256 + 32, :],
                      reads=[scr["kr_all_b"]], writes=[kh_b[i]])
        vflat = cx.sb([128, 64 * 520 + 128], BF16, "mvres", st)
        vres = vflat[:, 0:64 * 520].rearrange("p (t f) -> p t f", f=520)
        vres_b = Buf()
        P.op("gpsimd", lambda e: e.memset(vflat[:], 1.0), writes=[vres_b])
        for kt in range(64):
            V, Vb = cx.banks[6 + kt % 2], cx.bank_bufs[6 + kt % 2]
            for k in range(2):
                P.op("tensor", lambda e, V=V, k=k, kt=kt: e.matmul(
                    V[:, :], lhsT=ckv[:, k, kt * 128:(kt + 1) * 128], rhs=wv[:, k, :], start=(k == 0), stop=(k == 1)),
                    reads=[ckv_b] + wvb, writes=[Vb])
            dst = vres[:, kt, :].rearrange("p (h d) -> p h d", d=65)[:, :, 0:64]
            src = V[:, :].rearrange("p (h d) -> p h d", d=64)
            if kt % 2 == 0:
                P.op("scalar", lambda e, dst=dst, src=src: e.activation(out=dst, in_=src, func=ACTF.Copy), reads=[Vb], writes=[vres_b])
            else:
                P.op("vector", lambda e, dst=dst, src=src: e.tensor_copy(out=dst, in_=src), reads=[Vb], writes=[vres_b])
        qres = [cx.sb([128, T], BF16, "mq", st) for _ in range(2)]
        qres_b = [Buf(), Buf()]
        for i in range(2):
            P.op("gpsimd", lambda e, i=i: e.memset(qres[i][96:128, :], 0.0), writes=[qres_b[i]])
        if after_setup is not None:
            after_setup()
        ab = attn_bufs(cx, st)
        stream = AttnStream(cx, ab)
        blk_i = 0
        for hh in range(8):
            qr, qrb = qres[hh % 2], qres_b[hh % 2]
            P.dma("sync", qr[0:96, :], scr["mq"][hh * 96:(hh + 1) * 96, :], reads=[scr["mq_b"][hh]], writes=[qrb])
            k_, kb_ = kh[hh % 2], kh_b[hh % 2]
            for kc in range(16):
                bk_ = take_bank(cx, ab)
                V, Vb = cx.banks[bk_], cx.bank_bufs[bk_]
                for k in range(2):
                    P.op("tensor", lambda e, V=V, k=k, kc=kc, hh=hh: e.matmul(
                        V[0:64, :], lhsT=wk[:, k, hh * 64:(hh + 1) * 64], rhs=ckv[:, k, kc * 512:(kc + 1) * 512],
                        start=(k == 0), stop=(k == 1)), reads=[ckv_b] + wkb, writes=[Vb])
                P.op("vector", lambda e, V=V, k_=k_, kc=kc: e.tensor_copy(out=k_[0:64, kc * 512:(kc + 1) * 512], in_=V[0:64, :]),
                     reads=[Vb], writes=[kb_])
            for qc in range(NCH):
                tiles = [(k_[:, kt * 128:(kt + 1) * 128], vflat[:, kt * 520 + hh * 65:kt * 520 + hh * 65 + 128], [kb_, vres_b], None, None)
                         for kt in range(64)]
                acc = 6 + blk_i % 2
                dst = scr["oT"][hh * 64:(hh + 1) * 64, qc * 512:(qc + 1) * 512]
                stream.block(qr[:, qc * 512:(qc + 1) * 512], [qrb], tiles, 96 ** -0.5, 128, acc,
                             finish=lambda acc=acc, dst=dst, qc=qc: attn_finish(cx, ab, acc, 5, dst, [scr["oT_b"][qc]], consts))
                blk_i += 1
        stream.flush()


def phase_dil_attn(cx, aps, consts, scr):
    P = cx.P
    with contextlib.ExitStack() as st:
        hv = cx.sb([128, 2], F32, "dhv", st)
        hvb = Buf()
        P.dma("sync", hv[:], aps["halo_valid"], writes=[hvb])
        ab = attn_bufs(cx, st, npt=4)
        sets = []
        zb = Buf()
        for si in range(2):
            kx = [cx.sb([128, EXT], BF16, "dkx", st) for _ in range(3)]
            qx = [cx.sb([128, T], BF16, "dqx", st) for _ in range(3)]
            vxf = [cx.sb([128, 48 * 65 + 128], BF16, "dvx", st) for _ in range(3)]
            vx = [v[:, 0:48 * 65].rearrange("p (t f) -> p t f", f=65) for v in vxf]
            mk = [cx.sb([128, DIL_W[d]], BF16, "dmk", st) for d in DIL]
            bufs = {k_: [Buf() for _ in range(3)] for k_ in ("kx", "qx", "vx", "mk")}
            for gi in range(3):
                P.op("gpsimd", lambda e, kx=kx, gi=gi: e.memset(kx[gi][64:128, :], 0.0), writes=[bufs["kx"][gi]])
                P.op("gpsimd", lambda e, qx=qx, gi=gi: e.memset(qx[gi][64:128, :], 0.0), writes=[bufs["qx"][gi]])
            sets.append((kx, qx, vxf, vx, mk, bufs))

        def load_slot(s_):
            kx, qx, vxf, vx, mk, bufs = sets[s_ % 2]
            for gi, d in enumerate(DIL):
                hd = gi * 4 + s_
                r0 = hd * 64
                rs = s_ * 64
                q = "sync" if gi % 2 == 0 else "gpsimd"
                kb, qb, vb, mb = bufs["kx"][gi], bufs["qx"][gi], bufs["vx"][gi], bufs["mk"][gi]
                P.dma(q, mk[gi][:], aps["dmask"][hd], writes=[mb])
                P.dma(q, qx[gi][0:64, :], scr["dq"][r0:r0 + 64, :], reads=[scr["dq_b"]], writes=[qb])
                P.dma(q, kx[gi][0:64, 1024:1024 + T], scr["dk"][gi][rs:rs + 64, :], reads=[scr["dk_b"]], writes=[kb])
                P.dma(q, kx[gi][0:64, 0:1024], scr["dk_all"][gi][rs:rs + 64, T - 1024:T], reads=[scr["dk_all_b"]], writes=[kb])
                P.dma(q, kx[gi][0:64, 1024 + T:EXT], scr["dk_all"][gi][256 + rs:256 + rs + 64, 0:1024], reads=[scr["dk_all_b"]], writes=[kb])
                P.op("vector", lambda e, vxf=vxf, gi=gi: e.memset(vxf[gi][:], 1.0), writes=[vb])
                for hf2 in range(2):
                    P.dma(q, vx[gi][:, 8 + hf2 * 16:24 + hf2 * 16, 0:64],
                          scr["dv"][gi][hf2 * 2048:(hf2 + 1) * 2048, rs:rs + 64].rearrange("(t p) d -> p t d", p=128),
                          reads=[scr["dv_b"]], writes=[vb])
                P.dma(q, vx[gi][:, 0:8, 0:64], scr["dv_all"][gi][T - 1024:T, rs:rs + 64].rearrange("(t p) d -> p t d", p=128),
                      reads=[scr["dv_all_b"]], writes=[vb])
                P.dma(q, vx[gi][:, 40:48, 0:64], scr["dv_all"][gi][T:T + 1024, rs:rs + 64].rearrange("(t p) d -> p t d", p=128),
                      reads=[scr["dv_all_b"]], writes=[vb])

        def fix_halo(s_):
            kx, qx, vxf, vx, mk, bufs = sets[s_ % 2]
            for gi in range(3):
                vb = bufs["vx"][gi]
                P.op("vector", lambda e, vx=vx, gi=gi: e.tensor_scalar(out=vx[gi][:, 0:8, :], in0=vx[gi][:, 0:8, :], scalar1=hv[:, 0:1],
                                                                     scalar2=None, op0=ALU.mult), reads=[hvb], writes=[vb])
                P.op("vector", lambda e, vx=vx, gi=gi: e.tensor_scalar(out=vx[gi][:, 40:48, :], in0=vx[gi][:, 40:48, :], scalar1=hv[:, 1:2],
                                                                     scalar2=None, op0=ALU.mult), reads=[hvb], writes=[vb])

        blk_i = 0
        stream = AttnStream(cx, ab, pv_lag=3)
        load_slot(0)
        for s_ in range(4):
            fix_halo(s_)
            while stream.pending:
                stream._pv(stream.pending.pop(0))
            if s_ + 1 < 4:
                load_slot(s_ + 1)
            kx, qx, vxf, vx, mk, bufs = sets[s_ % 2]
            ab["mask_b"] = bufs["mk"]
            for qc in range(NCH):
                q0 = qc * 512
                tiles = []
                for gi, d in enumerate(DIL):
                    for dk0 in DIL_DK0[d]:
                        e0 = q0 + dk0 + 1024
                        j0 = DIL_J0[d] - dk0
                        v0 = (e0 // 128) * 65
                        tiles.append((kx[gi][:, e0:e0 + 128], vxf[gi][:, v0:v0 + 128], [bufs["kx"][gi], bufs["vx"][gi], bufs["qx"][gi]],
                                      mk[gi][:, j0:j0 + 512], qx[gi][:, q0:q0 + 512]))
                acc = 6 + blk_i % 2
                dst = scr["oT"][512 + s_ * 64:512 + (s_ + 1) * 64, q0:q0 + 512]
                stream.block(None, [], tiles, 0.125, 128, acc,
                             finish=lambda acc=acc, dst=dst, qc=qc: attn_finish(cx, ab, acc, 5, dst, [scr["oT_b"][qc]], consts))
                blk_i += 1
        stream.flush()


def run_dil_block(cx, st_bufs, q_list, tiles, scale, acc_bank):
    P = cx.P
    pts, pts_b = st_bufs["pt"], st_bufs["pt_b"]
    sbanks = st_bufs["sbanks"]
    ab, abb = cx.banks[acc_bank], cx.bank_bufs[acc_bank]
    n = len(tiles)
    LA = 2
    cnt = st_bufs["cnt"]
    for i in range(n + LA):
        if i < n:
            lhsT, v_ap, rb, mask, c0, c1, gi = tiles[i]
            u = cnt[0] + i
            sbk = sbanks[u % len(sbanks)]
            sb_, sbb_ = cx.banks[sbk], cx.bank_bufs[sbk]
            pt, ptb = pts[u % len(pts)], pts_b[u % len(pts)]
            P.op("tensor", lambda e, sb_=sb_, lhsT=lhsT, gi=gi: e.matmul(
                sb_[:, :], lhsT=lhsT, rhs=q_list[gi], start=True, stop=True), reads=list(rb), writes=[sbb_])
            P.op("scalar", lambda e, sb_=sb_, pt=pt: e.activation(out=pt[:], in_=sb_[:, :], func=ACTF.Exp, scale=scale),
                 reads=[sbb_], writes=[ptb])
            P.op("vector", lambda e, pt=pt, mask=mask: e.tensor_tensor(out=pt[:], in0=pt[:], in1=mask, op=ALU.mult),
                 reads=[ptb] + st_bufs["mask_b"], writes=[ptb])
        j = i - LA
        if j >= 0:
            lhsT, v_ap, rb, mask, c0, c1, gi = tiles[j]
            u = cnt[0] + j
            pt, ptb = pts[u % len(pts)], pts_b[u % len(pts)]
            P.op("tensor", lambda e, v_ap=v_ap, pt=pt, j=j: e.matmul(
                ab[0:65, :], lhsT=v_ap, rhs=pt[:], start=(j == 0), stop=(j == n - 1)), reads=list(rb) + [ptb], writes=[abb])
    cnt[0] += n


def build_program(layers=(), dbg=()):
    nc = bass.Bass("TRN2", target_bir_lowering=False)
    cx = Ctx(nc)
    P = cx.P

    def dram(name, shape, dtype, kind="Internal"):
        if name in dbg:
            kind = "ExternalOutput"
        return nc.dram_tensor(name, list(shape), dtype, kind=kind).ap()

    x_ap = dram("x", [T, D], F32, "ExternalInput")
    out_ap = dram("out", [T, D], F32, "ExternalOutput")
    ident_ap = dram("ident", [128, 128], F32, "ExternalInput")
    fin_g_ap = dram("final_norm", [128, D], F32, "ExternalInput")
    xT_ap = dram("xT", [D, T], F32)

    ident = cx.sb([128, 128], F32, "ident")
    ident_b = Buf()
    P.dma("sync", ident[:], ident_ap[:, :], writes=[ident_b])

    xT_b = [[Buf() for _ in range(8)] for _ in range(NCH)]
    consts = {}
    ones_bf = cx.sb([128, 128], BF16, "ones_bf")
    ones_bf_b = Buf()
    P.op("vector", lambda e: e.memset(ones_bf[:], 1.0), writes=[ones_bf_b])
    consts["ones_bf"] = (ones_bf, ones_bf_b)
    phase_transpose_in(cx, x_ap, xT_ap, xT_b, (ident, ident_b))
    P.barrier()
    blk_bf = cx.sb([128, 128], BF16, "blk_bf")
    blk_b = Buf()
    P.op("vector", lambda e: e.memset(blk_bf[:], 0.0), writes=[blk_b])
    P.op("vector", lambda e: e.memset(blk_bf[0:64, 0:64], 1.0), writes=[blk_b])
    P.op("vector", lambda e: e.memset(blk_bf[64:128, 64:128], 1.0), writes=[blk_b])
    consts["blk_bf"] = (blk_bf, blk_b)
    PAIRS = [[0, 1], [2, 3], [4, 5], [6, 7]]
    for l in layers:
        pending_op = None
        if "even" in l:
            li = l["even"]
            aps = {"norm": dram("ev_norm%d" % li, [128, 8], F32, "ExternalInput"),
                   "w": dram("ev_w%d" % li, [D, 3008], F32, "ExternalInput"),
                   "wuq": dram("ev_wuq%d" % li, [384, 1024], F32, "ExternalInput"),
                   "wuk": dram("ev_wuk%d" % li, [256, 512], F32, "ExternalInput"),
                   "wuv": dram("ev_wuv%d" % li, [256, 512], F32, "ExternalInput"),
                   "lgain": dram("ev_lgain%d" % li, [128, 5], F32, "ExternalInput")}
            for nm, shp, dt_ in (("ctab", [32, T], F32), ("stab", [32, T], F32), ("halo_valid", [128, 2], F32)):
                if "ev_" + nm not in consts:
                    consts["ev_" + nm] = dram("ev_" + nm, shp, dt_, "ExternalInput")
                aps[nm] = consts["ev_" + nm]
            if "ev_dmask" not in consts:
                consts["ev_dmask"] = [dram("ev_dmask%d" % hd, [128, DIL_W[DIL[hd // 4]]], BF16, "ExternalInput") for hd in range(12)]
            aps["dmask"] = consts["ev_dmask"]
            wo_ap = dram("ev_wo%d" % li, [768, D], F32, "ExternalInput")
            scr = {"lat": dram("lat%d" % li, [256, T], BF16), "lat_b": Buf(),
                   "lat_all": dram("lat_all%d" % li, [512, T], BF16), "lat_all_b": Buf(),
                   "kr": dram("kr%d" % li, [256, T], BF16), "kr_b": Buf(),
                   "kr_all": dram("kr_all%d" % li, [512, T], BF16), "kr_all_b": Buf(),
                   "mq": dram("mq%d" % li, [768, T], BF16), "mq_b": [Buf() for _ in range(8)],
                   "dq": dram("dq%d" % li, [768, T], BF16), "dq_b": Buf(),
                   "dk": [dram("dk%d_%d" % (li, g_), [256, T], BF16) for g_ in range(3)], "dk_b": Buf(),
                   "dv": [dram("dv%d_%d" % (li, g_), [T, 256], BF16) for g_ in range(3)], "dv_b": Buf(),
                   "dk_all": [dram("dk_all%d_%d" % (li, g_), [512, T], BF16) for g_ in range(3)], "dk_all_b": Buf(),
                   "dv_all": [dram("dv_all%d_%d" % (li, g_), [2 * T, 256], BF16) for g_ in range(3)], "dv_all_b": Buf(),
                   "oT": dram("eoT%d" % li, [768, T], BF16), "oT_b": [Buf() for _ in range(NCH)]}
            upto = l.get("upto", 5)
            phase_even_proj(cx, xT_ap, xT_b, aps, consts, scr)
            P.collective("AllGather", [scr["lat"][:, :]], [scr["lat_all"][:, :]], PAIRS, reads=[scr["lat_b"]], writes=[scr["lat_all_b"]])
            P.collective("AllGather", [scr["kr"][:, :]], [scr["kr_all"][:, :]], PAIRS, reads=[scr["kr_b"]], writes=[scr["kr_all_b"]])

            def dil_cc(scr=scr):
                for g_ in range(3):
                    P.collective("AllGather", [scr["dk"][g_][:, :]], [scr["dk_all"][g_][:, :]], PAIRS, reads=[scr["dk_b"]], writes=[scr["dk_all_b"]])
                    P.collective("AllGather", [scr["dv"][g_][:, :]], [scr["dv_all"][g_][:, :]], PAIRS, reads=[scr["dv_b"]], writes=[scr["dv_all_b"]])
            P.barrier()
            if upto >= 3:
                phase_mla_attn(cx, aps, consts, scr, after_setup=dil_cc)
                P.barrier()
            else:
                dil_cc()
                P.barrier()
            if upto >= 4:
                phase_dil_attn(cx, aps, consts, scr)
                P.barrier()
            if upto >= 5:
                if "ffn" in l:
                    pending_op = (wo_ap, 6, scr)
                else:
                    phase_out_proj(cx, xT_ap, xT_b, wo_ap, 6, scr)
                    P.barrier()
        if "gqa" in l:
            li = l["gqa"]
            aps = {"norm": dram("gqa_norm%d" % li, [128, 8], F32, "ExternalInput"),
                   "w": dram("gqa_w%d" % li, [D, 2816], F32, "ExternalInput"),
                   "hgain": dram("gqa_hgain%d" % li, [128, 4], F32, "ExternalInput"),
                   "ctab": dram("gqa_ctab", [128, T], F32, "ExternalInput") if "gqa_ctab" not in consts else consts["gqa_ctab"],
                   "stab": dram("gqa_stab", [128, T], F32, "ExternalInput") if "gqa_stab" not in consts else consts["gqa_stab"]}
            consts["gqa_ctab"], consts["gqa_stab"] = aps["ctab"], aps["stab"]
            wo_ap = dram("gqa_wo%d" % li, [D, D], F32, "ExternalInput")
            scr = {"qT": dram("qT%d" % li, [D, T], BF16), "qT_b": [Buf() for _ in range(8)],
                   "kT": dram("kT%d" % li, [256, T], BF16), "kT_b": Buf(),
                   "v": dram("v%d" % li, [T, 256], BF16), "v_b": Buf(),
                   "kT_all": dram("kT_all%d" % li, [512, T], BF16), "kT_all_b": Buf(),
                   "v_all": dram("v_all%d" % li, [2 * T, 256], BF16), "v_all_b": Buf(),
                   "oT": dram("oT%d" % li, [D, T], BF16), "oT_b": [Buf() for _ in range(NCH)]}
            upto = l.get("upto", 4)
            phase_gqa_proj(cx, xT_ap, xT_b, aps, consts, scr)
            if upto >= 2:
                P.collective("AllGather", [scr["kT"][:, :]], [scr["kT_all"][:, :]], PAIRS,
                             reads=[scr["kT_b"]], writes=[scr["kT_all_b"]])
                P.collective("AllGather", [scr["v"][:, :]], [scr["v_all"][:, :]], PAIRS,
                             reads=[scr["v_b"]], writes=[scr["v_all_b"]])
            P.barrier()
            if upto >= 3:
                phase_gqa_attn(cx, consts, scr)
                P.barrier()
            if upto >= 4:
                if "ffn" in l:
                    pending_op = (wo_ap, 8, scr)
                else:
                    phase_out_proj(cx, xT_ap, xT_b, wo_ap, 8, scr)
                    P.barrier()
        if "ffn" in l:
            li = l["ffn"]
            w_in_ap = dram("ffn_w_in%d" % li, [D, 2 * FFN_H], F32, "ExternalInput")
            w_out_ap = dram("ffn_w_out%d" % li, [FFN_H, D], F32, "ExternalInput")
            g_ap = dram("ffn_norm%d" % li, [128, 8], F32, "ExternalInput")
            if pending_op is not None:
                phase_outproj_ffn(cx, xT_ap, xT_b, pending_op[0], pending_op[1], pending_op[2], w_in_ap, w_out_ap, g_ap, consts)
            else:
                phase_ffn(cx, xT_ap, xT_b, w_in_ap, w_out_ap, g_ap, consts)
            P.barrier()
    phase_final(cx, xT_ap, xT_b, out_ap, fin_g_ap, (ident, ident_b))

    for q in ("sync", "gpsimd", "scalar"):
        ring = P.dma_ring[q]
        toks = [(id(s), v) for s, v in zip(ring["sems"], ring["vals"]) if v > 0]
        P._wait_dma("gpsimd", toks)
    P.emit()
    cx.stack.close()
    return nc


def gain_cols(g):
    g = np.asarray(g, dtype=np.float32)
    return np.ascontiguousarray(g.reshape(-1, 128).T)


def rope_tabs(pos, dim):
    freqs = 10000.0 ** (-np.arange(0, dim, 2, dtype=np.float32) / dim)
    ang = pos.astype(np.float32)[:, None] * freqs[None, :].astype(np.float32)
    return np.cos(ang).astype(np.float32), np.sin(ang).astype(np.float32)


GQ_ORDER = [hh for pair in zip(GQ_L, GQ_U) for hh in pair]
SWAP64 = np.concatenate([np.arange(16, 32), np.arange(0, 16), np.arange(48, 64), np.arange(32, 48)])


def gqa_host(inputs, li, h):
    wq = inputs["gqa_w_q"][li].astype(np.float32).reshape(D, 16, 64)
    wkv = inputs["gqa_w_kv"][li].astype(np.float32).reshape(D, 2, 4, 64)
    qa = wq[:, GQ_ORDER, :]
    qb = qa[:, :, SWAP64]
    ka = wkv[:, 0]
    kb = ka[:, :, SWAP64]
    v = wkv[:, 1]
    w = np.concatenate([qa.reshape(D, -1), qb.reshape(D, -1), ka.reshape(D, -1), kb.reshape(D, -1), v.reshape(D, -1)], axis=1)
    gq = inputs["gqa_q_norm"][li].astype(np.float32)
    gk = inputs["gqa_k_norm"][li].astype(np.float32)
    hg = np.stack([np.tile(gq, 2), np.tile(gq[SWAP64], 2), np.tile(gk, 2), np.tile(gk[SWAP64], 2)], axis=1)
    wo = inputs["gqa_w_o"][li].astype(np.float32).reshape(16, 64, D)[GQ_ORDER].reshape(D, D)
    return np.ascontiguousarray(w), np.ascontiguousarray(hg), np.ascontiguousarray(wo)


def gqa_tabs(h):
    t = np.arange(h * T, (h + 1) * T)
    cr, sr = rope_tabs(t // 64, 32)
    cc, sc = rope_tabs(t % 64, 32)
    C = np.concatenate([cr, cr, cc, cc], axis=1).T
    S = np.concatenate([-sr, sr, -sc, sc], axis=1).T
    return np.ascontiguousarray(np.tile(C, (2, 1))), np.ascontiguousarray(np.tile(S, (2, 1)))


SWAP32 = np.concatenate([np.arange(16, 32), np.arange(0, 16)])


def even_host(inputs, li):
    w = inputs["w_in_ab"][li].astype(np.float32)
    kr = w[:, 640:672]
    wcat = np.concatenate([w[:, 0:672], kr[:, SWAP32], w[:, 672:]], axis=1)
    uq = inputs["mla_w_uq"][li].astype(np.float32)
    wuq = np.concatenate([uq.reshape(384, 768), uq[:, :, 64 + SWAP32].reshape(384, 256)], axis=1)
    ukv = inputs["mla_w_ukv"][li].astype(np.float32)
    wuk = ukv[:, :, 0:64].reshape(256, 512)
    wuv = ukv[:, :, 64:128].reshape(256, 512)
    lg = np.concatenate([gain_cols(inputs["mla_q_norm"][li]), gain_cols(inputs["mla_kv_norm"][li])], axis=1)
    return [np.ascontiguousarray(a) for a in (wcat, wuq, wuk, wuv, lg)]


def even_consts(h):
    t = np.arange(h * T, (h + 1) * T)
    c, s_ = rope_tabs(t, 32)
    C = np.concatenate([c, c], axis=1).T
    S = np.concatenate([-s_, s_], axis=1).T
    hv = np.zeros((128, 2), np.float32)
    hv[:, 0] = 1.0 if h == 1 else 0.0
    hv[:, 1] = 1.0 if h == 0 else 0.0
    slopes = np.exp2(-8.0 * np.arange(1, 13, dtype=np.float32) / 12).astype(np.float32)
    masks = []
    for hd in range(12):
        d = DIL[hd // 4]
        p = np.arange(128)[:, None]
        j = np.arange(DIL_W[d])[None, :]
        delta = p - j + DIL_J0[d]
        ok = (delta % d == 0) & (np.abs(delta) <= 64 * d)
        mval = np.where(ok, np.exp(-slopes[hd] * np.abs(delta).astype(np.float32)), 0.0).astype(np.float32)
        masks.append(mval.astype(ml_dtypes.bfloat16))
    return np.ascontiguousarray(C), np.ascontiguousarray(S), hv, masks


def make_inputs_for_core(c, inputs, layers=()):
    b, h = c // 2, c % 2
    m = {}
    for l in layers:
        if "even" in l:
            li = l["even"]
            wcat, wuq, wuk, wuv, lg = even_host(inputs, li)
            m["ev_w%d" % li], m["ev_wuq%d" % li], m["ev_wuk%d" % li], m["ev_wuv%d" % li], m["ev_lgain%d" % li] = wcat, wuq, wuk, wuv, lg
            m["ev_norm%d" % li] = gain_cols(inputs["mix_norm_ab"][li])
            m["ev_wo%d" % li] = np.ascontiguousarray(inputs["w_out_ab"][li], dtype=np.float32)
            C, S, hv, masks = even_consts(h)
            m["ev_ctab"], m["ev_stab"], m["ev_halo_valid"] = C, S, hv
            for hd in range(12):
                m["ev_dmask%d" % hd] = masks[hd]
        if "gqa" in l:
            li = l["gqa"]
            w, hg, wo = gqa_host(inputs, li, h)
            m["gqa_w%d" % li] = w
            m["gqa_hgain%d" % li] = hg
            m["gqa_wo%d" % li] = wo
            m["gqa_norm%d" % li] = gain_cols(inputs["mix_norm_c"][li])
            m["gqa_ctab"], m["gqa_stab"] = gqa_tabs(h)
        if "ffn" in l:
            li = l["ffn"]
            m["ffn_w_in%d" % li] = np.ascontiguousarray(inputs["ffn_w_in"][li], dtype=np.float32)
            m["ffn_w_out%d" % li] = np.ascontiguousarray(inputs["ffn_w_out"][li], dtype=np.float32)
            m["ffn_norm%d" % li] = gain_cols(inputs["ffn_norm"][li])
    m["x"] = np.ascontiguousarray(inputs["x"][b, h * T:(h + 1) * T, :], dtype=np.float32)
    m["ident"] = np.eye(128, dtype=np.float32)
    m["final_norm"] = np.ascontiguousarray(np.broadcast_to(inputs["final_norm"].astype(np.float32).reshape(1, D), (128, D)))
    return m


LAYERS = ({"even": 0, "ffn": 0}, {"gqa": 0, "ffn": 1}, {"even": 1, "ffn": 2}, {"gqa": 1, "ffn": 3})


def kernel(**inputs):
    inputs = {k: np.asarray(v) for k, v in inputs.items()}
    nc = build_program(layers=LAYERS)
    in_maps = [make_inputs_for_core(c, inputs, LAYERS) for c in range(N_CORES)]
    res = run_bass_kernel_spmd(nc, in_maps, core_ids=list(range(N_CORES)))
    out = np.empty((BATCH, SEQ, D), dtype=np.float32)
    for c in range(N_CORES):
        b, h = c // 2, c % 2
        out[b, h * T:(h + 1) * T, :] = np.asarray(res.results[c]["out"])
    return out
```
